# Optimizing a Trainium2 kernel written in Bass

```python
import math
import jax
import jax.numpy as jnp
from jax import lax
import numpy as np

D_MODEL = 1024
BATCH = 8
SEQ = 4096
DEPTH = 4

GRID_W = 64
CTX_LEN = 256
N_MIXERS = 3
MIX_SHORTCONV = 0
MIX_HYENA = 1
MIX_ATTN = 2
CONV_W = 3
EPS = 1e-6
HY_ORDER = 2
HY_EMB = 33
HY_BANDS = (HY_EMB - 1) // 2
HY_FILTER_HIDDEN = 64
HY_FAST_DECAY = 0.3
HY_SLOW_DECAY = 1.5
HY_TARGET = 1e-2
HEAD_DIM = 128
N_Q_HEADS = D_MODEL // HEAD_DIM
N_KV_HEADS = 2
Q_PER_KV = N_Q_HEADS // N_KV_HEADS
AXIS_DIM = HEAD_DIM // 2
ROPE_THETA = 10000.0
Q_BLOCK = 128
N_EXPERTS = 16
EC_CAPACITY_FACTOR = 2
D_EXPERT = D_MODEL

kernel_name = "hybrid_conv_hyena_gqa_ec_moe_dit"


def _layers_of(kind):
    return list(range(kind, DEPTH, N_MIXERS))


def rms_norm(x, g):
    xf = x.astype(jnp.float32)
    y = xf * lax.rsqrt(jnp.mean(xf * xf, axis=-1, keepdims=True) + EPS)
    return (y * g.astype(jnp.float32)).astype(x.dtype)


def modulate(x, shift, scale):
    return x * (1.0 + scale) + shift


def dwconv_centred(x, w):
    L = x.shape[1]
    p = CONV_W // 2
    xp = jnp.pad(x, ((0, 0), (p, p), (0, 0)))
    y = xp[:, 0:L] * w[0]
    for k in range(1, CONV_W):
        y = y + xp[:, k:k + L] * w[k]
    return y


def short_conv_mixer(h, w_in, conv_w, w_out):
    gate_b, gate_c, v = jnp.split(h @ w_in, 3, axis=-1)
    return (gate_b * dwconv_centred(gate_c * v, conv_w)) @ w_out


def hyena_filter_spectrum(L, w1, b1, w2, b2, w3, sin_freq):
    f32 = jnp.float32
    t = jnp.linspace(0.0, 1.0, L, dtype=f32)[:, None]
    bands = jnp.linspace(1e-4, HY_BANDS - 1, HY_BANDS, dtype=f32)[None, :]
    ang = (2.0 * math.pi / L) * jnp.arange(L, dtype=f32)[:, None] * bands
    z = jnp.concatenate([t, jnp.cos(ang), -jnp.sin(ang)], axis=-1)
    freq = sin_freq.astype(f32)
    a = jnp.sin(freq * (z @ w1.astype(f32) + b1.astype(f32)))
    a = jnp.sin(freq * (a @ w2.astype(f32) + b2.astype(f32)))
    h = (a @ w3.astype(f32)).reshape(L, 2, HY_ORDER, D_MODEL)
    deltas = jnp.abs(jnp.linspace(math.log(HY_TARGET) / HY_SLOW_DECAY,
                                  math.log(HY_TARGET) / HY_FAST_DECAY, D_MODEL, dtype=f32))
    h = h * jnp.exp(-t * deltas[None, :])[:, None, None, :]
    fwd = h[:, 0]
    bwd = h[1:, 1][::-1]
    full = jnp.concatenate([fwd, jnp.zeros((1, HY_ORDER, D_MODEL), f32), bwd], axis=0)
    full = full / jnp.sum(jnp.abs(full), axis=0, keepdims=True)
    return jnp.fft.rfft(full, axis=0)


def long_conv(u, filt_f, skip):
    L = u.shape[1]
    uf = u.astype(jnp.float32)
    y = jnp.fft.irfft(jnp.fft.rfft(uf, n=2 * L, axis=1) * filt_f[None], n=2 * L, axis=1)[:, :L]
    return (y + uf * skip.astype(jnp.float32)).astype(u.dtype)


def hyena_mixer(h, w_in, conv_w, f_w1, f_b1, f_w2, f_b2, f_w3, sin_freq, skip, w_out):
    L = h.shape[1]
    parts = jnp.split(dwconv_centred(h @ w_in, conv_w), HY_ORDER + 1, axis=-1)
    z, gates = parts[0], parts[1:]
    filt = hyena_filter_spectrum(L, f_w1, f_b1, f_w2, f_b2, f_w3, sin_freq)
    for o in range(HY_ORDER):
        z = gates[o] * long_conv(z, filt[:, o], skip[o])
    return z @ w_out


def axial_rope_tables(n_tokens, dtype):
    f32 = jnp.float32
    rows = n_tokens // GRID_W
    row = jnp.repeat(jnp.arange(rows, dtype=f32), GRID_W)
    col = jnp.tile(jnp.arange(GRID_W, dtype=f32), rows)
    inv = ROPE_THETA ** (-jnp.arange(0, AXIS_DIM, 2, dtype=f32) / AXIS_DIM)

    def axis_angles(pos):
        a = pos[:, None] * inv[None, :]
        return jnp.concatenate([a, a], axis=-1)

    ang = jnp.concatenate([axis_angles(row), axis_angles(col)], axis=-1)
    return jnp.cos(ang).astype(dtype), jnp.sin(ang).astype(dtype)


def _rotate_axial(x):
    q = AXIS_DIM // 2
    xr, xc = x[..., :AXIS_DIM], x[..., AXIS_DIM:]
    return jnp.concatenate([-xr[..., q:], xr[..., :q], -xc[..., q:], xc[..., :q]], axis=-1)


def apply_rope(x, cos, sin):
    return x * cos + _rotate_axial(x) * sin


def _attend(q, k, v):
    s = jnp.einsum('bqkgd,bskd->bkgqs', q, k, preferred_element_type=jnp.float32) * (HEAD_DIM ** -0.5)
    p = jax.nn.softmax(s, axis=-1)
    return jnp.einsum('bkgqs,bskd->bqkgd', p.astype(v.dtype), v)


def gqa_mixer(h_lat, h_ctx, w_qkv, q_g, k_g, w_o, ctx_out):
    B, L, _ = h_lat.shape
    Lc = h_ctx.shape[1]
    QD = N_Q_HEADS * HEAD_DIM
    KVD = N_KV_HEADS * HEAD_DIM

    def heads_q(t, n):
        return rms_norm(t.reshape(B, n, N_KV_HEADS, Q_PER_KV, HEAD_DIM), q_g)

    def heads_kv(t, n):
        k = rms_norm(t[..., :KVD].reshape(B, n, N_KV_HEADS, HEAD_DIM), k_g)
        v = t[..., KVD:].reshape(B, n, N_KV_HEADS, HEAD_DIM)
        return k, v

    qkv_l = h_lat @ w_qkv
    q_l = heads_q(qkv_l[..., :QD], L)
    k_l, v_l = heads_kv(qkv_l[..., QD:], L)
    cos, sin = axial_rope_tables(L, q_l.dtype)
    q_l = apply_rope(q_l, cos[None, :, None, None], sin[None, :, None, None])
    k_l = apply_rope(k_l, cos[None, :, None], sin[None, :, None])
    if ctx_out:
        qkv_c = h_ctx @ w_qkv
        q_c = heads_q(qkv_c[..., :QD], Lc)
        kv_c = qkv_c[..., QD:]
    else:
        kv_c = h_ctx @ w_qkv[:, QD:]
    k_c, v_c = heads_kv(kv_c, Lc)

    k_all = jnp.concatenate([k_c, k_l], axis=1)
    v_all = jnp.concatenate([v_c, v_l], axis=1)
    n_blk = L // Q_BLOCK
    qb = jnp.moveaxis(q_l.reshape(B, n_blk, Q_BLOCK, N_KV_HEADS, Q_PER_KV, HEAD_DIM), 1, 0)
    o_l = lax.map(lambda qblk: _attend(qblk, k_all, v_all), qb)
    y_l = jnp.moveaxis(o_l, 0, 1).reshape(B, L, QD) @ w_o
    y_c = _attend(q_c, k_c, v_c).reshape(B, Lc, QD) @ w_o if ctx_out else None
    return y_l, y_c


def expert_choice_ffn(h, w_router, w_gate, w_up, w_down):
    B, N, D = h.shape
    cap = EC_CAPACITY_FACTOR * N // N_EXPERTS
    aff = jax.nn.softmax((h @ w_router).astype(jnp.float32), axis=-1)
    g, idx = lax.top_k(jnp.swapaxes(aff, 1, 2), cap)
    xs = jax.vmap(lambda hb, ib: hb[ib])(h, idx)
    hid = jax.nn.silu(jnp.einsum('becd,edf->becf', xs, w_gate)) * jnp.einsum('becd,edf->becf', xs, w_up)
    ys = jnp.einsum('becf,efd->becd', hid, w_down) * g[..., None].astype(h.dtype)
    return jax.vmap(lambda yb, ib: jnp.zeros((N, D), yb.dtype).at[ib.reshape(-1)].add(yb.reshape(-1, D)))(ys, idx)


def setup_inputs(seed: int = 0) -> dict:
    key = jax.random.key(seed)
    ks = iter(jax.random.split(key, 40))
    D, F = D_MODEL, D_EXPERT
    nA, nB, nC = (len(_layers_of(k)) for k in range(N_MIXERS))
    QKV = (N_Q_HEADS + 2 * N_KV_HEADS) * HEAD_DIM

    def nrm(shape, s):
        return jax.random.normal(next(ks), shape, jnp.float32) * s

    return {
        "x": nrm((BATCH, SEQ, D), 1.0),
        "c": nrm((BATCH, D), 1.0),
        "ctx": nrm((BATCH, CTX_LEN, D), 1.0),
        "c_ctx": nrm((D,), 1.0),
        "ada_w": nrm((DEPTH, D, 6 * D), 0.5 * D ** -0.5),
        "ada_b": nrm((DEPTH, 6 * D), 0.01),
        "norm_g": 1.0 + nrm((DEPTH, 2, D), 0.02),
        "sc_w_in": nrm((nA, D, 3 * D), D ** -0.5),
        "sc_conv": nrm((nA, CONV_W, D), CONV_W ** -0.5),
        "sc_w_out": nrm((nA, D, D), D ** -0.5),
        "hy_w_in": nrm((nB, D, (HY_ORDER + 1) * D), D ** -0.5),
        "hy_conv": nrm((nB, CONV_W, (HY_ORDER + 1) * D), CONV_W ** -0.5),
        "hy_f_w1": nrm((nB, HY_EMB, HY_FILTER_HIDDEN), HY_EMB ** -0.5),
        "hy_f_b1": nrm((nB, HY_FILTER_HIDDEN), 0.02),
        "hy_f_w2": nrm((nB, HY_FILTER_HIDDEN, HY_FILTER_HIDDEN), HY_FILTER_HIDDEN ** -0.5),
        "hy_f_b2": nrm((nB, HY_FILTER_HIDDEN), 0.02),
        "hy_f_w3": nrm((nB, HY_FILTER_HIDDEN, 2 * HY_ORDER * D), HY_FILTER_HIDDEN ** -0.5),
        "hy_sin_freq": 1.0 + nrm((nB, HY_FILTER_HIDDEN), 0.02),
        "hy_skip": nrm((nB, HY_ORDER, D), 1.0),
        "hy_w_out": nrm((nB, D, D), D ** -0.5),
        "at_w_qkv": nrm((nC, D, QKV), D ** -0.5),
        "at_q_g": 1.0 + nrm((nC, HEAD_DIM), 0.02),
        "at_k_g": 1.0 + nrm((nC, HEAD_DIM), 0.02),
        "at_w_o": nrm((nC, N_Q_HEADS * HEAD_DIM, D), (N_Q_HEADS * HEAD_DIM) ** -0.5),
        "moe_router": nrm((DEPTH, D, N_EXPERTS), D ** -0.5),
        "moe_w_gate": nrm((DEPTH, N_EXPERTS, D, F), D ** -0.5),
        "moe_w_up": nrm((DEPTH, N_EXPERTS, D, F), D ** -0.5),
        "moe_w_down": nrm((DEPTH, N_EXPERTS, F, D), F ** -0.5),
    }


def reference(x, c, ctx, c_ctx, ada_w, ada_b, norm_g, sc_w_in, sc_conv, sc_w_out,
              hy_w_in, hy_conv, hy_f_w1, hy_f_b1, hy_f_w2, hy_f_b2, hy_f_w3, hy_sin_freq, hy_skip, hy_w_out,
              at_w_qkv, at_q_g, at_k_g, at_w_o, moe_router, moe_w_gate, moe_w_up, moe_w_down):
    last_attn = max(_layers_of(MIX_ATTN), default=-1)
    lat, cx = x, ctx
    silu_c = jax.nn.silu(c)
    silu_cc = jax.nn.silu(c_ctx)
    for i in range(DEPTH):
        kind = i % N_MIXERS
        j = i // N_MIXERS
        need_ctx = i <= last_attn
        upd_ctx = i < last_attn

        m_l = (silu_c @ ada_w[i] + ada_b[i])[:, None, :]
        sh1, sc1, g1, sh2, sc2, g2 = jnp.split(m_l, 6, axis=-1)
        hl = modulate(rms_norm(lat, norm_g[i, 0]), sh1, sc1)
        if need_ctx:
            m_c = silu_cc @ ada_w[i] + ada_b[i]
            csh1, csc1, cg1, csh2, csc2, cg2 = jnp.split(m_c, 6, axis=-1)
            hc = modulate(rms_norm(cx, norm_g[i, 0]), csh1, csc1)

        yc = None
        if kind == MIX_SHORTCONV:
            yl = short_conv_mixer(hl, sc_w_in[j], sc_conv[j], sc_w_out[j])
            if upd_ctx:
                yc = short_conv_mixer(hc, sc_w_in[j], sc_conv[j], sc_w_out[j])
        elif kind == MIX_HYENA:
            hy_args = (hy_w_in[j], hy_conv[j], hy_f_w1[j], hy_f_b1[j], hy_f_w2[j], hy_f_b2[j],
                       hy_f_w3[j], hy_sin_freq[j], hy_skip[j], hy_w_out[j])
            yl = hyena_mixer(hl, *hy_args)
            if upd_ctx:
                yc = hyena_mixer(hc, *hy_args)
        else:
            yl, yc = gqa_mixer(hl, hc, at_w_qkv[j], at_q_g[j], at_k_g[j], at_w_o[j], upd_ctx)

        lat = lat + g1 * yl
        hl2 = modulate(rms_norm(lat, norm_g[i, 1]), sh2, sc2)
        lat = lat + g2 * expert_choice_ffn(hl2, moe_router[i], moe_w_gate[i], moe_w_up[i], moe_w_down[i])
        if upd_ctx:
            cx = cx + cg1 * yc
            hc2 = modulate(rms_norm(cx, norm_g[i, 1]), csh2, csc2)
            cx = cx + cg2 * expert_choice_ffn(hc2, moe_router[i], moe_w_gate[i], moe_w_up[i], moe_w_down[i])
    return lat
```

```python
import math
from contextlib import ExitStack

import numpy as np
import concourse.bass as bass
import concourse.mybir as mybir
from concourse.bass_utils import run_bass_kernel_spmd

F32 = mybir.dt.float32
BF16 = mybir.dt.bfloat16
I32 = mybir.dt.int32
U32 = mybir.dt.uint32
ALU = mybir.AluOpType
AF = mybir.ActivationFunctionType
AX = mybir.AxisListType


class Buf:
    def __init__(self, t=None, name=""):
        self.t = t
        self.name = name
        self.w = None
        self.r = {}

    def __getitem__(self, k):
        return self.t[k]


class K:
    EPOCH = 28000
    NDMA = 12

    def __init__(self, nc):
        self.nc = nc
        self.stack = ExitStack()
        self.eng = dict(pe=nc.tensor, act=nc.scalar, dve=nc.vector, pool=nc.gpsimd, sp=nc.sync)
        self.sems = {}
        self.nsem = 0
        self.cur = {}
        self.waited = {e: {} for e in self.eng}
        self.own = {e: set() for e in self.eng}
        self.last = {}
        for e in ("pe", "act", "dve", "pool"):
            self.cur[e] = [self._new_sem(e), 0]
            self.own[e].add(self.cur[e][0])
        self.dq = {}
        self.dqi = {}
        for q in ("sp", "pool", "act"):
            self.dq[q] = [[self._new_sem("d" + q), 0] for _ in range(self.NDMA)]
            self.dqi[q] = 0
        self.same_engine_sync = True

    def _new_sem(self, name):
        key = self.nsem
        self.nsem += 1
        self.sems[key] = self.stack.enter_context(self.nc.semaphore(f"s{key}_{name}"))
        return key

    def _wait(self, e, evs):
        need = {}
        for ev in evs:
            if ev is None:
                continue
            k, v = ev
            if v > need.get(k, 0):
                need[k] = v
        for k, v in need.items():
            if e == "pe" and k in self.own["pe"]:
                continue
            if (not self.same_engine_sync) and k in self.own[e]:
                continue
            if self.waited[e].get(k, 0) >= v:
                continue
            self.eng[e].wait_ge(self.sems[k], v)
            self.waited[e][k] = v

    def _deps(self, reads, writes):
        evs = []
        for b in reads:
            evs.append(b.w)
        for b in writes:
            evs.append(b.w)
            evs.extend(b.r.items())
        return evs

    def _mark(self, ev, reads, writes):
        k, v = ev
        for b in reads:
            if v > b.r.get(k, 0):
                b.r[k] = v
        for b in writes:
            b.w = ev
            b.r = {}

    def op(self, e, fn, reads=(), writes=()):
        self._wait(e, self._deps(reads, writes))
        ins = fn(self.eng[e])
        c = self.cur[e]
        c[1] += 1
        ins.then_inc(self.sems[c[0]], 1)
        ev = (c[0], c[1])
        self.last[e] = ev
        if c[1] >= self.EPOCH:
            self.cur[e] = [self._new_sem(e), 0]
            self.own[e].add(self.cur[e][0])
        self._mark(ev, reads, writes)
        return ev

    def dma(self, q, fn, reads=(), writes=()):
        ring = self.dq[q]
        slot = ring[self.dqi[q]]
        self.dqi[q] = (self.dqi[q] + 1) % len(ring)
        evs = self._deps(reads, writes)
        if slot[1] > 0:
            evs.append((slot[0], slot[1]))
        self._wait(q, evs)
        if slot[1] >= self.EPOCH:
            slot[0] = self._new_sem("d" + q)
            slot[1] = 0
        ins = fn(self.eng[q])
        slot[1] += 16
        ins.then_inc(self.sems[slot[0]], 16)
        ev = (slot[0], slot[1])
        self._mark(ev, reads, writes)
        return ev

    def all_events(self):
        evs = []
        for e, ev in self.last.items():
            evs.append(ev)
        for q, ring in self.dq.items():
            for s in ring:
                if s[1] > 0:
                    evs.append((s[0], s[1]))
        return evs

    def barrier(self, engines=("pe", "act", "dve", "pool", "sp")):
        evs = self.all_events()
        for e in engines:
            self._wait(e, evs)

    def close(self):
        self.stack.close()


D = 1024
NCH = 8
NE = 16
HD = 128
NQH = 8
NKVH = 2
EPS = 1e-6


class Prog:
    def __init__(self, nc, T, LC):
        self.nc = nc
        self.k = K(nc)
        self.T = T
        self.LC = LC
        self.gs = ExitStack()
        k = self.k
        self.ident = self.sb(self.gs, "ident", [128, 128], F32)
        self.identb = self.sb(self.gs, "identb", [128, 128], BF16)
        self.iota_row = self.sb(self.gs, "iota_row", [128, 512], F32)
        self.pidx = self.sb(self.gs, "pidx", [128, 1], F32)
        self.iota_h = self.sb(self.gs, "iota_h", [128, 512], mybir.dt.float16)
        self.ones_f = self.sb(self.gs, "ones_f", [128, 128], F32)
        self.ones_b = self.sb(self.gs, "ones_b", [128, 128], BF16)
        k.op("pool", lambda e: e.iota(self.iota_row[:], pattern=[[1, 512]], base=0, channel_multiplier=0,
                                      allow_small_or_imprecise_dtypes=True), writes=[self.iota_row])
        k.op("pool", lambda e: e.iota(self.pidx[:], pattern=[[0, 1]], base=0, channel_multiplier=1,
                                      allow_small_or_imprecise_dtypes=True), writes=[self.pidx])
        k.op("dve", lambda e: e.tensor_scalar(out=self.ident[:], in0=self.iota_row[:, 0:128], scalar1=self.pidx[:, 0:1],
                                              scalar2=None, op0=ALU.is_equal),
             reads=[self.iota_row, self.pidx], writes=[self.ident])
        k.op("dve", lambda e: e.tensor_copy(out=self.identb[:], in_=self.ident[:]), reads=[self.ident], writes=[self.identb])
        k.op("dve", lambda e: e.tensor_copy(out=self.iota_h[:], in_=self.iota_row[:]), reads=[self.iota_row], writes=[self.iota_h])
        k.op("dve", lambda e: e.memset(self.ones_f[:], 1.0), writes=[self.ones_f])
        k.op("dve", lambda e: e.memset(self.ones_b[:], 1.0), writes=[self.ones_b])
        self.ps = [Buf(self.gs.enter_context(nc.psum_tensor(f"ps{i}", [128, 1024], F32)), f"ps{i}") for i in range(4)]
        self.psi = 0

    def sb(self, st, name, shape, dt):
        self._uid = getattr(self, "_uid", 0) + 1
        name = f"{name}_{self._uid}"
        return Buf(st.enter_context(self.nc.sbuf_tensor(name, shape, dt)), name)

    def psum(self):
        p = self.ps[self.psi]
        self.psi = (self.psi + 1) % len(self.ps)
        return p

    def end_phase(self, st):
        self.k.barrier()
        st.close()

    def finish(self):
        self.k.barrier()
        self.gs.close()
        self.k.close()


def pbf(p):
    return p.t[:].bitcast(BF16)


def _pm_view(ap2d):
    return ap2d.rearrange("(c p) n -> p c n", p=128)


class Layers(Prog):
    def setup_small(self, st):
        k = self.k
        self.eps_t = self.sb(st, "eps_t", [128, 1], F32)
        k.op("dve", lambda e: e.memset(self.eps_t[:], EPS), writes=[self.eps_t])
        self.mhalf = self.sb(st, "mhalf", [128, 1], F32)
        k.op("dve", lambda e: e.memset(self.mhalf[:], -0.5), writes=[self.mhalf])
        self.sm = [[self.sb(st, f"sm{i}_{j}", [128, 1], F32) for j in range(2)] for i in range(4)]
        self.smi = 0

    def small(self):
        s = self.sm[self.smi]
        self.smi = (self.smi + 1) % len(self.sm)
        return s

    def make_sil_rep(self, st, name, vec_pm_ap):
        k = self.k
        v = self.sb(st, name + "_v", [128, 8], F32)
        sg = self.sb(st, name + "_sg", [128, 8], F32)
        rep = self.sb(st, name + "_rep", [128, 8, 128], F32)
        k.dma("sp", lambda e: e.dma_start(out=v[:], in_=vec_pm_ap), writes=[v])
        k.op("act", lambda e: e.activation(out=sg[:], in_=v[:], func=AF.Silu), reads=[v], writes=[sg])
        for c in range(8):
            k.op("dve", lambda e: e.tensor_copy(out=rep[:, c, :], in_=sg[:, c:c + 1].to_broadcast([128, 128])),
                 reads=[sg], writes=[rep])
        return rep

    def ada_precompute(self, c_pm, cc_pm, ada_w, ada_b, depth, rows_d):
        k = self.k
        with ExitStack() as ls:
            S2 = self.sb(ls, "ap_S2", [128, 8, 2], F32)
            for si, vec in enumerate((c_pm, cc_pm)):
                v = self.sb(ls, f"ap_v{si}", [128, 8], F32)
                sg = self.sb(ls, f"ap_sg{si}", [128, 8], F32)
                k.dma("sp", lambda e: e.dma_start(out=v[:], in_=vec), writes=[v])
                k.op("act", lambda e: e.activation(out=sg[:], in_=v[:], func=AF.Silu), reads=[v], writes=[sg])
                k.op("dve", lambda e: e.tensor_copy(out=S2[:, :, si], in_=sg[:]), reads=[sg], writes=[S2])
            wb = [self.sb(ls, f"ap_w{j}", [128, 8, 1024], F32) for j in range(2)]
            bb = self.sb(ls, "ap_b", [2, 6 * D], F32)
            rowt = [self.sb(ls, f"ap_r{j}", [2, 6 * D], F32) for j in range(2)]
            n = 0
            for i in range(depth):
                av = _pm_view(ada_w[i])
                rt = rowt[i % 2]
                k.dma("sp", lambda e: e.dma_start(out=bb[:], in_=ada_b[i:i + 1, :].partition_broadcast(2)), writes=[bb])
                for j in range(6):
                    wj = wb[n % 2]
                    n += 1
                    k.dma("sp", lambda e: e.dma_start(out=wj[:], in_=av[:, :, j * 1024:(j + 1) * 1024]), writes=[wj])
                    p = self.psum()
                    for h in range(2):
                        for c in range(8):
                            k.op("pe", lambda e: e.matmul(p[0:2, h * 512:(h + 1) * 512], lhsT=S2[:, c, :], rhs=wj[:, c, h * 512:(h + 1) * 512],
                                                          start=(c == 0), stop=(c == 7)), reads=[S2, wj], writes=[p])
                    k.op("dve", lambda e: e.tensor_tensor(out=rt[:, j * 1024:(j + 1) * 1024], in0=p[0:2, :], in1=bb[:, j * 1024:(j + 1) * 1024],
                                                          op=ALU.add), reads=[p, bb], writes=[rt])
                k.dma("sp", lambda e: e.dma_start(out=rows_d[i], in_=rt[:]), reads=[rt])
            self.k.barrier()

    def ada_load(self, st, row_ap, norm_g_i, tag):
        k = self.k
        m = [self.sb(st, f"mod{tag}_{j}", [128, 1024], F32) for j in range(6)]
        for j in range(6):
            k.dma("sp", lambda e: e.dma_start(out=m[j][:], in_=row_ap[0:1, j * 1024:(j + 1) * 1024].partition_broadcast(128)), writes=[m[j]])
        with ExitStack() as ls:
            gb = [self.sb(ls, f"adag{tag}_{j}", [128, 1024], F32) for j in range(2)]
            for (jsc, gi) in ((1, 0), (4, 1)):
                k.dma("sp", lambda e: e.dma_start(out=gb[gi][:], in_=norm_g_i[gi:gi + 1, :].partition_broadcast(128)), writes=[gb[gi]])
                k.op("dve", lambda e: e.scalar_tensor_tensor(out=m[jsc][:], in0=m[jsc][:], scalar=1.0, in1=gb[gi][:],
                                                             op0=ALU.add, op1=ALU.mult), reads=[m[jsc], gb[gi]], writes=[m[jsc]])
            self.k.barrier()
        sh1, a1, g1, sh2, a2, g2 = m
        return a1, sh1, g1, a2, sh2, g2

    def ada_phase(self, st, sil_rep, ada_w_i, ada_b_i, norm_g_i, tag):
        k = self.k
        m = [self.sb(st, f"mod{tag}_{j}", [128, 1024], F32) for j in range(6)]
        av = _pm_view(ada_w_i)
        with ExitStack() as ls:
            if not isinstance(sil_rep, Buf):
                sil_rep = self.make_sil_rep(ls, "silrep" + tag, sil_rep)
            wb = [self.sb(ls, f"adaw{tag}_{j}", [128, 8, 1024], F32) for j in range(2)]
            bb = [self.sb(ls, f"adab{tag}_{j}", [128, 1024], F32) for j in range(2)]
            for j in range(6):
                wj, bj = wb[j % 2], bb[j % 2]
                k.dma("sp", lambda e: e.dma_start(out=wj[:], in_=av[:, :, j * 1024:(j + 1) * 1024]), writes=[wj])
                k.dma("sp", lambda e: e.dma_start(out=bj[:], in_=ada_b_i[0:1, j * 1024:(j + 1) * 1024].partition_broadcast(128)),
                      writes=[bj])
                p = self.psum()
                for h in range(2):
                    for c in range(8):
                        k.op("pe", lambda e: e.matmul(p[:, h * 512:(h + 1) * 512], lhsT=sil_rep[:, c, :],
                                                      rhs=wj[:, c, h * 512:(h + 1) * 512], start=(c == 0), stop=(c == 7)),
                             reads=[sil_rep, wj], writes=[p])
                k.op("dve", lambda e: e.tensor_tensor(out=m[j][:], in0=p[:], in1=bj[:], op=ALU.add),
                     reads=[p, bj], writes=[m[j]])
            for (jsc, gi) in ((1, 0), (4, 1)):
                gb = bb[gi]
                k.dma("sp", lambda e: e.dma_start(out=gb[:], in_=norm_g_i[gi:gi + 1, :].partition_broadcast(128)), writes=[gb])
                k.op("dve", lambda e: e.scalar_tensor_tensor(out=m[jsc][:], in0=m[jsc][:], scalar=1.0, in1=gb[:],
                                                             op0=ALU.add, op1=ALU.mult),
                     reads=[m[jsc], gb], writes=[m[jsc]])
            self.k.barrier()
        sh1, a1, g1, sh2, a2, g2 = m
        return a1, sh1, g1, a2, sh2, g2

    def norm_stats_a(self, xt, tmp, rows=128):
        k = self.k
        ss, rs = self.small()
        R = slice(0, rows)
        k.op("act", lambda e: e.activation(out=tmp[R, :], in_=xt[R, :], func=AF.Square, accum_out=ss[R, :]),
             reads=[xt], writes=[tmp, ss])
        return (ss, rs)

    def norm_stats_b(self, pr, rows=128):
        k = self.k
        ss, rs = pr
        R = slice(0, rows)
        k.op("dve", lambda e: e.tensor_scalar(out=ss[R, :], in0=ss[R, :], scalar1=1.0 / D, scalar2=EPS, op0=ALU.mult, op1=ALU.add),
             reads=[ss], writes=[ss])
        k.op("pool", lambda e: e.tensor_tensor(out=rs[R, :], in0=ss[R, :], in1=self.mhalf[R, 0:1], op=ALU.pow),
             reads=[ss, self.mhalf], writes=[rs])
        return rs

    def norm_stats(self, xt, tmp, rows=128):
        return self.norm_stats_b(self.norm_stats_a(xt, tmp, rows), rows)

    def norm_apply(self, xt, rs, A, B, tmp, out_h, rows=128):
        k = self.k
        R = slice(0, rows)
        k.op("dve", lambda e: e.scalar_tensor_tensor(out=tmp[R, :], in0=xt[R, :], scalar=rs[R, 0:1], in1=A[R, :],
                                                     op0=ALU.mult, op1=ALU.mult),
             reads=[xt, rs, A], writes=[tmp])
        k.op("dve", lambda e: e.tensor_tensor(out=out_h[R, :], in0=tmp[R, :], in1=B[R, :], op=ALU.add),
             reads=[tmp, B], writes=[out_h])

    def norm_tile(self, xt, A, B, tmp, out_h, rows=128):
        rs = self.norm_stats(xt, tmp, rows)
        self.norm_apply(xt, rs, A, B, tmp, out_h, rows)

    def transpose_bf_to(self, src, dst, col0, rows=128):
        k = self.k
        p = self.psum()
        pv = pbf(p)[:, 0:1024].rearrange("p (c t) -> p c t", c=8)
        for c in range(8):
            k.op("pe", lambda e: e.transpose(pv[:, c, 0:rows], src[0:rows, c * 128:(c + 1) * 128], self.identb[0:rows, 0:rows]),
                 reads=[src, self.identb], writes=[p])
        k.op("act", lambda e: e.copy(out=dst[:, :, col0:col0 + rows], in_=pv[:, :, 0:rows]), reads=[p], writes=[dst])

    def phase_a(self, st, src_ap, T, A1, B1, tag, hT=None):
        k = self.k
        if hT is None:
            hT = self.sb(st, f"hT{tag}", [128, 8, T], BF16)
        with ExitStack() as ls:
            xts = [self.sb(ls, f"pa_x{i}", [128, 1024], F32) for i in range(3)]
            tmps = [self.sb(ls, f"pa_t{i}", [128, 1024], F32) for i in range(3)]
            hbs = [self.sb(ls, f"pa_h{i}", [128, 1024], BF16) for i in range(3)]
            NTT = T // 128
            rss = {}

            def st1(tt):
                xt, tmp = xts[tt % 3], tmps[tt % 3]
                k.dma("sp", lambda e: e.dma_start(out=xt[:], in_=src_ap[tt * 128:(tt + 1) * 128, :]), writes=[xt])
                rss[tt] = self.norm_stats_a(xt, tmp)

            def st1b(tt):
                rss[tt] = self.norm_stats_b(rss[tt])

            def st2(tt):
                self.norm_apply(xts[tt % 3], rss.pop(tt), A1, B1, tmps[tt % 3], hbs[tt % 3])

            def st3(tt):
                self.transpose_bf_to(hbs[tt % 3], hT, tt * 128)

            for step in range(NTT + 2):
                if step < NTT:
                    st1(step)
                if 0 <= step - 1 < NTT:
                    st2(step - 1)
                if step < NTT:
                    st1b(step)
                if 0 <= step - 2 < NTT:
                    st3(step - 2)
            self.k.barrier()
        return hT

    def load_w_bf(self, dst, w_ap2d, col0, ncols):
        v = _pm_view(w_ap2d)
        self.k.dma("pool", lambda e: e.dma_start(out=dst[:, :, 0:ncols], in_=v[:, :, col0:col0 + ncols]), writes=[dst])

    def phase_b_conv(self, hT, T, w_in, cw_pm, zT_d, tag):
        k = self.k
        TB = min(512, T)
        with ExitStack() as ls:
            win = self.sb(ls, "cv_win", [128, 8, 3072], BF16)
            for q in range(3):
                v = _pm_view(w_in)
                k.dma("pool", lambda e: e.dma_start(out=win[:, :, q * 1024:(q + 1) * 1024], in_=v[:, :, q * 1024:(q + 1) * 1024]),
                      writes=[win])
            cw = self.sb(ls, "cv_cw", [128, 8, 3], F32)
            k.dma("sp", lambda e: e.dma_start(out=cw[:], in_=cw_pm), writes=[cw])
            cv = self.sb(ls, "cv_cv", [128, T + 2], F32)
            gb = self.sb(ls, "cv_gb", [128, T], BF16)
            acc = self.sb(ls, "cv_acc", [128, T], F32)
            vt = [self.sb(ls, f"cv_vt{i}", [128, TB], F32) for i in range(2)]
            zr = [self.sb(ls, f"cv_zr{i}", [128, T], BF16) for i in range(1)]
            k.op("dve", lambda e: e.memset(cv[:, 0:1], 0.0), writes=[cv])
            k.op("dve", lambda e: e.memset(cv[:, T + 1:T + 2], 0.0), writes=[cv])
            for j in range(8):
                for tb in range(T // TB):
                    ts = slice(tb * TB, (tb + 1) * TB)
                    pb_, pc_, pv_ = self.psum(), self.psum(), self.psum()
                    for q, p in ((0, pb_), (1, pc_), (2, pv_)):
                        for c in range(8):
                            k.op("pe", lambda e: e.matmul(p[:, 0:TB], lhsT=win[:, c, q * 1024 + j * 128:q * 1024 + (j + 1) * 128],
                                                          rhs=hT[:, c, ts], start=(c == 0), stop=(c == 7)),
                                 reads=[win, hT], writes=[p])
                    v_ = vt[tb % 2]
                    k.op("act", lambda e: e.copy(out=v_[:, 0:TB], in_=pv_[:, 0:TB]), reads=[pv_], writes=[v_])
                    k.op("dve", lambda e: e.tensor_tensor(out=cv[:, 1 + tb * TB:1 + (tb + 1) * TB], in0=pc_[:, 0:TB], in1=v_[:, 0:TB],
                                                          op=ALU.mult), reads=[pc_, v_], writes=[cv])
                    k.op("act", lambda e: e.copy(out=gb[:, ts], in_=pb_[:, 0:TB]), reads=[pb_], writes=[gb])
                z = zr[0]
                k.op("dve", lambda e: e.tensor_scalar(out=acc[:], in0=cv[:, 0:T], scalar1=cw[:, j, 0:1], scalar2=None, op0=ALU.mult),
                     reads=[cv, cw], writes=[acc])
                k.op("dve", lambda e: e.scalar_tensor_tensor(out=acc[:], in0=cv[:, 1:T + 1], scalar=cw[:, j, 1:2], in1=acc[:],
                                                             op0=ALU.mult, op1=ALU.add), reads=[cv, cw, acc], writes=[acc])
                k.op("dve", lambda e: e.scalar_tensor_tensor(out=acc[:], in0=cv[:, 2:T + 2], scalar=cw[:, j, 2:3], in1=acc[:],
                                                             op0=ALU.mult, op1=ALU.add), reads=[cv, cw, acc], writes=[acc])
                k.op("dve", lambda e: e.tensor_tensor(out=z[:], in0=acc[:], in1=gb[:], op=ALU.mult), reads=[acc, gb], writes=[z])
                k.dma("sp", lambda e: e.dma_start(out=zT_d[j * 128:(j + 1) * 128, :], in_=z[:]), reads=[z])
            self.k.barrier()

    def phase_c(self, T, zT_d, w_out, lat_src, lat_dst, G1, A2, B2, w_router, h2_d, st_out, tag, want_moe=True, z_tok=False):
        k = self.k
        TB = min(512, T)
        affT = self.sb(st_out, f"affT{tag}", [16, T], F32) if want_moe else None
        with ExitStack() as ls:
            wo = self.sb(ls, "pc_wo", [128, 8, 1024], BF16)
            self.load_w_bf(wo, w_out, 0, 1024)
            zts = [self.sb(ls, f"pc_z{i}", [128, 8, TB], BF16) for i in range(2)]
            xts = [self.sb(ls, f"pc_x{i}", [128, 1024], F32) for i in range(3)]
            tmps = [self.sb(ls, f"pc_t{i}", [128, 1024], F32) for i in range(3)]
            h2s = [self.sb(ls, f"pc_h{i}", [128, 1024], F32) for i in range(2)]
            h2bs = [self.sb(ls, f"pc_hb{i}", [128, 1024], BF16) for i in range(2)]
            h2Ts = [self.sb(ls, f"pc_hT{i}", [128, 8, TB], F32) for i in range(2)]
            if want_moe:
                wr = self.sb(ls, "pc_wr", [128, 8, 16], F32)
                k.dma("sp", lambda e: e.dma_start(out=wr[:], in_=_pm_view(w_router)), writes=[wr])
            if not z_tok:
                zv = zT_d.rearrange("(c p) t -> p c t", p=128)
            else:
                zrow = [self.sb(ls, f"pc_zr{i}", [128, 1024], BF16) for i in range(2)]
            NSUB = TB // 128
            tiles = [(tb, sub) for tb in range(T // TB) for sub in range(NSUB)]

            def stA(i):
                tb, sub = tiles[i]
                zt = zts[tb % 2]
                if sub == 0:
                    if not z_tok:
                        k.dma("sp", lambda e: e.dma_start(out=zt[:], in_=zv[:, :, tb * TB:(tb + 1) * TB]), writes=[zt])
                    else:
                        for sb_ in range(NSUB):
                            zr_ = zrow[sb_ % 2]
                            r0 = tb * TB + sb_ * 128
                            k.dma("sp", lambda e: e.dma_start(out=zr_[:], in_=zT_d[r0:r0 + 128, :]), writes=[zr_])
                            self.transpose_bf_to(zr_, zt, sb_ * 128)
                tt = tb * NSUB + sub
                xt, tmp = xts[i % 3], tmps[i % 3]
                rows = slice(tt * 128, (tt + 1) * 128)
                k.dma("sp", lambda e: e.dma_start(out=xt[:], in_=lat_src[rows, :]), writes=[xt])
                p = self.psum()
                for c in range(8):
                    for h in range(2):
                        k.op("pe", lambda e: e.matmul(p[:, h * 512:(h + 1) * 512], lhsT=zt[:, c, sub * 128:(sub + 1) * 128],
                                                      rhs=wo[:, c, h * 512:(h + 1) * 512], start=(c == 0), stop=(c == 7)),
                             reads=[zt, wo], writes=[p])
                k.op("dve", lambda e: e.tensor_tensor(out=tmp[:], in0=p[:], in1=G1[:], op=ALU.mult), reads=[p, G1], writes=[tmp])
                k.op("dve", lambda e: e.tensor_tensor(out=xt[:], in0=tmp[:], in1=xt[:], op=ALU.add), reads=[tmp, xt], writes=[xt])
                k.dma("pool", lambda e: e.dma_start(out=lat_dst[rows, :], in_=xt[:]), reads=[xt])
                if want_moe:
                    return self.norm_stats_a(xt, tmp)
                return None

            def stB(i, rs):
                tb, sub = tiles[i]
                tt = tb * NSUB + sub
                rows = slice(tt * 128, (tt + 1) * 128)
                xt, tmp, h2, h2b = xts[i % 3], tmps[i % 3], h2s[i % 2], h2bs[i % 2]
                self.norm_apply(xt, rs, A2, B2, tmp, h2)
                k.op("act", lambda e: e.copy(out=h2b[:], in_=h2[:]), reads=[h2], writes=[h2b])
                k.dma("pool", lambda e: e.dma_start(out=h2_d[rows, :], in_=h2b[:]), reads=[h2b])

            def stC(i):
                tb, sub = tiles[i]
                h2, h2T = h2s[i % 2], h2Ts[tb % 2]
                p2 = self.psum()
                p2v = p2.t[:].rearrange("p (c t) -> p c t", c=8)
                for c in range(8):
                    k.op("pe", lambda e: e.transpose(p2v[:, c, :], h2[:, c * 128:(c + 1) * 128], self.ident[:]),
                         reads=[h2, self.ident], writes=[p2])
                k.op("act", lambda e: e.copy(out=h2T[:, :, sub * 128:(sub + 1) * 128], in_=p2v), reads=[p2], writes=[h2T])
                if sub == NSUB - 1:
                    p3 = self.psum()
                    for c in range(8):
                        k.op("pe", lambda e: e.matmul(p3[0:16, 0:TB], lhsT=wr[:, c, :], rhs=h2T[:, c, :], start=(c == 0), stop=(c == 7)),
                             reads=[wr, h2T], writes=[p3])
                    k.op("act", lambda e: e.activation(out=affT[:, tb * TB:(tb + 1) * TB], in_=p3[0:16, 0:TB], func=AF.Exp),
                         reads=[p3], writes=[affT])

            NTL = len(tiles)
            rsd = {}
            for step in range(NTL + 2):
                if step < NTL:
                    rsd[step] = stA(step)
                if want_moe and 0 <= step - 1 < NTL:
                    stB(step - 1, rsd.pop(step - 1))
                if want_moe and step < NTL:
                    rsd[step] = self.norm_stats_b(rsd[step])
                if want_moe and 0 <= step - 2 < NTL:
                    stC(step - 2)
            if want_moe:
                rc = self.sb(ls, "pc_rc", [16, TB], F32)
                for tb in range(T // TB):
                    ts = slice(tb * TB, (tb + 1) * TB)
                    p = self.psum()
                    k.op("pe", lambda e: e.matmul(p[0:16, 0:TB], lhsT=self.ones_f[0:16, 0:16], rhs=affT[:, ts], start=True, stop=True),
                         reads=[self.ones_f, affT], writes=[p])
                    k.op("dve", lambda e: e.reciprocal(out=rc[:], in_=p[0:16, 0:TB]), reads=[p], writes=[rc])
                    k.op("dve", lambda e: e.tensor_tensor(out=affT[:, ts], in0=affT[:, ts], in1=rc[:], op=ALU.mult),
                         reads=[affT, rc], writes=[affT])
            self.k.barrier()
        return affT


class Moe(Layers):
    def moe_route(self, T, affT, idx, gsel, n_iter=27, dbg=None):
        k = self.k
        cap = 2 * T // NE
        CW = min(cap, 128)
        NJT = cap // CW
        NT = T // 128
        if True:
            with ExitStack() as ls:
                junk = self.sb(ls, "mo_junk", [16, T], F32)
                mask = self.sb(ls, "mo_mask", [16, T], F32)
                cum = self.sb(ls, "mo_cum", [16, T], F32)
                sv = {n: self.sb(ls, "mo_" + n, [16, 1], F32) for n in ("lo", "hi", "mid", "cnt", "ge", "nge", "t1")}
                lo, hi, mid, cnt, ge, nge, t1 = (sv[n] for n in ("lo", "hi", "mid", "cnt", "ge", "nge", "t1"))
                if T >= 1024:
                    W = T // 8
                    if not hasattr(self, "aff_scr"):
                        self.aff_scr = self.nc.dram_tensor("s_affscr", [16, T], F32).ap()
                        self.aff_scr_buf = Buf(None, "aff_scr")
                    a128 = self.sb(ls, "mo_a128", [128, W], F32)
                    j128 = self.sb(ls, "mo_j128", [128, W], F32)
                    G = self.sb(ls, "mo_G", [16, 128], F32)
                    GT8 = self.sb(ls, "mo_GT8", [128, 16], F32)
                    Bd = self.sb(ls, "mo_Bd", [128, 128], F32)
                    bv = {n_: self.sb(ls, "mo_b" + n_, [128, 1], F32) for n_ in ("lo", "hi", "mid", "cnt", "ge", "nge", "t1")}
                    k.dma("sp", lambda e: e.dma_start(out=self.aff_scr[:, 0:T], in_=affT[:]), reads=[affT], writes=[self.aff_scr_buf])
                    k.dma("sp", lambda e: e.dma_start(out=a128[:], in_=self.aff_scr[:, 0:T].rearrange("e (s j) -> (e s) j", s=8)),
                          reads=[self.aff_scr_buf], writes=[a128])
                    k.op("pool", lambda e: e.iota(G[:], pattern=[[1, 16], [0, 8]], base=0, channel_multiplier=0,
                                                  allow_small_or_imprecise_dtypes=True), writes=[G])
                    k.op("dve", lambda e: e.tensor_scalar(out=G[:], in0=G[:], scalar1=self.pidx[0:16, 0:1], scalar2=None, op0=ALU.is_equal),
                         reads=[G, self.pidx], writes=[G])
                    p = self.psum()
                    k.op("pe", lambda e: e.matmul(p[:, 0:128], lhsT=G[:], rhs=G[:], start=True, stop=True), reads=[G], writes=[p])
                    k.op("dve", lambda e: e.tensor_copy(out=Bd[:], in_=p[:, 0:128]), reads=[p], writes=[Bd])
                    p = self.psum()
                    k.op("pe", lambda e: e.transpose(p[:, 0:16], G[:], self.ident[0:16, 0:16]), reads=[G, self.ident], writes=[p])
                    k.op("dve", lambda e: e.tensor_scalar(out=GT8[:], in0=p[:, 0:16], scalar1=0.125, scalar2=None, op0=ALU.mult),
                         reads=[p], writes=[GT8])
                    blo, bhi, bmid, bcnt, bge, bnge, bt1 = (bv[n_] for n_ in ("lo", "hi", "mid", "cnt", "ge", "nge", "t1"))
                    k.op("dve", lambda e: e.memset(blo[:], 0.0), writes=[blo])
                    k.op("dve", lambda e: e.memset(bhi[:], 1.5), writes=[bhi])
                    for _ in range(n_iter):
                        k.op("dve", lambda e: e.tensor_tensor(out=bmid[:], in0=blo[:], in1=bhi[:], op=ALU.add), reads=[blo, bhi], writes=[bmid])
                        k.op("dve", lambda e: e.tensor_scalar(out=bmid[:], in0=bmid[:], scalar1=0.5, scalar2=None, op0=ALU.mult),
                             reads=[bmid], writes=[bmid])
                        k.op("dve", lambda e: e.tensor_scalar(out=j128[:], in0=a128[:], scalar1=bmid[:, 0:1], scalar2=0.0,
                                                              op0=ALU.is_ge, op1=ALU.add, accum_out=bcnt[:]),
                             reads=[a128, bmid], writes=[j128, bcnt])
                        pt_ = self.psum()
                        k.op("pe", lambda e: e.matmul(pt_[:, 0:1], lhsT=Bd[:], rhs=bcnt[:], start=True, stop=True), reads=[Bd, bcnt], writes=[pt_])
                        k.op("dve", lambda e: e.tensor_scalar(out=bge[:], in0=pt_[:, 0:1], scalar1=float(cap), scalar2=None, op0=ALU.is_ge),
                             reads=[pt_], writes=[bge])
                        k.op("dve", lambda e: e.tensor_scalar(out=bnge[:], in0=pt_[:, 0:1], scalar1=float(cap), scalar2=None, op0=ALU.is_lt),
                             reads=[pt_], writes=[bnge])
                        k.op("dve", lambda e: e.tensor_tensor(out=bt1[:], in0=bmid[:], in1=blo[:], op=ALU.subtract), reads=[bmid, blo], writes=[bt1])
                        k.op("dve", lambda e: e.scalar_tensor_tensor(out=blo[:], in0=bt1[:], scalar=bge[:, 0:1], in1=blo[:],
                                                                     op0=ALU.mult, op1=ALU.add), reads=[bt1, bge, blo], writes=[blo])
                        k.op("dve", lambda e: e.tensor_tensor(out=bt1[:], in0=bmid[:], in1=bhi[:], op=ALU.subtract), reads=[bmid, bhi], writes=[bt1])
                        k.op("dve", lambda e: e.scalar_tensor_tensor(out=bhi[:], in0=bt1[:], scalar=bnge[:, 0:1], in1=bhi[:],
                                                                     op0=ALU.mult, op1=ALU.add), reads=[bt1, bnge, bhi], writes=[bhi])
                    p = self.psum()
                    k.op("pe", lambda e: e.matmul(p[0:16, 0:1], lhsT=GT8[:], rhs=blo[:], start=True, stop=True), reads=[GT8, blo], writes=[p])
                    k.op("dve", lambda e: e.tensor_copy(out=lo[:], in_=p[0:16, 0:1]), reads=[p], writes=[lo])
                else:
                    k.op("dve", lambda e: e.memset(lo[:], 0.0), writes=[lo])
                    k.op("dve", lambda e: e.memset(hi[:], 1.5), writes=[hi])
                    for _ in range(n_iter):
                        k.op("dve", lambda e: e.tensor_tensor(out=mid[:], in0=lo[:], in1=hi[:], op=ALU.add), reads=[lo, hi], writes=[mid])
                        k.op("dve", lambda e: e.tensor_scalar(out=mid[:], in0=mid[:], scalar1=0.5, scalar2=None, op0=ALU.mult),
                             reads=[mid], writes=[mid])
                        k.op("dve", lambda e: e.tensor_scalar(out=junk[:], in0=affT[:], scalar1=mid[:, 0:1], scalar2=0.0,
                                                              op0=ALU.is_ge, op1=ALU.add, accum_out=cnt[:]),
                             reads=[affT, mid], writes=[junk, cnt])
                        k.op("dve", lambda e: e.tensor_scalar(out=ge[:], in0=cnt[:], scalar1=float(cap), scalar2=None, op0=ALU.is_ge),
                             reads=[cnt], writes=[ge])
                        k.op("dve", lambda e: e.tensor_scalar(out=nge[:], in0=cnt[:], scalar1=float(cap), scalar2=None, op0=ALU.is_lt),
                             reads=[cnt], writes=[nge])
                        k.op("dve", lambda e: e.tensor_tensor(out=t1[:], in0=mid[:], in1=lo[:], op=ALU.subtract), reads=[mid, lo], writes=[t1])
                        k.op("dve", lambda e: e.scalar_tensor_tensor(out=lo[:], in0=t1[:], scalar=ge[:, 0:1], in1=lo[:],
                                                                     op0=ALU.mult, op1=ALU.add), reads=[t1, ge, lo], writes=[lo])
                        k.op("dve", lambda e: e.tensor_tensor(out=t1[:], in0=mid[:], in1=hi[:], op=ALU.subtract), reads=[mid, hi], writes=[t1])
                        k.op("dve", lambda e: e.scalar_tensor_tensor(out=hi[:], in0=t1[:], scalar=nge[:, 0:1], in1=hi[:],
                                                                     op0=ALU.mult, op1=ALU.add), reads=[t1, nge, hi], writes=[hi])
                k.op("dve", lambda e: e.tensor_scalar(out=mask[:], in0=affT[:], scalar1=lo[:, 0:1], scalar2=None, op0=ALU.is_ge),
                     reads=[affT, lo], writes=[mask])
                k.op("dve", lambda e: e.memset(junk[:], 1.0), writes=[junk])
                k.op("dve", lambda e: e.tensor_tensor_scan(out=cum[:], data0=junk[:], data1=mask[:], initial=0.0,
                                                           op0=ALU.mult, op1=ALU.add), reads=[junk, mask], writes=[cum])
                k.op("dve", lambda e: e.tensor_tensor(out=cum[:], in0=cum[:], in1=mask[:], op=ALU.mult), reads=[cum, mask], writes=[cum])
                k.op("dve", lambda e: e.tensor_scalar(out=cum[:], in0=cum[:], scalar1=-1.0, scalar2=None, op0=ALU.add),
                     reads=[cum], writes=[cum])
                pmT = self.sb(ls, "mo_pmT", [128, NT, 16], F32)
                gT = self.sb(ls, "mo_gT", [128, NT, 16], F32)
                rT = self.sb(ls, "mo_rT", [128, NT, 16], F32)
                GB = self.sb(ls, "mo_GB", [128, NT, NE, 5], BF16)
                gp = self.sb(ls, "mo_gp", [128, NT, 16], BF16)
                hl = self.sb(ls, "mo_hl", [128, NT, 2], F32)
                for tc in range(NT):
                    cs = slice(tc * 128, (tc + 1) * 128)
                    p = self.psum()
                    k.op("pe", lambda e: e.transpose(p[:, 0:16], cum[:, cs], self.ident[0:16, 0:16]),
                         reads=[cum, self.ident], writes=[p])
                    k.op("pe", lambda e: e.transpose(p[:, 16:32], affT[:, cs], self.ident[0:16, 0:16]),
                         reads=[affT, self.ident], writes=[p])
                    k.op("act", lambda e: e.copy(out=pmT[:, tc, :], in_=p[:, 0:16]), reads=[p], writes=[pmT])
                    k.op("act", lambda e: e.copy(out=gT[:, tc, :], in_=p[:, 16:32]), reads=[p], writes=[gT])
                    k.op("dve", lambda e: e.memset(hl[:, tc, 0:1], float(tc)), writes=[hl])
                k.op("dve", lambda e: e.tensor_copy(out=hl[:, :, 1:2], in_=self.pidx[:, 0:1].unsqueeze(1).to_broadcast([128, NT, 1])),
                     reads=[self.pidx], writes=[hl])
                for piece in range(3):
                    k.op("dve", lambda e: e.tensor_copy(out=gp[:], in_=gT[:]), reads=[gT], writes=[gp])
                    k.op("dve", lambda e: e.tensor_copy(out=GB[:, :, :, 2 + piece], in_=gp[:]), reads=[gp], writes=[GB])
                    if piece < 2:
                        k.op("dve", lambda e: e.tensor_copy(out=rT[:], in_=gp[:]), reads=[gp], writes=[rT])
                        k.op("dve", lambda e: e.tensor_tensor(out=gT[:], in0=gT[:], in1=rT[:], op=ALU.subtract), reads=[gT, rT], writes=[gT])
                for c2 in range(2):
                    k.op("dve", lambda e: e.tensor_copy(out=GB[:, :, :, c2], in_=hl[:, :, c2:c2 + 1].to_broadcast([128, NT, NE])),
                         reads=[hl], writes=[GB])
                ohs = [self.sb(ls, f"mo_oh{i}", [128, cap], BF16) for i in range(4)]
                selS = [self.sb(ls, f"mo_selS{i}", [8, cap], F32) for i in range(2)]
                selT = [self.sb(ls, f"mo_selT{i}", [128, NJT, 5], F32) for i in range(2)]
                n = 0
                for ex in range(NE):
                    p = self.psum()
                    for tc in range(NT):
                        oh = ohs[n % 4]
                        n += 1
                        if True:
                            k.op("dve", lambda e: e.tensor_scalar(out=oh[:], in0=self.iota_h[:, 0:cap], scalar1=pmT[:, tc, ex:ex + 1],
                                                                  scalar2=None, op0=ALU.is_equal),
                                 reads=[self.iota_h, pmT], writes=[oh])
                        k.op("pe", lambda e: e.matmul(p[0:5, 0:cap], lhsT=GB[:, tc, ex, :], rhs=oh[:], start=(tc == 0), stop=(tc == NT - 1)),
                             reads=[GB, oh], writes=[p])
                    sS, sT = selS[ex % 2], selT[ex % 2]
                    k.op("act", lambda e: e.copy(out=sS[0:5, :], in_=p[0:5, 0:cap]), reads=[p], writes=[sS])
                    p2 = self.psum()
                    for jt in range(NJT):
                        k.op("pe", lambda e: e.transpose(p2[0:CW, jt * 8:jt * 8 + 5], sS[0:5, jt * CW:(jt + 1) * CW], self.ident[0:5, 0:5]),
                             reads=[sS, self.ident], writes=[p2])
                    p2v = p2.t[:, 0:NJT * 8].rearrange("p (j c) -> p j c", c=8)
                    k.op("act", lambda e: e.copy(out=sT[0:CW, :, :], in_=p2v[0:CW, :, 0:5]), reads=[p2], writes=[sT])
                    cols = slice(ex * NJT, (ex + 1) * NJT)
                    k.op("dve", lambda e: e.scalar_tensor_tensor(out=sT[0:CW, :, 0], in0=sT[0:CW, :, 0], scalar=128.0, in1=sT[0:CW, :, 1],
                                                                 op0=ALU.mult, op1=ALU.add), reads=[sT], writes=[sT])
                    k.op("dve", lambda e: e.tensor_copy(out=idx[0:CW, cols], in_=sT[0:CW, :, 0]), reads=[sT], writes=[idx])
                    k.op("dve", lambda e: e.tensor_tensor(out=sT[0:CW, :, 2], in0=sT[0:CW, :, 2], in1=sT[0:CW, :, 3], op=ALU.add),
                         reads=[sT], writes=[sT])
                    k.op("dve", lambda e: e.tensor_tensor(out=gsel[0:CW, cols], in0=sT[0:CW, :, 2], in1=sT[0:CW, :, 4], op=ALU.add),
                         reads=[sT], writes=[gsel])
                if dbg is not None:
                    k.dma("sp", lambda e: e.dma_start(out=dbg[0], in_=idx[:]), reads=[idx])
                    k.dma("sp", lambda e: e.dma_start(out=dbg[1], in_=gsel[:]), reads=[gsel])
                    k.dma("sp", lambda e: e.dma_start(out=dbg[2], in_=affT[:]), reads=[affT])
                    k.dma("sp", lambda e: e.dma_start(out=dbg[3], in_=cum[:]), reads=[cum])
                self.k.barrier()

    def moe_experts(self, streams, wg_d, wu_d, wd_d):
        k = self.k
        lat_sc = Buf(None, "lat_scatter")
        for s_ in streams:
            T = s_["T"]
            s_["cap"] = 2 * T // NE
            s_["CW"] = min(s_["cap"], 128)
            s_["NJT"] = s_["cap"] // s_["CW"]
            s_["reg"] = self.nc.gpsimd.to_reg(T - 1)
        if True:
            with ExitStack() as ls:
                W = [[self.sb(ls, f"mo_w{n_}{i}", [128, 8, 1024], BF16) for n_ in "gud"] for i in range(2)]
                for si, s_ in enumerate(streams):
                    cap = s_["cap"]
                    s_["xs"] = [self.sb(ls, f"mo_xs{si}_{i}", [128, 1024], BF16) for i in range(4 if cap > 128 else 2)]
                    s_["xsT"] = [self.sb(ls, f"mo_xsT{si}_{i}", [128, 8, cap], BF16) for i in range(2)]
                    s_["hidT"] = [self.sb(ls, f"mo_hidT{si}_{i}", [128, 8, cap], BF16) for i in range(2)]
                    s_["sg"] = [self.sb(ls, f"mo_sg{si}_{i}", [128, cap], F32) for i in range(2)]
                    s_["ys"] = [self.sb(ls, f"mo_ys{si}_{i}", [128, 1024], F32) for i in range(2)]
                    s_["nys"] = 0

                def load_w(ex):
                    for n_, src in enumerate((wg_d, wu_d, wd_d)):
                        self.load_w_bf(W[ex % 2][n_], src[ex], 0, 1024)

                def gather(ex):
                    for s_ in streams:
                        NJT, CW = s_["NJT"], s_["CW"]
                        for jt in range(NJT):
                            col = ex * NJT + jt
                            x_ = s_["xs"][(ex * NJT + jt) % len(s_["xs"])]
                            idx, h2_d, reg = s_["idx"], s_["h2_d"], s_["reg"]
                            k.dma("pool", lambda e: e.indirect_dma_start(
                                out=x_[0:CW, :], out_offset=None, in_=h2_d[:, :],
                                in_offset=bass.IndirectOffsetOnAxis(ap=idx[0:CW, col:col + 1], axis=0),
                                bounds_check=reg, oob_is_err=False), reads=[idx], writes=[x_])

                load_w(0)
                gather(0)
                def xpose(ex):
                    for s_ in streams:
                        NJT, CW = s_["NJT"], s_["CW"]
                        xT = s_["xsT"][ex % 2]
                        for jt in range(NJT):
                            x_ = s_["xs"][(ex * NJT + jt) % len(s_["xs"])]
                            p = self.psum()
                            pv = pbf(p)[:, 0:1024].rearrange("p (c t) -> p c t", c=8)
                            for c in range(8):
                                k.op("pe", lambda e: e.transpose(pv[:, c, 0:CW], x_[0:CW, c * 128:(c + 1) * 128], self.identb[0:CW, 0:CW]),
                                     reads=[x_, self.identb], writes=[p])
                            k.op("act", lambda e: e.copy(out=xT[:, :, jt * CW:(jt + 1) * CW], in_=pv[:, :, 0:CW]), reads=[p], writes=[xT])

                xpose(0)
                if NE > 1:
                    load_w(1)
                for ex in range(NE):
                    wg, wu, wd = W[ex % 2]
                    for s_ in streams:
                        cap, NJT, CW = s_["cap"], s_["NJT"], s_["CW"]
                        xT, hidT, sg = s_["xsT"][ex % 2], s_["hidT"][ex % 2], s_["sg"]
                        for fc in range(8):
                            pg, pu = self.psum(), self.psum()
                            for (p, w) in ((pg, wg), (pu, wu)):
                                for c in range(8):
                                    k.op("pe", lambda e: e.matmul(p[:, 0:cap], lhsT=w[:, c, fc * 128:(fc + 1) * 128], rhs=xT[:, c, :],
                                                                  start=(c == 0), stop=(c == 7)), reads=[w, xT], writes=[p])
                            sg_ = sg[fc % 2]
                            k.op("act", lambda e: e.activation(out=sg_[:], in_=pg[:, 0:cap], func=AF.Silu), reads=[pg], writes=[sg_])
                            k.op("dve", lambda e: e.tensor_tensor(out=hidT[:, fc, :], in0=sg_[:], in1=pu[:, 0:cap], op=ALU.mult),
                                 reads=[sg_, pu], writes=[hidT])
                    if ex + 1 < NE:
                        gather(ex + 1)
                        xpose(ex + 1)
                    for s_ in streams:
                        cap, NJT, CW = s_["cap"], s_["NJT"], s_["CW"]
                        hidT, idx, gsel, G2, lat_d, reg = s_["hidT"][ex % 2], s_["idx"], s_["gsel"], s_["G2"], s_["lat_d"], s_["reg"]
                        for jt in range(NJT):
                            col = ex * NJT + jt
                            p = self.psum()
                            for fc in range(8):
                                for h in range(2):
                                    k.op("pe", lambda e: e.matmul(p[0:CW, h * 512:(h + 1) * 512], lhsT=hidT[:, fc, jt * CW:(jt + 1) * CW],
                                                                  rhs=wd[:, fc, h * 512:(h + 1) * 512], start=(fc == 0), stop=(fc == 7)),
                                         reads=[hidT, wd], writes=[p])
                            y_ = s_["ys"][s_["nys"] % 2]
                            s_["nys"] += 1
                            k.op("dve", lambda e: e.scalar_tensor_tensor(out=y_[0:CW, :], in0=p[0:CW, :], scalar=gsel[0:CW, col:col + 1],
                                                                         in1=G2[0:CW, :], op0=ALU.mult, op1=ALU.mult),
                                 reads=[p, gsel, G2], writes=[y_])
                            k.dma("pool", lambda e: e.indirect_dma_start(
                                out=lat_d[:, :], out_offset=bass.IndirectOffsetOnAxis(ap=idx[0:CW, col:col + 1], axis=0),
                                in_=y_[0:CW, :], in_offset=None, bounds_check=reg, oob_is_err=False, compute_op=ALU.add),
                                reads=[idx, y_], writes=[lat_sc])
                    if ex + 2 < NE:
                        load_w(ex + 2)
                self.k.barrier()

    def moe_phase(self, T, affT, h2_d, lat_d, G2, wg_d, wu_d, wd_d, tag, n_iter=30, dbg=None):
        cap = 2 * T // NE
        NJT = cap // min(cap, 128)
        with ExitStack() as ms:
            idx = self.sb(ms, "mo_idx", [128, NE * NJT], I32)
            gsel = self.sb(ms, "mo_gsel", [128, NE * NJT], F32)
            self.moe_route(T, affT, idx, gsel, n_iter=n_iter, dbg=dbg)
            self.moe_experts([dict(T=T, idx=idx, gsel=gsel, h2_d=h2_d, lat_d=lat_d, G2=G2)], wg_d, wu_d, wd_d)
            self.k.barrier()


class Attn(Moe):
    def phase_b_attn(self, hT, hcT, T, LC, w_qkv, qg_pm, kg_pm, rope_d, rotT_d, zT_d):
        k = self.k
        NK = LC + T
        NKC = NK // 128
        QB = min(512, T)
        with ExitStack() as ls:
            wq = self.sb(ls, "at_wq", [128, 8, 1536], BF16)
            for q in range(3):
                self.k.dma("pool", lambda e: e.dma_start(out=wq[:, :, q * 512:(q + 1) * 512],
                                                        in_=_pm_view(w_qkv)[:, :, q * 512:(q + 1) * 512]), writes=[wq])
            gq = self.sb(ls, "at_gq", [128, 1], F32)
            gk = self.sb(ls, "at_gk", [128, 1], F32)
            k.dma("sp", lambda e: e.dma_start(out=gq[:], in_=qg_pm), writes=[gq])
            k.dma("sp", lambda e: e.dma_start(out=gk[:], in_=kg_pm), writes=[gk])
            rotf = self.sb(ls, "at_rotf", [128, 128], F32)
            rot = self.sb(ls, "at_rot", [128, 128], BF16)
            k.dma("sp", lambda e: e.dma_start(out=rotf[:], in_=rotT_d), writes=[rotf])
            k.op("dve", lambda e: e.tensor_copy(out=rot[:], in_=rotf[:]), reads=[rotf], writes=[rot])
            negc = self.sb(ls, "at_negc", [128, 1], F32)
            ab = self.sb(ls, "at_ab", [128, 2], F32)
            k.op("act", lambda e: e.activation(out=ab[:, 0:1], in_=gq[:], func=AF.Abs), reads=[gq], writes=[ab])
            k.op("act", lambda e: e.activation(out=ab[:, 1:2], in_=gk[:], func=AF.Abs), reads=[gk], writes=[ab])
            p = self.psum()
            k.op("pe", lambda e: e.transpose(p[0:2, 0:128], ab[:, 0:2], self.ident[:]), reads=[ab, self.ident], writes=[p])
            mx = self.sb(ls, "at_mx", [2, 1], F32)
            k.op("dve", lambda e: e.tensor_reduce(out=mx[:], in_=p[0:2, 0:128], axis=AX.X, op=ALU.max), reads=[p], writes=[mx])
            mq = self.sb(ls, "at_mq", [128, 2], F32)
            for i in range(2):
                p = self.psum()
                sel = self.sb(ls, f"at_sel{i}", [2, 128], F32)
                k.op("dve", lambda e: e.memset(sel[:], 0.0), writes=[sel])
                k.op("dve", lambda e: e.tensor_scalar(out=sel[:], in0=self.ones_f[0:2, :], scalar1=self.ident[0:2, i:i + 1], scalar2=None,
                                                      op0=ALU.mult), reads=[self.ones_f, self.ident], writes=[sel])
                k.op("pe", lambda e: e.matmul(p[:, 0:1], lhsT=sel[:], rhs=mx[:], start=True, stop=True), reads=[sel, mx], writes=[p])
                k.op("dve", lambda e: e.tensor_copy(out=mq[:, i:i + 1], in_=p[:, 0:1]), reads=[p], writes=[mq])
            k.op("dve", lambda e: e.scalar_tensor_tensor(out=negc[:], in0=mq[:, 0:1], scalar=-math.sqrt(HD), in1=mq[:, 1:2],
                                                         op0=ALU.mult, op1=ALU.mult), reads=[mq], writes=[negc])

            kT = [self.sb(ls, f"at_kT{i}", [128, NK], BF16) for i in range(NKVH)]
            V = self.sb(ls, "at_V", [128, NKC, NKVH, 129], BF16)
            k.op("dve", lambda e: e.memset(V[:, :, :, 128:129], 1.0), writes=[V])
            sq = [self.sb(ls, "at_sq0", [128, QB], F32)] * 2
            rst = [self.sb(ls, f"at_rst{i}", [128, QB], F32) for i in range(2)]
            xn = [self.sb(ls, f"at_xn{i}", [128, QB], F32) for i in range(2)]
            xnb = [self.sb(ls, f"at_xnb{i}", [128, QB], BF16) for i in range(2)]
            rp = [self.sb(ls, f"at_rp{i}", [128, 2, QB], F32) for i in range(2)]
            tmp = [self.sb(ls, "at_tmp0", [128, QB], F32)] * 2
            cnt = [0]

            def qk_block(srcT, col0, w, wcol, g, dst_ap, rope_blk):
                i = cnt[0] % 2
                cnt[0] += 1
                pq = self.psum()
                for c in range(8):
                    k.op("pe", lambda e: e.matmul(pq[:, 0:w], lhsT=wq[:, c, wcol:wcol + 128], rhs=srcT[:, c, col0:col0 + w],
                                                  start=(c == 0), stop=(c == 7)), reads=[wq, srcT], writes=[pq])
                k.op("act", lambda e: e.activation(out=sq[i][:, 0:w], in_=pq[:, 0:w], func=AF.Square), reads=[pq], writes=[sq[i]])
                k.op("pe", lambda e: e.matmul(pq[:, 512:512 + w], lhsT=self.ones_f[:], rhs=sq[i][:, 0:w], start=True, stop=True),
                     reads=[self.ones_f, sq[i]], writes=[pq])
                k.op("act", lambda e: e.activation(out=rst[i][:, 0:w], in_=pq[:, 512:512 + w], func=AF.Sqrt, bias=self.eps_t[:, 0:1],
                                                   scale=1.0 / HD), reads=[pq, self.eps_t], writes=[rst[i]])
                k.op("dve", lambda e: e.reciprocal(out=rst[i][:, 0:w], in_=rst[i][:, 0:w]), reads=[rst[i]], writes=[rst[i]])
                if rope_blk is None:
                    k.op("dve", lambda e: e.scalar_tensor_tensor(out=dst_ap[0], in0=pq[:, 0:w], scalar=g[:, 0:1], in1=rst[i][:, 0:w],
                                                                 op0=ALU.mult, op1=ALU.mult), reads=[pq, g, rst[i]], writes=[dst_ap[1]])
                    return
                k.op("dve", lambda e: e.scalar_tensor_tensor(out=xn[i][:, 0:w], in0=pq[:, 0:w], scalar=g[:, 0:1], in1=rst[i][:, 0:w],
                                                             op0=ALU.mult, op1=ALU.mult), reads=[pq, g, rst[i]], writes=[xn[i]])
                k.op("act", lambda e: e.copy(out=xnb[i][:, 0:w], in_=xn[i][:, 0:w]), reads=[xn[i]], writes=[xnb[i]])
                pr = self.psum()
                k.op("pe", lambda e: e.matmul(pr[:, 0:w], lhsT=rot[:], rhs=xnb[i][:, 0:w], start=True, stop=True),
                     reads=[rot, xnb[i]], writes=[pr])
                k.op("dve", lambda e: e.tensor_tensor(out=tmp[i][:, 0:w], in0=pr[:, 0:w], in1=rope_blk[:, 1, 0:w], op=ALU.mult),
                     reads=[pr, rope_blk], writes=[tmp[i]])
                k.op("pool", lambda e: e.tensor_tensor(out=xn[i][:, 0:w], in0=xn[i][:, 0:w], in1=rope_blk[:, 0, 0:w], op=ALU.mult),
                     reads=[xn[i], rope_blk], writes=[xn[i]])
                k.op("dve", lambda e: e.tensor_tensor(out=dst_ap[0], in0=xn[i][:, 0:w], in1=tmp[i][:, 0:w], op=ALU.add),
                     reads=[xn[i], tmp[i]], writes=[dst_ap[1]])

            def v_tile(srcT, col0, kc):
                pv = self.psum()
                for c in range(8):
                    k.op("pe", lambda e: e.matmul(pv[:, 0:256], lhsT=srcT[:, c, col0:col0 + 128], rhs=wq[:, c, 1280:1536],
                                                  start=(c == 0), stop=(c == 7)), reads=[wq, srcT], writes=[pv])
                k.op("act", lambda e: e.copy(out=V[:, kc, :, 0:128], in_=pv[:, 0:256].rearrange("p (a d) -> p a d", a=NKVH)),
                     reads=[pv], writes=[V])

            for b0 in range(0, LC, QB):
                w = min(QB, LC - b0)
                for kv in range(NKVH):
                    qk_block(hcT, b0, w, 1024 + kv * 128, gk, (kT[kv][:, b0:b0 + w], kT[kv]), None)
            for t0 in range(0, LC, 128):
                v_tile(hcT, t0, t0 // 128)
            for qb in range(T // QB):
                rb = rp[qb % 2]
                k.dma("sp", lambda e: e.dma_start(out=rb[:, :, 0:QB], in_=rope_d[:, :, qb * QB:(qb + 1) * QB].rearrange("a p t -> p a t")),
                      writes=[rb])
                for kv in range(NKVH):
                    qk_block(hT, qb * QB, QB, 1024 + kv * 128, gk, (kT[kv][:, LC + qb * QB:LC + (qb + 1) * QB], kT[kv]), rb)
            for t0 in range(0, T, 128):
                v_tile(hT, t0, (LC + t0) // 128)
            NS = QB // 128
            qTb = [self.sb(ls, f"at_qT{i}", [128, QB], BF16) for i in range(2)]
            pT = [self.sb(ls, f"at_pT{i}", [128, 2 * QB], BF16) for i in range(3)]
            rden = [self.sb(ls, f"at_rden{i}", [128, 1], F32) for i in range(2)]
            otok = [self.sb(ls, f"at_ot{i}", [128, 1024], BF16) for i in range(2 * NS)]
            scale = 1.0 / math.sqrt(HD)
            groups = [(g0, min(2, NKC - g0)) for g0 in range(0, NKC, 2)]
            n = 0
            items = [(qb, h) for qb in range(T // QB) for h in range(NQH)]

            def prep(i):
                qb, h = items[i]
                rb = rp[qb % 2]
                if h == 0:
                    k.dma("sp", lambda e: e.dma_start(out=rb[:, :, 0:QB], in_=rope_d[:, :, qb * QB:(qb + 1) * QB].rearrange("a p t -> p a t")),
                          writes=[rb])
                qT = qTb[i % 2]
                qk_block(hT, qb * QB, QB, h * 128, gq, (qT[:, 0:QB], qT), rb)

            prep(0)
            for it_, (qb, h) in enumerate(items):
                if True:
                    kv = h // (NQH // NKVH)
                    qT = qTb[it_ % 2]
                    if it_ + 1 < len(items):
                        prep(it_ + 1)
                    accs = [self.ps[0], self.ps[1]]

                    def emit_s(gi):
                        g0, ng = groups[gi]
                        st_ = self.ps[2 + (gi % 2)]
                        for j in range(ng):
                            kc = g0 + j
                            k.op("pe", lambda e: e.matmul(st_[:, j * 512:j * 512 + QB], lhsT=kT[kv][:, kc * 128:(kc + 1) * 128], rhs=qT[:, 0:QB],
                                                          start=True, stop=True), reads=[kT[kv], qT], writes=[st_])
                        return st_

                    sts = {0: emit_s(0)}
                    for gi, (g0, ng) in enumerate(groups):
                        if gi + 1 < len(groups):
                            sts[gi + 1] = emit_s(gi + 1)
                        st_ = sts.pop(gi)
                        pt = pT[n % 3]
                        n += 1
                        if QB == 512:
                            k.op("act", lambda e: e.activation(out=pt[:, 0:ng * 512], in_=st_[:, 0:ng * 512], func=AF.Exp, bias=negc[:, 0:1], scale=scale),
                                 reads=[st_, negc], writes=[pt])
                        else:
                            for j in range(ng):
                                k.op("act", lambda e: e.activation(out=pt[:, j * QB:(j + 1) * QB], in_=st_[:, j * 512:j * 512 + QB], func=AF.Exp,
                                                                   bias=negc[:, 0:1], scale=scale), reads=[st_, negc], writes=[pt])
                        for j in range(ng):
                            kc = g0 + j
                            for s_ in range(NS):
                                a_ = accs[s_ // 2]
                                c0 = (s_ % 2) * 512
                                k.op("pe", lambda e: e.matmul(a_[:, c0:c0 + 129], lhsT=pt[:, j * QB + s_ * 128:j * QB + (s_ + 1) * 128],
                                                              rhs=V[:, kc, kv, :], start=(kc == 0), stop=(kc == NKC - 1)),
                                     reads=[pt, V], writes=[a_])
                    for s_ in range(NS):
                        a_ = accs[s_ // 2]
                        c0 = (s_ % 2) * 512
                        rd = rden[s_ % 2]
                        ot = otok[(qb % 2) * NS + s_]
                        k.op("dve", lambda e: e.reciprocal(out=rd[:], in_=a_[:, c0 + 128:c0 + 129]), reads=[a_], writes=[rd])
                        k.op("dve", lambda e: e.tensor_scalar(out=ot[:, h * 128:(h + 1) * 128], in0=a_[:, c0:c0 + 128], scalar1=rd[:, 0:1],
                                                              scalar2=None, op0=ALU.mult), reads=[a_, rd], writes=[ot])
                for s_ in range(NS if h == NQH - 1 else 0):
                    ot = otok[(qb % 2) * NS + s_]
                    r0 = qb * QB + s_ * 128
                    k.dma("sp", lambda e: e.dma_start(out=zT_d[r0:r0 + 128, :], in_=ot[:]), reads=[ot])
            self.k.barrier()


TWO_PI = 2.0 * math.pi


class Hyena(Attn):
    def hy_filter(self, L, cst, wts, scr, fft=None):
        k = self.k
        NLT = L // 128
        NFT = NLT
        LB = min(512, L)
        with ExitStack() as fs:
            rS = [self.sb(fs, f"hf_rS{o}", [128, 1024], F32) for o in range(2)]
            with ExitStack() as ls:
                zT = self.sb(ls, "hf_zT", [33, L], F32)
                w1 = self.sb(ls, "hf_w1", [33, 64], F32)
                w2 = self.sb(ls, "hf_w2", [64, 64], F32)
                w3 = self.sb(ls, "hf_w3", [64, 4096], F32)
                sm = self.sb(ls, "hf_sm", [64, 4], F32)
                fb = self.sb(ls, "hf_fb", [64, 2], F32)
                a1T = self.sb(ls, "hf_a1T", [64, L], F32)
                a2T = self.sb(ls, "hf_a2T", [64, L + 1], F32)
                arg = self.sb(ls, "hf_arg", [64, LB], F32)
                m1 = self.sb(ls, "hf_m1", [64, LB], F32)
                k.dma("sp", lambda e: e.dma_start(out=zT[:], in_=cst["zT"]), writes=[zT])
                k.dma("sp", lambda e: e.dma_start(out=w1[:], in_=wts["w1"]), writes=[w1])
                k.dma("sp", lambda e: e.dma_start(out=w2[:], in_=wts["w2"]), writes=[w2])
                k.dma("sp", lambda e: e.dma_start(out=w3[:], in_=wts["w3"]), writes=[w3])
                k.dma("sp", lambda e: e.dma_start(out=sm[:, 0:1], in_=wts["b1"]), writes=[sm])
                k.dma("sp", lambda e: e.dma_start(out=sm[:, 1:2], in_=wts["b2"]), writes=[sm])
                k.dma("sp", lambda e: e.dma_start(out=sm[:, 2:3], in_=wts["freq"]), writes=[sm])
                k.op("dve", lambda e: e.tensor_scalar(out=fb[:], in0=sm[:, 0:2], scalar1=sm[:, 2:3], scalar2=None, op0=ALU.mult),
                     reads=[sm], writes=[fb])
                k.op("dve", lambda e: e.memset(a2T[:, L:L + 1], 0.0), writes=[a2T])
                for (wl, kdim, src, dst, bi) in ((w1, 33, zT, a1T, 0), (w2, 64, a1T, a2T, 1)):
                    for b0 in range(0, L, LB):
                        p = self.psum()
                        k.op("pe", lambda e: e.matmul(p[0:64, 0:LB], lhsT=wl[0:kdim, :], rhs=src[0:kdim, b0:b0 + LB], start=True, stop=True),
                             reads=[wl, src], writes=[p])
                        k.op("dve", lambda e: e.tensor_scalar(out=arg[:], in0=p[0:64, 0:LB], scalar1=sm[:, 2:3], scalar2=fb[:, bi:bi + 1],
                                                              op0=ALU.mult, op1=ALU.add), reads=[p, sm, fb], writes=[arg])
                        k.op("dve", lambda e: e.tensor_scalar(out=m1[:], in0=arg[:], scalar1=math.pi, scalar2=-TWO_PI,
                                                              op0=ALU.is_gt, op1=ALU.mult), reads=[arg], writes=[m1])
                        k.op("dve", lambda e: e.tensor_tensor(out=m1[:], in0=m1[:], in1=arg[:], op=ALU.add), reads=[m1, arg], writes=[m1])
                        k.op("dve", lambda e: e.tensor_scalar(out=arg[:], in0=arg[:], scalar1=-math.pi, scalar2=TWO_PI,
                                                              op0=ALU.is_lt, op1=ALU.mult), reads=[arg], writes=[arg])
                        k.op("dve", lambda e: e.tensor_tensor(out=arg[:], in0=m1[:], in1=arg[:], op=ALU.add), reads=[m1, arg], writes=[arg])
                        k.op("act", lambda e: e.activation(out=dst[:, b0:b0 + LB], in_=arg[:], func=AF.Sin), reads=[arg], writes=[dst])
                a2b = self.sb(ls, "hf_a2b", [64, L + 1], BF16)
                w3b = self.sb(ls, "hf_w3b", [64, 4096], BF16)
                k.op("dve", lambda e: e.tensor_copy(out=a2b[:], in_=a2T[:]), reads=[a2T], writes=[a2b])
                k.op("act", lambda e: e.copy(out=w3b[:], in_=w3[:]), reads=[w3], writes=[w3b])
                dl = self.sb(ls, "hf_dl", [128, 1024], F32)
                tl = self.sb(ls, "hf_tl", [128, NLT, 2], F32)
                k.dma("sp", lambda e: e.dma_start(out=dl[:], in_=cst["delta"][0:1, :].partition_broadcast(128)), writes=[dl])
                k.dma("sp", lambda e: e.dma_start(out=tl[:], in_=cst["tl"]), writes=[tl])
                dec = [[self.sb(ls, f"hf_dec{i}{j}", [128, 1024], F32) for j in range(2)] for i in range(2)]
                fbb = [[self.sb(ls, f"hf_f{i}{j}", [128, 1024], F32) for j in range(2)] for i in range(2)]
                abb = [[self.sb(ls, f"hf_a{i}{j}", [128, 1024], BF16) for j in range(2)] for i in range(2)]
                kpm = [self.sb(ls, f"hf_kpm{i}", [128, 1024], BF16) for i in range(4)]
                Sps = self.ps[3]
                nps = 0
                pend = []

                def flush_one():
                    a_s, lt_s = pend.pop(0)
                    for h in range(2):
                        k.op("pe", lambda e: e.matmul(Sps[:, h * 512:(h + 1) * 512], lhsT=self.ones_b[:], rhs=a_s[:, h * 512:(h + 1) * 512],
                                                      start=(lt_s == 0), stop=(lt_s == NLT - 1)), reads=[self.ones_b, a_s], writes=[Sps])

                for o in range(2):
                    for lt in range(NLT):
                        par = lt % 2
                        for dr in (0, 1):
                            p = self.ps[nps % 3]
                            nps += 1
                            c0 = dr * 2048 + o * 1024
                            for h in range(2):
                                k.op("pe", lambda e: e.matmul(p[:, h * 512:(h + 1) * 512], lhsT=a2b[:, lt * 128 + dr:lt * 128 + dr + 128],
                                                              rhs=w3b[:, c0 + h * 512:c0 + (h + 1) * 512], start=True, stop=True),
                                     reads=[a2b, w3b], writes=[p])
                            d_, f_, a_ = dec[par][dr], fbb[par][dr], abb[par][dr]
                            k.op("act", lambda e: e.activation(out=d_[:], in_=dl[:], func=AF.Exp, scale=tl[:, lt, dr:dr + 1]),
                                 reads=[dl, tl], writes=[d_])
                            k.op("dve", lambda e: e.tensor_tensor(out=f_[:], in0=p[:], in1=d_[:], op=ALU.mult),
                                 reads=[p, d_], writes=[f_])
                            k.op("act", lambda e: e.activation(out=a_[:], in_=f_[:], func=AF.Abs), reads=[f_], writes=[a_])
                        f0, f1, a0, a1 = fbb[par][0], fbb[par][1], abb[par][0], abb[par][1]
                        kp_, km_ = kpm[par * 2], kpm[par * 2 + 1]
                        k.op("dve", lambda e: e.tensor_tensor(out=kp_[:], in0=f0[:], in1=f1[:], op=ALU.add), reads=[f0, f1], writes=[kp_])
                        k.op("dve", lambda e: e.tensor_tensor(out=km_[:], in0=f0[:], in1=f1[:], op=ALU.subtract), reads=[f0, f1], writes=[km_])
                        k.dma("sp", lambda e: e.dma_start(out=scr["kp"][o, lt * 128:(lt + 1) * 128, :], in_=kp_[:]), reads=[kp_])
                        k.dma("sp", lambda e: e.dma_start(out=scr["km"][o, lt * 128:(lt + 1) * 128, :], in_=km_[:]), reads=[km_])
                        k.op("pool", lambda e: e.tensor_tensor(out=a0[:], in0=a0[:], in1=a1[:], op=ALU.add), reads=[a0, a1], writes=[a0])
                        pend.append((a0, lt))
                        if len(pend) > 1:
                            flush_one()
                    while pend:
                        flush_one()
                    k.op("dve", lambda e: e.reciprocal(out=rS[o][:], in_=Sps[:]), reads=[Sps], writes=[rS[o]])
                self.k.barrier()
            if fft is not None:
                with ExitStack() as ls:
                    W1 = self.sb(ls, "hf_W1", [64, 128], BF16)
                    Tf = self.sb(ls, "hf_Tf", [128, 4, 64, 128], BF16)
                    sk = self.sb(ls, "hf_skf", [128, 1024], F32)
                    k.dma("sp", lambda e: e.dma_start(out=W1[:], in_=fft["W1"]), writes=[W1])
                    k.dma("sp", lambda e: e.dma_start(out=Tf[:], in_=fft["Tf"]), writes=[Tf])
                    for o in range(2):
                        k.dma("sp", lambda e: e.dma_start(out=sk[:], in_=wts["skip"][o:o + 1, :].partition_broadcast(128)), writes=[sk])
                        self.fft_stage1(scr["kp"][o], fft["Gp"], W1)
                        self.fft_stage1(scr["km"][o], fft["Gm"], W1)
                        self.fft_filter_stage2(fft["Gp"], fft["Gm"], Tf, rS[o], sk, fft["KA"][o], fft["KB"][o])
                    self.k.barrier()
                return
            with ExitStack() as ls:
                src = self.sb(ls, "hf_src", [128, NLT, 1024], BF16)
                mb = [self.sb(ls, f"hf_mb{i}", [128, NLT, 128], BF16) for i in range(2)]
                ph = self.sb(ls, "hf_ph", [128, NFT, 2], F32)
                sk = self.sb(ls, "hf_sk", [128, 1024], F32)
                At = [self.sb(ls, f"hf_At{i}", [128, 1024], F32) for i in range(2)]
                t1 = self.sb(ls, "hf_t1", [128, 1024], F32)
                t2 = self.sb(ls, "hf_t2", [128, 1024], F32)
                Ko = [self.sb(ls, f"hf_Ko{i}", [128, 1024], BF16) for i in range(4)]
                k.dma("sp", lambda e: e.dma_start(out=ph[:], in_=cst["ph"]), writes=[ph])
                for o in range(2):
                    k.dma("sp", lambda e: e.dma_start(out=sk[:], in_=wts["skip"][o:o + 1, :].partition_broadcast(128)), writes=[sk])
                    for pas in range(2):
                        sd = scr["kp"] if pas == 0 else scr["km"]
                        k.dma("sp", lambda e: e.dma_start(out=src[:], in_=sd[o].rearrange("(c p) n -> p c n", p=128)), writes=[src])
                        for ft in range(NFT):
                            m = mb[ft % 2]
                            k.dma("sp", lambda e: e.dma_start(out=m[:], in_=cst["Mb"][pas, ft]), writes=[m])
                            p = self.psum()
                            for h in range(2):
                                for c in range(NLT):
                                    k.op("pe", lambda e: e.matmul(p[:, h * 512:(h + 1) * 512], lhsT=m[:, c, :], rhs=src[:, c, h * 512:(h + 1) * 512],
                                                                  start=(c == 0), stop=(c == NLT - 1)), reads=[m, src], writes=[p])
                            rows = slice(ft * 128, (ft + 1) * 128)
                            if pas == 0:
                                a_ = At[ft % 2]
                                k.op("act", lambda e: e.copy(out=a_[:], in_=p[:]), reads=[p], writes=[a_])
                                k.dma("sp", lambda e: e.dma_start(out=scr["A"][rows, :], in_=a_[:]), reads=[a_])
                            else:
                                a_ = At[ft % 2]
                                kc_, ks_ = Ko[(ft % 2) * 2], Ko[(ft % 2) * 2 + 1]
                                cph, sph = ph[:, ft, 0:1], ph[:, ft, 1:2]
                                k.dma("sp", lambda e: e.dma_start(out=a_[:], in_=scr["A"][rows, :]), writes=[a_])
                                k.op("act", lambda e: e.activation(out=t1[:], in_=a_[:], func=AF.Copy, scale=cph),
                                     reads=[a_, ph], writes=[t1])
                                k.op("dve", lambda e: e.scalar_tensor_tensor(out=t1[:], in0=p[:], scalar=sph, in1=t1[:], op0=ALU.mult, op1=ALU.add),
                                     reads=[p, ph, t1], writes=[t1])
                                k.op("pool", lambda e: e.tensor_tensor(out=t1[:], in0=t1[:], in1=rS[o][:], op=ALU.mult), reads=[t1, rS[o]], writes=[t1])
                                k.op("pool", lambda e: e.tensor_tensor(out=kc_[:], in0=t1[:], in1=sk[:], op=ALU.add), reads=[t1, sk], writes=[kc_])
                                k.op("act", lambda e: e.activation(out=t2[:], in_=a_[:], func=AF.Copy, scale=sph),
                                     reads=[a_, ph], writes=[t2])
                                k.op("dve", lambda e: e.scalar_tensor_tensor(out=t2[:], in0=p[:], scalar=cph, in1=t2[:], op0=ALU.mult, op1=ALU.subtract),
                                     reads=[p, ph, t2], writes=[t2])
                                k.op("dve", lambda e: e.tensor_tensor(out=ks_[:], in0=t2[:], in1=rS[o][:], op=ALU.mult), reads=[t2, rS[o]], writes=[ks_])
                                k.dma("sp", lambda e: e.dma_start(out=scr["K"][o, 0, rows, :], in_=kc_[:]), reads=[kc_])
                                k.dma("sp", lambda e: e.dma_start(out=scr["K"][o, 1, rows, :], in_=ks_[:]), reads=[ks_])
                        self.k.barrier()
                self.k.barrier()
            self.k.barrier()

    def hy_proj(self, hT, T, w_in, cw_pm, src, g_d, z_d=None):
        k = self.k
        TB = min(512, T)
        NLT = T // 128
        with ExitStack() as ls:
            cw = self.sb(ls, "hp_cw", [128, 24, 3], F32)
            k.dma("sp", lambda e: e.dma_start(out=cw[:], in_=cw_pm), writes=[cw])
            wq = [self.sb(ls, f"hp_w{i}", [128, 8, 128], BF16) for i in range(2)]
            u = self.sb(ls, "hp_u", [128, T + 2], F32)
            acc = self.sb(ls, "hp_acc", [128, T], F32)
            row = self.sb(ls, "hp_row", [128, T], BF16)
            k.op("dve", lambda e: e.memset(u[:, 0:1], 0.0), writes=[u])
            k.op("dve", lambda e: e.memset(u[:, T + 1:T + 2], 0.0), writes=[u])
            n = 0
            for q in (1, 2, 0):
                for j in range(8):
                    w = wq[n % 2]
                    n += 1
                    self.load_w_bf(w, w_in, q * 1024 + j * 128, 128)
                    for tb in range(T // TB):
                        p = self.psum()
                        for c in range(8):
                            k.op("pe", lambda e: e.matmul(p[:, 0:TB], lhsT=w[:, c, :], rhs=hT[:, c, tb * TB:(tb + 1) * TB],
                                                          start=(c == 0), stop=(c == 7)), reads=[w, hT], writes=[p])
                        k.op("act", lambda e: e.copy(out=u[:, 1 + tb * TB:1 + (tb + 1) * TB], in_=p[:, 0:TB]), reads=[p], writes=[u])
                    ci = q * 8 + j
                    k.op("dve", lambda e: e.tensor_scalar(out=acc[:], in0=u[:, 0:T], scalar1=cw[:, ci, 0:1], scalar2=None, op0=ALU.mult),
                         reads=[u, cw], writes=[acc])
                    k.op("dve", lambda e: e.scalar_tensor_tensor(out=acc[:], in0=u[:, 1:T + 1], scalar=cw[:, ci, 1:2], in1=acc[:],
                                                                 op0=ALU.mult, op1=ALU.add), reads=[u, cw, acc], writes=[acc])
                    k.op("dve", lambda e: e.scalar_tensor_tensor(out=row[:], in0=u[:, 2:T + 2], scalar=cw[:, ci, 2:3], in1=acc[:],
                                                                 op0=ALU.mult, op1=ALU.add), reads=[u, cw, acc], writes=[row])
                    for l0 in range(0, NLT, 8):
                        nl = min(8, NLT - l0)
                        p = self.psum()
                        pv = pbf(p)[:, 0:1024].rearrange("p (c t) -> p c t", c=8)
                        for i in range(nl):
                            lt = l0 + i
                            k.op("pe", lambda e: e.transpose(pv[:, i, :], row[:, lt * 128:(lt + 1) * 128], self.identb[:]),
                                 reads=[row, self.identb], writes=[p])
                        k.op("act", lambda e: e.copy(out=src[:, l0:l0 + nl, j * 128:(j + 1) * 128], in_=pv[:, 0:nl, :]), reads=[p], writes=[src])
                if q != 0:
                    k.dma("sp", lambda e: e.dma_start(out=g_d[q - 1].rearrange("(c p) n -> p c n", p=128), in_=src[:]), reads=[src])
                elif z_d is not None:
                    k.dma("sp", lambda e: e.dma_start(out=z_d.rearrange("(c p) n -> p c n", p=128), in_=src[:]), reads=[src])
            self.k.barrier()

    def hy_conv(self, L, src, cst, K_d, g_d, Y_d, z_d, o):
        k = self.k
        NLT = L // 128
        NFT = NLT
        with ExitStack() as ls:
            mb = [self.sb(ls, f"hc_mb{i}", [128, NLT, 128], BF16) for i in range(4)]
            Kt = [self.sb(ls, f"hc_K{i}", [128, 1024], BF16) for i in range(4)]
            tt_ = [self.sb(ls, f"hc_t{i}", [128, 1024], F32) for i in range(4)]
            Yo = [self.sb(ls, f"hc_Y{i}", [128, 1024], BF16) for i in range(4)]
            for ft in range(NFT):
                mc, ms = mb[(ft % 2) * 2], mb[(ft % 2) * 2 + 1]
                kc_, ks_ = Kt[(ft % 2) * 2], Kt[(ft % 2) * 2 + 1]
                rows = slice(ft * 128, (ft + 1) * 128)
                k.dma("sp", lambda e: e.dma_start(out=mc[:], in_=cst["Mb"][0, ft]), writes=[mc])
                k.dma("sp", lambda e: e.dma_start(out=ms[:], in_=cst["Mb"][1, ft]), writes=[ms])
                k.dma("sp", lambda e: e.dma_start(out=kc_[:], in_=K_d[o, 0, rows, :]), writes=[kc_])
                k.dma("sp", lambda e: e.dma_start(out=ks_[:], in_=K_d[o, 1, rows, :]), writes=[ks_])
                pc, ps_ = self.psum(), self.psum()
                for (p, m) in ((pc, mc), (ps_, ms)):
                    for h in range(2):
                        for c in range(NLT):
                            k.op("pe", lambda e: e.matmul(p[:, h * 512:(h + 1) * 512], lhsT=m[:, c, :], rhs=src[:, c, h * 512:(h + 1) * 512],
                                                          start=(c == 0), stop=(c == NLT - 1)), reads=[m, src], writes=[p])
                yc, ys = Yo[(ft % 2) * 2], Yo[(ft % 2) * 2 + 1]
                t1, t2, t3, t4 = tt_
                k.op("dve", lambda e: e.tensor_tensor(out=t1[:], in0=pc[:], in1=kc_[:], op=ALU.mult), reads=[pc, kc_], writes=[t1])
                k.op("dve", lambda e: e.tensor_tensor(out=t2[:], in0=ps_[:], in1=ks_[:], op=ALU.mult), reads=[ps_, ks_], writes=[t2])
                k.op("pool", lambda e: e.tensor_tensor(out=yc[:], in0=t1[:], in1=t2[:], op=ALU.subtract), reads=[t1, t2], writes=[yc])
                k.op("dve", lambda e: e.tensor_tensor(out=t3[:], in0=pc[:], in1=ks_[:], op=ALU.mult), reads=[pc, ks_], writes=[t3])
                k.op("dve", lambda e: e.tensor_tensor(out=t4[:], in0=ps_[:], in1=kc_[:], op=ALU.mult), reads=[ps_, kc_], writes=[t4])
                k.op("pool", lambda e: e.tensor_tensor(out=ys[:], in0=t3[:], in1=t4[:], op=ALU.add), reads=[t3, t4], writes=[ys])
                k.dma("sp", lambda e: e.dma_start(out=Y_d[0, rows, :], in_=yc[:]), reads=[yc])
                k.dma("sp", lambda e: e.dma_start(out=Y_d[1, rows, :], in_=ys[:]), reads=[ys])
            self.k.barrier()
            srcv = src.t[:].rearrange("p c n -> p (c n)")
            Ych = srcv[:, 0:NFT * 512].rearrange("p (c n) -> p c n", n=512)
            Ysh = srcv[:, NFT * 512:2 * NFT * 512].rearrange("p (c n) -> p c n", n=512)
            gt = [self.sb(ls, f"hc_g{i}", [128, 512], BF16) for i in range(2)]
            zo = [self.sb(ls, f"hc_z{i}", [128, 512], BF16) for i in range(2)]
            for hh in range(2):
                cs = slice(hh * 512, (hh + 1) * 512)
                k.dma("sp", lambda e: e.dma_start(out=Ych, in_=Y_d[0].rearrange("(c p) n -> p c n", p=128)[:, :, cs]), writes=[src])
                k.dma("sp", lambda e: e.dma_start(out=Ysh, in_=Y_d[1].rearrange("(c p) n -> p c n", p=128)[:, :, cs]), writes=[src])
                for tt in range(NLT):
                    mc, ms = mb[(tt % 2) * 2], mb[(tt % 2) * 2 + 1]
                    rows = slice(tt * 128, (tt + 1) * 128)
                    g_ = gt[tt % 2]
                    z_ = zo[tt % 2]
                    k.dma("sp", lambda e: e.dma_start(out=mc[:], in_=cst["Mb"][0, tt]), writes=[mc])
                    k.dma("sp", lambda e: e.dma_start(out=ms[:], in_=cst["Mb"][1, tt]), writes=[ms])
                    k.dma("sp", lambda e: e.dma_start(out=g_[:], in_=g_d[o, rows, cs]), writes=[g_])
                    p = self.psum()
                    for (m, Yh, first) in ((mc, Ych, True), (ms, Ysh, False)):
                        for c in range(NFT):
                            k.op("pe", lambda e: e.matmul(p[:, 0:512], lhsT=m[:, c, :], rhs=Yh[:, c, :],
                                                          start=(first and c == 0), stop=((not first) and c == NFT - 1)),
                                 reads=[m, src], writes=[p])
                    k.op("dve", lambda e: e.scalar_tensor_tensor(out=z_[:], in0=p[:, 0:512], scalar=1.0 / L, in1=g_[:],
                                                                 op0=ALU.mult, op1=ALU.mult), reads=[p, g_], writes=[z_])
                    k.dma("sp", lambda e: e.dma_start(out=z_d[rows, cs], in_=z_[:]), reads=[z_])
                self.k.barrier()
            self.k.barrier()


def hy_consts(L, with_M=True):
    import ml_dtypes
    N = 2 * L
    f32 = np.float32
    t = np.linspace(0.0, 1.0, L, dtype=f32)[:, None]
    bands = np.linspace(1e-4, 15.0, 16, dtype=f32)[None, :]
    ang = f32(2.0 * math.pi / L) * np.arange(L, dtype=f32)[:, None] * bands
    z = np.concatenate([t, np.cos(ang), -np.sin(ang)], axis=-1).astype(f32)
    zT = np.ascontiguousarray(z.T)
    NT_ = L // 128
    fi = np.arange(L, dtype=np.float64)
    th = 2.0 * np.pi * (fi + 0.5) / N
    angm = th[:, None] * (fi[None, :] + 0.5)
    Mb = np.empty((2, NT_, 128, NT_, 128) if with_M else (1,), dtype=ml_dtypes.bfloat16)
    for i, fn in enumerate((np.cos, np.sin) if with_M else ()):
        M = fn(angm).astype(f32)
        Mb[i] = M.reshape(NT_, 128, NT_, 128).transpose(0, 3, 2, 1).astype(ml_dtypes.bfloat16)
    ph = np.stack([np.cos(th / 2), np.sin(th / 2)], -1).astype(f32)
    ph = np.ascontiguousarray(ph.reshape(NT_, 128, 2).transpose(1, 0, 2))
    tl_ = np.linspace(0.0, 1.0, L, dtype=f32).astype(np.float64)
    tl1 = np.concatenate([tl_[1:], [L / (L - 1.0)]])
    tl = np.stack([-tl_, -tl1], -1).astype(f32)
    tl = np.ascontiguousarray(tl.reshape(NT_, 128, 2).transpose(1, 0, 2))
    delta = np.abs(np.linspace(math.log(1e-2) / 1.5, math.log(1e-2) / 0.3, D, dtype=f32)).reshape(1, D).astype(f32)
    return dict(zT=zT, Mb=Mb, ph=ph, tl=tl, delta=delta)


DEPTH = 4
GRID_W = 64


def rope_tables_np(T):
    f32 = np.float32
    rows = T // GRID_W
    row = np.repeat(np.arange(rows, dtype=f32), GRID_W)
    col = np.tile(np.arange(GRID_W, dtype=f32), rows)
    inv = (f32(10000.0) ** (-np.arange(0, 64, 2, dtype=f32) / f32(64))).astype(f32)

    def axis_angles(pos):
        a = pos[:, None] * inv[None, :]
        return np.concatenate([a, a], axis=-1)

    ang = np.concatenate([axis_angles(row), axis_angles(col)], axis=-1).astype(f32)
    return np.ascontiguousarray(np.stack([np.cos(ang).T, np.sin(ang).T]).astype(f32))


def rot_T_np():
    R = np.zeros((128, 128), np.float32)
    for base in (0, 64):
        for i in range(32):
            R[base + i, base + i + 32] = -1.0
            R[base + 32 + i, base + i] = 1.0
    return np.ascontiguousarray(R.T)


def build_model(T, LC, depth=DEPTH):
    nc = bass.Bass("TRN2", target_bir_lowering=False)

    def dt(n, s, d=F32, kind="ExternalInput"):
        return nc.dram_tensor(n, s, d, kind=kind).ap()

    nA = len(range(0, depth, 3))
    nB = len(range(1, depth, 3))
    nC = len(range(2, depth, 3))
    last_attn = max(list(range(2, depth, 3)), default=-1)
    I = {}
    I["x"] = dt("x", [T, D]); I["ctx"] = dt("ctx", [LC, D]); I["c_pm"] = dt("c_pm", [128, 8]); I["cc_pm"] = dt("cc_pm", [128, 8])
    I["ada_w"] = dt("ada_w", [depth, D, 6 * D]); I["ada_b"] = dt("ada_b", [depth, 6 * D]); I["norm_g"] = dt("norm_g", [depth, 2, D])
    I["sc_w_in"] = dt("sc_w_in", [nA, D, 3 * D]); I["sc_cw_pm"] = dt("sc_cw_pm", [nA, 128, 8, 3]); I["sc_w_out"] = dt("sc_w_out", [nA, D, D])
    if nB:
        I["hy_w_in"] = dt("hy_w_in", [nB, D, 3 * D]); I["hy_cw_pm"] = dt("hy_cw_pm", [nB, 128, 24, 3])
        I["hy_f_w1"] = dt("hy_f_w1", [nB, 33, 64]); I["hy_f_b1"] = dt("hy_f_b1", [nB, 64, 1]); I["hy_f_w2"] = dt("hy_f_w2", [nB, 64, 64])
        I["hy_f_b2"] = dt("hy_f_b2", [nB, 64, 1]); I["hy_f_w3"] = dt("hy_f_w3", [nB, 64, 4 * D]); I["hy_sin_freq"] = dt("hy_sin_freq", [nB, 64, 1])
        I["hy_skip"] = dt("hy_skip", [nB, 2, D]); I["hy_w_out"] = dt("hy_w_out", [nB, D, D])
        cst = {}
        use_fft = (T == 4096)
        for tag, L in (("l", T), ("c", LC)):
            n_ = L // 128
            cst[tag] = dict(zT=dt(f"hz_{tag}", [33, L]), ph=dt(f"hp_{tag}", [128, n_, 2]),
                            tl=dt(f"ht_{tag}", [128, n_, 2]), delta=dt(f"hd_{tag}", [1, D]))
            if not (use_fft and tag == "l"):
                cst[tag]["Mb"] = dt(f"hM_{tag}", [2, n_, 128, n_, 128], BF16)
        if use_fft:
            fc = dict(W1=dt("fW1", [64, 128], BF16), W4=dt("fW4", [128, 64], BF16), Td=dt("fTd", [128, 3, 64, 128], BF16),
                      Tf=dt("fTf", [128, 4, 64, 128], BF16))
    if nC:
        I["at_w_qkv"] = dt("at_w_qkv", [nC, D, 1536]); I["at_qg"] = dt("at_qg", [nC, 128, 1]); I["at_kg"] = dt("at_kg", [nC, 128, 1])
        I["at_w_o"] = dt("at_w_o", [nC, D, D]); I["rope"] = dt("rope", [2, 128, T]); I["rotT"] = dt("rotT", [128, 128])
    I["moe_router"] = dt("moe_router", [depth, D, NE]); I["moe_w_gate"] = dt("moe_w_gate", [depth, NE, D, D])
    I["moe_w_up"] = dt("moe_w_up", [depth, NE, D, D]); I["moe_w_down"] = dt("moe_w_down", [depth, NE, D, D])
    out = dt("out", [T, D], kind="ExternalOutput")
    cxs = dt("s_cxs", [LC, D], F32, "Internal")
    zT_d = dt("s_zT", [D, T], BF16, "Internal")
    h2_d = dt("s_h2", [T, D], BF16, "Internal")
    ztok_d = dt("s_ztok", [T, D], BF16, "Internal")
    rows_d = dt("s_adarows", [depth, 2, 6 * D], F32, "Internal")
    h2c_d = dt("s_h2c", [LC, D], BF16, "Internal")
    if nB:
        hs = dict(kp=dt("s_kp", [2, T, D], BF16, "Internal"), km=dt("s_km", [2, T, D], BF16, "Internal"), A=dt("s_A", [T, D], F32, "Internal"))
        Kl = dt("s_Kl", [2, 2, T, D], BF16, "Internal"); Kc_ = dt("s_Kc", [2, 2, LC, D], BF16, "Internal")
        g_d = dt("s_g", [2, T, D], BF16, "Internal"); Y_d = dt("s_Y", [2, T, D], BF16, "Internal")
        z1_d = dt("s_z1", [T, D], BF16, "Internal"); z2_d = dt("s_z2", [T, D], BF16, "Internal")
        if use_fft:
            Gp = dt("s_Gp", [2, 64, 64, D], BF16, "Internal"); Gm = dt("s_Gm", [2, 64, 64, D], BF16, "Internal")
            Zd = dt("s_Zd", [2, 64, 64, D], BF16, "Internal")
            KA = dt("s_KA", [2, 64, 128, D], BF16, "Internal"); KB = dt("s_KB", [2, 64, 128, D], BF16, "Internal")
            zt_d = dt("s_zt", [T, D], BF16, "Internal")

    P = HyFFT(nc, T, LC)
    with ExitStack() as st:
        P.setup_small(st)
        rep = {"l": 0, "c": 1}
        P.ada_precompute(I["c_pm"], I["cc_pm"], I["ada_w"], I["ada_b"], depth, rows_d)
        for i in range(depth):
            kind, j = i % 3, i // 3
            need_ctx, upd_ctx = i <= last_attn, i < last_attn
            streams = []
            if upd_ctx:
                streams.append(("c", LC, I["ctx"] if i == 0 else cxs, cxs))
            streams.append(("l", T, I["x"] if i == 0 else out, out))
            aw, ab, ng = I["ada_w"][i], I["ada_b"][i:i + 1, :], I["norm_g"][i]
            if kind == 1:
                wts = dict(w1=I["hy_f_w1"][j], b1=I["hy_f_b1"][j], w2=I["hy_f_w2"][j], b2=I["hy_f_b2"][j], w3=I["hy_f_w3"][j],
                           freq=I["hy_sin_freq"][j], skip=I["hy_skip"][j])
                Ks = {}
                for (tag, L, _, _) in streams:
                    scr = dict(kp=hs["kp"][:, 0:L, :], km=hs["km"][:, 0:L, :], A=hs["A"][0:L, :], K=(Kl if tag == "l" else Kc_))
                    if use_fft and tag == "l":
                        P.hy_filter(L, cst[tag], wts, scr, fft=dict(W1=fc["W1"], Tf=fc["Tf"], Gp=Gp, Gm=Gm, KA=KA, KB=KB))
                    else:
                        P.hy_filter(L, cst[tag], wts, scr)
                    Ks[tag] = scr["K"]
            lay = ExitStack()
            pers = {}
            for (tag, L, _, _) in streams:
                cap_ = 2 * L // NE
                njt_ = cap_ // min(cap_, 128)
                pers[tag] = dict(idx=P.sb(lay, f"idx{tag}", [128, NE * njt_], I32), gsel=P.sb(lay, f"gs{tag}", [128, NE * njt_], F32))
                if tag != streams[-1][0]:
                    pers[tag]["G2"] = P.sb(lay, f"G2{tag}", [128, 1024], F32)
            for (tag, L, src_ap, dst_ap) in streams:
                with ExitStack() as s1:
                    A1, B1, G1, A2, B2, G2 = P.ada_load(s1, rows_d[i, rep[tag]:rep[tag] + 1, :], ng, tag)
                    if kind == 0:
                        with ExitStack() as s2:
                            hT = P.phase_a(s2, src_ap, L, A1, B1, tag)
                            P.phase_b_conv(hT, L, I["sc_w_in"][j], I["sc_cw_pm"][j], zT_d[:, 0:L], tag)
                            P.end_phase(s2)
                        zsrc, wo, ztok = zT_d[:, 0:L], I["sc_w_out"][j], False
                    elif kind == 1 and use_fft and tag == "l":
                        n_ = L // 128
                        with ExitStack() as s2:
                            src = P.sb(s2, "hy_src", [128, n_, 1024], BF16)
                            with ExitStack() as s2b:
                                hT = P.phase_a(s2b, src_ap, L, A1, B1, tag)
                                P.hy_proj(hT, L, I["hy_w_in"][j], I["hy_cw_pm"][j], src, g_d[:, 0:L, :], z_d=zt_d)
                                P.end_phase(s2b)
                            P.end_phase(s2)
                        with ExitStack() as s2:
                            W1 = P.sb(s2, "fW1", [64, 128], BF16); W4 = P.sb(s2, "fW4", [128, 64], BF16)
                            Td = P.sb(s2, "fTd", [128, 3, 64, 128], BF16)
                            P.k.dma("sp", lambda e: e.dma_start(out=W1[:], in_=fc["W1"]), writes=[W1])
                            P.k.dma("sp", lambda e: e.dma_start(out=W4[:], in_=fc["W4"]), writes=[W4])
                            P.k.dma("sp", lambda e: e.dma_start(out=Td[:], in_=fc["Td"]), writes=[Td])
                            P.fft_conv(zt_d, Gp, Zd, W1, W4, Td, KA[0], KB[0], g_d[0], z1_d)
                            P.fft_conv(z1_d, Gp, Zd, W1, W4, Td, KA[1], KB[1], g_d[1], z2_d)
                            P.end_phase(s2)
                        zsrc, wo, ztok = z2_d[0:L, :], I["hy_w_out"][j], True
                    elif kind == 1:
                        n_ = L // 128
                        with ExitStack() as s2:
                            src = P.sb(s2, "hy_src", [128, n_, 1024], BF16)
                            with ExitStack() as s2b:
                                hT = P.phase_a(s2b, src_ap, L, A1, B1, tag)
                                P.hy_proj(hT, L, I["hy_w_in"][j], I["hy_cw_pm"][j], src, g_d[:, 0:L, :])
                                P.end_phase(s2b)
                            P.hy_conv(L, src, cst[tag], Ks[tag], g_d[:, 0:L, :], Y_d[:, 0:L, :], z1_d[0:L, :], 0)
                            P.k.dma("sp", lambda e: e.dma_start(out=src[:], in_=z1_d[0:L, :].rearrange("(c p) n -> p c n", p=128)), writes=[src])
                            P.hy_conv(L, src, cst[tag], Ks[tag], g_d[:, 0:L, :], Y_d[:, 0:L, :], z2_d[0:L, :], 1)
                            P.end_phase(s2)
                        zsrc, wo, ztok = z2_d[0:L, :], I["hy_w_out"][j], True
                    else:
                        with ExitStack() as s2:
                            hcT = P.sb(s2, "hcT", [128, 8, LC], BF16)
                            with ExitStack() as sc:
                                cA1, cB1, *_ = P.ada_load(sc, rows_d[i, 1:2, :], ng, "c")
                                P.phase_a(sc, I["ctx"] if i == 0 else cxs, LC, cA1, cB1, "c", hT=hcT)
                                P.end_phase(sc)
                            hT = P.phase_a(s2, src_ap, L, A1, B1, tag)
                            P.phase_b_attn(hT, hcT, L, LC, I["at_w_qkv"][j], I["at_qg"][j], I["at_kg"][j], I["rope"], I["rotT"], ztok_d)
                            P.end_phase(s2)
                        zsrc, wo, ztok = ztok_d, I["at_w_o"][j], True
                    h2x = (h2_d if tag == "l" else h2c_d)[0:L, :]
                    with ExitStack() as s3:
                        affT = P.phase_c(L, zsrc, wo, src_ap, dst_ap, G1, A2, B2, I["moe_router"][i], h2x, s3, tag, z_tok=ztok)
                        P.moe_route(L, affT, pers[tag]["idx"], pers[tag]["gsel"])
                        P.end_phase(s3)
                    pers[tag].update(T=L, h2_d=h2x, lat_d=dst_ap)
                    if tag != streams[-1][0]:
                        P.k.op("dve", lambda e: e.tensor_copy(out=pers[tag]["G2"][:], in_=G2[:]), reads=[G2], writes=[pers[tag]["G2"]])
                    else:
                        pers[tag]["G2"] = G2
                        P.moe_experts([pers[t_] for (t_, _, _, _) in streams], I["moe_w_gate"][i], I["moe_w_up"][i], I["moe_w_down"][i])
                    P.end_phase(s1)
            P.end_phase(lay)
    P.finish()
    return nc


def host_inputs(inp, b, T, LC, depth=DEPTH):
    f = lambda a: np.ascontiguousarray(np.asarray(a, dtype=np.float32))
    pm = lambda v: np.ascontiguousarray(np.asarray(v, np.float32).reshape(8, 128).T)
    m = {}
    m["x"] = f(inp["x"][b]); m["ctx"] = f(inp["ctx"][b]); m["c_pm"] = pm(inp["c"][b]); m["cc_pm"] = pm(inp["c_ctx"])
    for n in ("ada_w", "ada_b", "norm_g", "sc_w_in", "sc_w_out", "moe_router", "moe_w_gate", "moe_w_up", "moe_w_down"):
        m[n] = f(inp[n])
    sc = np.asarray(inp["sc_conv"], np.float32)
    m["sc_cw_pm"] = np.ascontiguousarray(sc.reshape(sc.shape[0], 3, 8, 128).transpose(0, 3, 2, 1))
    if depth > 1:
        for n in ("hy_w_in", "hy_f_w1", "hy_f_w2", "hy_f_w3", "hy_skip", "hy_w_out"):
            m[n] = f(inp[n])
        hc_ = np.asarray(inp["hy_conv"], np.float32)
        m["hy_cw_pm"] = np.ascontiguousarray(hc_.reshape(hc_.shape[0], 3, 24, 128).transpose(0, 3, 2, 1))
        for n in ("hy_f_b1", "hy_f_b2", "hy_sin_freq"):
            a = np.asarray(inp[n], np.float32)
            m[n] = np.ascontiguousarray(a.reshape(a.shape[0], 64, 1))
    if depth > 2:
        m["at_w_qkv"] = f(inp["at_w_qkv"]); m["at_w_o"] = f(inp["at_w_o"])
        for n, s in (("at_qg", "at_q_g"), ("at_kg", "at_k_g")):
            a = np.asarray(inp[s], np.float32)
            m[n] = np.ascontiguousarray(a.reshape(a.shape[0], 128, 1))
    return m


_CONST_CACHE = {}


def const_inputs(T, LC, depth=DEPTH):
    key = (T, LC, depth)
    if key not in _CONST_CACHE:
        m = {}
        if depth > 1:
            for tag, L in (("l", T), ("c", LC)):
                fft_l = (T == 4096 and tag == "l")
                hc = hy_consts(L, with_M=not fft_l)
                m[f"hz_{tag}"] = hc["zT"]; m[f"hp_{tag}"] = hc["ph"]; m[f"ht_{tag}"] = hc["tl"]; m[f"hd_{tag}"] = hc["delta"]
                if not fft_l:
                    m[f"hM_{tag}"] = hc["Mb"]
            if T == 4096:
                m.update(hy_fft_consts())
        if depth > 2:
            m["rope"] = rope_tables_np(T); m["rotT"] = rot_T_np()
        _CONST_CACHE[key] = m
    return _CONST_CACHE[key]


def kernel(**inputs):
    B, T, _ = inputs["x"].shape
    LC = inputs["ctx"].shape[1]
    depth = inputs["ada_w"].shape[0]
    nc = build_model(T, LC, depth)
    cm = const_inputs(T, LC, depth)
    in_maps = []
    for b in range(B):
        m = host_inputs(inputs, b, T, LC, depth)
        m.update(cm)
        in_maps.append(m)
    res = run_bass_kernel_spmd(nc, in_maps, core_ids=list(range(B)))
    return np.stack([np.asarray(r["out"], dtype=np.float32) for r in res.results], axis=0)


def hy_fft_consts():
    import ml_dtypes
    L = 4096
    N = 2 * L
    a = np.arange(64)[:, None]
    fa = np.arange(64)[None, :]
    al = 2 * np.pi * (fa + 0.5) * a / 128.0
    W1 = np.concatenate([np.cos(al), -np.sin(al)], 1)
    W4 = np.concatenate([np.cos(al).T, -np.sin(al).T], 0) * (2.0 / N)
    b = np.arange(64)[:, None]
    fb = np.arange(32)[None, :]
    names = ("T2", "T2s", "T3", "MA1", "MA2", "MB1", "MB2")
    Ms = {n: np.zeros((64, 128, 128)) for n in names}
    for f_a in range(64):
        fD = f_a + 128 * fb
        fM = 127 - f_a + 128 * fb
        pD = 2 * np.pi * (fD + 0.5) * (b + 0.5) / N
        pM = 2 * np.pi * (fM + 0.5) * (b + 0.5) / N
        hD = np.pi * (fD + 0.5) / N
        hM = np.pi * (fM + 0.5) / N
        cD, sD, cM, sM = np.cos(pD), np.sin(pD), np.cos(pM), np.sin(pM)

        def put(name, blk_re, blk_im, ro, mi):
            c0 = ro * 64 + mi * 32
            Ms[name][f_a, 0:64, c0:c0 + 32] += blk_re
            Ms[name][f_a, 64:128, c0:c0 + 32] += blk_im

        ReD, ImD, ReM, ImM = (cD, sD), (-sD, cD), (cM, -sM), (-sM, -cM)
        neg = lambda t: (-t[0], -t[1])
        put("T2", *ReD, 0, 0); put("T2", *ImD, 1, 0); put("T2", *ReM, 0, 1); put("T2", *ImM, 1, 1)
        put("T2s", *neg(ImD), 0, 0); put("T2s", *ReD, 1, 0); put("T2s", *neg(ImM), 0, 1); put("T2s", *ReM, 1, 1)
        for ro in (0, 1):
            for (Re_, Im_, h_, mi) in ((ReD, ImD, hD, 0), (ReM, ImM, hM, 1)):
                ch, sh = np.cos(h_), np.sin(h_)
                put("MA1", Re_[0] * ch, Re_[1] * ch, ro, mi); put("MA2", -Im_[0] * sh, -Im_[1] * sh, ro, mi)
                put("MB1", Re_[0] * sh, Re_[1] * sh, ro, mi); put("MB2", Im_[0] * ch, Im_[1] * ch, ro, mi)

        def put3(ri, mi, blk, ro):
            r0 = ri * 64 + mi * 32
            Ms["T3"][f_a, r0:r0 + 32, ro * 64:ro * 64 + 64] += blk.T

        put3(0, 0, cD, 0); put3(1, 0, -sD, 0); put3(0, 1, cM, 0); put3(1, 1, -sM, 0)
        put3(0, 0, sD, 1); put3(1, 0, cD, 1); put3(0, 1, -sM, 1); put3(1, 1, -cM, 1)
    bf = lambda x: np.ascontiguousarray(x.astype(np.float32).astype(ml_dtypes.bfloat16))
    out = {"fW1": bf(W1), "fW4": bf(W4)}
    out["fTd"] = bf(np.stack([Ms["T2"], Ms["T2s"], Ms["T3"]], 0).transpose(2, 0, 1, 3))
    out["fTf"] = bf(np.stack([Ms["MA1"], Ms["MA2"], Ms["MB1"], Ms["MB2"]], 0).transpose(2, 0, 1, 3))
    return out


class HyFFT(Hyena):
    def fft_stage1(self, src_d, Gd, W1, ncols=1024):
        k = self.k
        BBS = 8
        xv = src_d.rearrange("(a b) c -> a b c", b=64)
        gv = Gd.rearrange("r f b c -> (r f) b c")
        with ExitStack() as ls:
            xt = [self.sb(ls, f"f1_x{i}", [64, BBS, ncols], BF16) for i in range(2)]
            gt = [self.sb(ls, f"f1_g{i}", [128, ncols], BF16) for i in range(6)]
            n = 0
            for bb in range(64 // BBS):
                x_ = xt[bb % 2]
                k.dma("sp", lambda e: e.dma_start(out=x_[:], in_=xv[:, bb * BBS:(bb + 1) * BBS, :]), writes=[x_])
                for bi in range(BBS):
                    b = bb * BBS + bi
                    p = self.psum()
                    for h in range(ncols // 512):
                        k.op("pe", lambda e: e.matmul(p[:, h * 512:(h + 1) * 512], lhsT=W1[:], rhs=x_[:, bi, h * 512:(h + 1) * 512],
                                                      start=True, stop=True), reads=[W1, x_], writes=[p])
                    g_ = gt[n % 6]
                    n += 1
                    k.op("act", lambda e: e.copy(out=g_[:], in_=p[:, 0:ncols]), reads=[p], writes=[g_])
                    k.dma("pool", lambda e: e.dma_start(out=gv[:, b, :], in_=g_[:]), reads=[g_])
            self.k.barrier()

    def _load_gblock(self, dst, Gd, fa0, nfa):
        for r in range(2):
            self.k.dma("sp", lambda e: e.dma_start(out=dst[r * 64:(r + 1) * 64, 0:nfa, :],
                                                   in_=Gd[r, fa0:fa0 + nfa, :, :].rearrange("f b c -> b f c")), writes=[dst])

    def fft_filter_stage2(self, Gp, Gm, Tf, rS, sk, KA_d, KB_d):
        k = self.k
        FBS = 4
        with ExitStack() as ls:
            gp = [self.sb(ls, f"f2_gp{i}", [128, FBS, 1024], BF16) for i in range(2)]
            gm = [self.sb(ls, f"f2_gm{i}", [128, FBS, 1024], BF16) for i in range(2)]
            tmp = [self.sb(ls, f"f2_t{i}", [128, 1024], F32) for i in range(2)]
            ko = [self.sb(ls, f"f2_k{i}", [128, 1024], BF16) for i in range(4)]
            for fb_ in range(64 // FBS):
                gp_, gm_ = gp[fb_ % 2], gm[fb_ % 2]
                self._load_gblock(gp_, Gp, fb_ * FBS, FBS)
                self._load_gblock(gm_, Gm, fb_ * FBS, FBS)
                for fi in range(FBS):
                    fa = fb_ * FBS + fi
                    pA, pB = self.psum(), self.psum()
                    for (p, m1, m2) in ((pA, 0, 1), (pB, 2, 3)):
                        for h in range(2):
                            hs = slice(h * 512, (h + 1) * 512)
                            k.op("pe", lambda e: e.matmul(p[:, hs], lhsT=Tf[:, m1, fa, :], rhs=gp_[:, fi, hs], start=True, stop=False),
                                 reads=[Tf, gp_], writes=[p])
                            k.op("pe", lambda e: e.matmul(p[:, hs], lhsT=Tf[:, m2, fa, :], rhs=gm_[:, fi, hs], start=False, stop=True),
                                 reads=[Tf, gm_], writes=[p])
                    t_ = tmp[fa % 2]
                    ka, kb = ko[(fa % 2) * 2], ko[(fa % 2) * 2 + 1]
                    k.op("dve", lambda e: e.tensor_tensor(out=t_[:], in0=pA[:], in1=rS[:], op=ALU.mult), reads=[pA, rS], writes=[t_])
                    k.op("pool", lambda e: e.tensor_tensor(out=ka[:], in0=t_[:], in1=sk[:], op=ALU.add), reads=[t_, sk], writes=[ka])
                    k.op("dve", lambda e: e.tensor_tensor(out=kb[:], in0=pB[:], in1=rS[:], op=ALU.mult), reads=[pB, rS], writes=[kb])
                    k.dma("pool", lambda e: e.dma_start(out=KA_d[fa], in_=ka[:]), reads=[ka])
                    k.dma("pool", lambda e: e.dma_start(out=KB_d[fa], in_=kb[:]), reads=[kb])
            self.k.barrier()

    def fft_conv(self, src_d, Gd, Zd, W1, W4, Td, KA_d, KB_d, gate_d, z_d):
        k = self.k
        self.fft_stage1(src_d, Gd, W1)
        FBS = 4
        with ExitStack() as ls:
            gb = [self.sb(ls, f"f3_g{i}", [128, FBS, 1024], BF16) for i in range(2)]
            kk = [self.sb(ls, f"f3_k{i}", [128, 1024], BF16) for i in range(4)]
            t1 = [self.sb(ls, f"f3_t1{i}", [128, 1024], F32) for i in range(2)]
            t2 = [self.sb(ls, f"f3_t2{i}", [128, 1024], F32) for i in range(2)]
            Y = [self.sb(ls, f"f3_Y{i}", [128, 1024], BF16) for i in range(2)]
            zt = [self.sb(ls, f"f3_z{i}", [128, 1024], BF16) for i in range(2)]
            pUs = {}

            def stX(fa):
                fb_, fi = fa // FBS, fa % FBS
                g_ = gb[fb_ % 2]
                if fi == 0:
                    self._load_gblock(g_, Gd, fb_ * FBS, FBS)
                ka, kb = kk[(fa % 2) * 2], kk[(fa % 2) * 2 + 1]
                k.dma("sp", lambda e: e.dma_start(out=ka[:], in_=KA_d[fa]), writes=[ka])
                k.dma("sp", lambda e: e.dma_start(out=kb[:], in_=KB_d[fa]), writes=[kb])
                pU, pS = self.psum(), self.psum()
                for (p, m) in ((pU, 0), (pS, 1)):
                    for h in range(2):
                        hs = slice(h * 512, (h + 1) * 512)
                        k.op("pe", lambda e: e.matmul(p[:, hs], lhsT=Td[:, m, fa, :], rhs=g_[:, fi, hs], start=True, stop=True),
                             reads=[Td, g_], writes=[p])
                a_, b_, y_ = t1[fa % 2], t2[fa % 2], Y[fa % 2]
                k.op("dve", lambda e: e.tensor_tensor(out=a_[:], in0=pU[:], in1=ka[:], op=ALU.mult), reads=[pU, ka], writes=[a_])
                k.op("dve", lambda e: e.tensor_tensor(out=b_[:], in0=pS[:], in1=kb[:], op=ALU.mult), reads=[pS, kb], writes=[b_])
                k.op("pool", lambda e: e.tensor_tensor(out=y_[:], in0=a_[:], in1=b_[:], op=ALU.add), reads=[a_, b_], writes=[y_])
                pUs[fa] = pU

            def stZ(fa):
                y_, z_ = Y[fa % 2], zt[fa % 2]
                pZ = pUs.pop(fa)
                for h in range(2):
                    hs = slice(h * 512, (h + 1) * 512)
                    k.op("pe", lambda e: e.matmul(pZ[:, hs], lhsT=Td[:, 2, fa, :], rhs=y_[:, hs], start=True, stop=True),
                         reads=[Td, y_], writes=[pZ])
                k.op("act", lambda e: e.copy(out=z_[:], in_=pZ[:]), reads=[pZ], writes=[z_])
                for r in range(2):
                    k.dma("pool", lambda e: e.dma_start(out=Zd[r, fa, :, :], in_=z_[r * 64:(r + 1) * 64, :]), reads=[z_])

            stX(0)
            for fa in range(64):
                if fa + 1 < 64:
                    stX(fa + 1)
                stZ(fa)
            self.k.barrier()
        BBS = 8
        with ExitStack() as ls:
            zz = [self.sb(ls, f"f4_z{i}", [128, BBS, 1024], BF16) for i in range(2)]
            gg = [self.sb(ls, f"f4_g{i}", [64, BBS, 1024], BF16) for i in range(2)]
            oo = [self.sb(ls, f"f4_o{i}", [64, BBS, 1024], BF16) for i in range(2)]
            zv = Zd.rearrange("r f b c -> (r f) b c")
            gv = gate_d.rearrange("(a b) c -> a b c", b=64)
            ov = z_d.rearrange("(a b) c -> a b c", b=64)
            for bb in range(64 // BBS):
                z_, g_, o_ = zz[bb % 2], gg[bb % 2], oo[bb % 2]
                bs = slice(bb * BBS, (bb + 1) * BBS)
                k.dma("sp", lambda e: e.dma_start(out=z_[:], in_=zv[:, bs, :]), writes=[z_])
                k.dma("sp", lambda e: e.dma_start(out=g_[:], in_=gv[:, bs, :]), writes=[g_])
                for bi in range(BBS):
                    p = self.psum()
                    for h in range(2):
                        hs = slice(h * 512, (h + 1) * 512)
                        k.op("pe", lambda e: e.matmul(p[0:64, hs], lhsT=W4[:], rhs=z_[:, bi, hs], start=True, stop=True),
                             reads=[W4, z_], writes=[p])
                    k.op("dve", lambda e: e.tensor_tensor(out=o_[:, bi, :], in0=p[0:64, :], in1=g_[:, bi, :], op=ALU.mult),
                         reads=[p, g_], writes=[o_])
                k.dma("pool", lambda e: e.dma_start(out=ov[:, bs, :], in_=o_[:]), reads=[o_])
            self.k.barrier()
```

```python
import math
from contextlib import ExitStack

import numpy as np
import concourse.bass as bass
import concourse.mybir as mybir
from concourse.bass_utils import run_bass_kernel_spmd

F32 = mybir.dt.float32
BF16 = mybir.dt.bfloat16
I32 = mybir.dt.int32
U32 = mybir.dt.uint32
ALU = mybir.AluOpType
AF = mybir.ActivationFunctionType
AX = mybir.AxisListType


class Buf:
    def __init__(self, t=None, name=""):
        self.t = t
        self.name = name
        self.w = None
        self.r = {}

    def __getitem__(self, k):
        return self.t[k]


class K:
    EPOCH = 28000
    NDMA = 12

    def __init__(self, nc):
        self.nc = nc
        self.stack = ExitStack()
        self.eng = dict(pe=nc.tensor, act=nc.scalar, dve=nc.vector, pool=nc.gpsimd, sp=nc.sync)
        self.sems = {}
        self.nsem = 0
        self.cur = {}
        self.waited = {e: {} for e in self.eng}
        self.own = {e: set() for e in self.eng}
        self.last = {}
        for e in ("pe", "act", "dve", "pool"):
            self.cur[e] = [self._new_sem(e), 0]
            self.own[e].add(self.cur[e][0])
        self.dq = {}
        self.dqi = {}
        for q in ("sp", "pool", "act"):
            self.dq[q] = [[self._new_sem("d" + q), 0] for _ in range(self.NDMA)]
            self.dqi[q] = 0
        self.same_engine_sync = True

    def _new_sem(self, name):
        key = self.nsem
        self.nsem += 1
        self.sems[key] = self.stack.enter_context(self.nc.semaphore(f"s{key}_{name}"))
        return key

    def _wait(self, e, evs):
        need = {}
        for ev in evs:
            if ev is None:
                continue
            k, v = ev
            if v > need.get(k, 0):
                need[k] = v
        for k, v in need.items():
            if e == "pe" and k in self.own["pe"]:
                continue
            if (not self.same_engine_sync) and k in self.own[e]:
                continue
            if self.waited[e].get(k, 0) >= v:
                continue
            self.eng[e].wait_ge(self.sems[k], v)
            self.waited[e][k] = v

    def _deps(self, reads, writes):
        evs = []
        for b in reads:
            evs.append(b.w)
        for b in writes:
            evs.append(b.w)
            evs.extend(b.r.items())
        return evs

    def _mark(self, ev, reads, writes):
        k, v = ev
        for b in reads:
            if v > b.r.get(k, 0):
                b.r[k] = v
        for b in writes:
            b.w = ev
            b.r = {}

    def op(self, e, fn, reads=(), writes=()):
        self._wait(e, self._deps(reads, writes))
        ins = fn(self.eng[e])
        c = self.cur[e]
        c[1] += 1
        ins.then_inc(self.sems[c[0]], 1)
        ev = (c[0], c[1])
        self.last[e] = ev
        if c[1] >= self.EPOCH:
            self.cur[e] = [self._new_sem(e), 0]
            self.own[e].add(self.cur[e][0])
        self._mark(ev, reads, writes)
        return ev

    def dma(self, q, fn, reads=(), writes=()):
        ring = self.dq[q]
        slot = ring[self.dqi[q]]
        self.dqi[q] = (self.dqi[q] + 1) % len(ring)
        evs = self._deps(reads, writes)
        if slot[1] > 0:
            evs.append((slot[0], slot[1]))
        self._wait(q, evs)
        if slot[1] >= self.EPOCH:
            slot[0] = self._new_sem("d" + q)
            slot[1] = 0
        ins = fn(self.eng[q])
        slot[1] += 16
        ins.then_inc(self.sems[slot[0]], 16)
        ev = (slot[0], slot[1])
        self._mark(ev, reads, writes)
        return ev

    def all_events(self):
        evs = []
        for e, ev in self.last.items():
            evs.append(ev)
        for q, ring in self.dq.items():
            for s in ring:
                if s[1] > 0:
                    evs.append((s[0], s[1]))
        return evs

    def barrier(self, engines=("pe", "act", "dve", "pool", "sp")):
        evs = self.all_events()
        for e in engines:
            self._wait(e, evs)

    def close(self):
        self.stack.close()


D = 1024
NCH = 8
NE = 16
HD = 128
NQH = 8
NKVH = 2
EPS = 1e-6


class Prog:
    def __init__(self, nc, T, LC):
        self.nc = nc
        self.k = K(nc)
        self.T = T
        self.LC = LC
        self.gs = ExitStack()
        k = self.k
        self.ident = self.sb(self.gs, "ident", [128, 128], F32)
        self.identb = self.sb(self.gs, "identb", [128, 128], BF16)
        self.iota_row = self.sb(self.gs, "iota_row", [128, 512], F32)
        self.pidx = self.sb(self.gs, "pidx", [128, 1], F32)
        self.iota_h = self.sb(self.gs, "iota_h", [128, 512], mybir.dt.float16)
        self.ones_f = self.sb(self.gs, "ones_f", [128, 128], F32)
        self.ones_b = self.sb(self.gs, "ones_b", [128, 128], BF16)
        k.op("pool", lambda e: e.iota(self.iota_row[:], pattern=[[1, 512]], base=0, channel_multiplier=0,
                                      allow_small_or_imprecise_dtypes=True), writes=[self.iota_row])
        k.op("pool", lambda e: e.iota(self.pidx[:], pattern=[[0, 1]], base=0, channel_multiplier=1,
                                      allow_small_or_imprecise_dtypes=True), writes=[self.pidx])
        k.op("dve", lambda e: e.tensor_scalar(out=self.ident[:], in0=self.iota_row[:, 0:128], scalar1=self.pidx[:, 0:1],
                                              scalar2=None, op0=ALU.is_equal),
             reads=[self.iota_row, self.pidx], writes=[self.ident])
        k.op("dve", lambda e: e.tensor_copy(out=self.identb[:], in_=self.ident[:]), reads=[self.ident], writes=[self.identb])
        k.op("dve", lambda e: e.tensor_copy(out=self.iota_h[:], in_=self.iota_row[:]), reads=[self.iota_row], writes=[self.iota_h])
        k.op("dve", lambda e: e.memset(self.ones_f[:], 1.0), writes=[self.ones_f])
        k.op("dve", lambda e: e.memset(self.ones_b[:], 1.0), writes=[self.ones_b])
        self.ps = [Buf(self.gs.enter_context(nc.psum_tensor(f"ps{i}", [128, 1024], F32)), f"ps{i}") for i in range(4)]
        self.psi = 0

    def sb(self, st, name, shape, dt):
        self._uid = getattr(self, "_uid", 0) + 1
        name = f"{name}_{self._uid}"
        return Buf(st.enter_context(self.nc.sbuf_tensor(name, shape, dt)), name)

    def psum(self):
        p = self.ps[self.psi]
        self.psi = (self.psi + 1) % len(self.ps)
        return p

    def end_phase(self, st):
        self.k.barrier()
        st.close()

    def finish(self):
        self.k.barrier()
        self.gs.close()
        self.k.close()


def pbf(p):
    return p.t[:].bitcast(BF16)


def _pm_view(ap2d):
    return ap2d.rearrange("(c p) n -> p c n", p=128)


class Layers(Prog):
    def setup_small(self, st):
        k = self.k
        self.eps_t = self.sb(st, "eps_t", [128, 1], F32)
        k.op("dve", lambda e: e.memset(self.eps_t[:], EPS), writes=[self.eps_t])
        self.mhalf = self.sb(st, "mhalf", [128, 1], F32)
        k.op("dve", lambda e: e.memset(self.mhalf[:], -0.5), writes=[self.mhalf])
        self.sm = [[self.sb(st, f"sm{i}_{j}", [128, 1], F32) for j in range(2)] for i in range(4)]
        self.smi = 0

    def small(self):
        s = self.sm[self.smi]
        self.smi = (self.smi + 1) % len(self.sm)
        return s

    def make_sil_rep(self, st, name, vec_pm_ap):
        k = self.k
        v = self.sb(st, name + "_v", [128, 8], F32)
        sg = self.sb(st, name + "_sg", [128, 8], F32)
        rep = self.sb(st, name + "_rep", [128, 8, 128], F32)
        k.dma("sp", lambda e: e.dma_start(out=v[:], in_=vec_pm_ap), writes=[v])
        k.op("act", lambda e: e.activation(out=sg[:], in_=v[:], func=AF.Silu), reads=[v], writes=[sg])
        for c in range(8):
            k.op("dve", lambda e: e.tensor_copy(out=rep[:, c, :], in_=sg[:, c:c + 1].to_broadcast([128, 128])),
                 reads=[sg], writes=[rep])
        return rep

    def ada_precompute(self, c_pm, cc_pm, ada_w, ada_b, depth, rows_d):
        k = self.k
        with ExitStack() as ls:
            S2 = self.sb(ls, "ap_S2", [128, 8, 2], F32)
            for si, vec in enumerate((c_pm, cc_pm)):
                v = self.sb(ls, f"ap_v{si}", [128, 8], F32)
                sg = self.sb(ls, f"ap_sg{si}", [128, 8], F32)
                k.dma("sp", lambda e: e.dma_start(out=v[:], in_=vec), writes=[v])
                k.op("act", lambda e: e.activation(out=sg[:], in_=v[:], func=AF.Silu), reads=[v], writes=[sg])
                k.op("dve", lambda e: e.tensor_copy(out=S2[:, :, si], in_=sg[:]), reads=[sg], writes=[S2])
            wb = [self.sb(ls, f"ap_w{j}", [128, 8, 1024], F32) for j in range(2)]
            bb = self.sb(ls, "ap_b", [2, 6 * D], F32)
            rowt = [self.sb(ls, f"ap_r{j}", [2, 6 * D], F32) for j in range(2)]
            n = 0
            for i in range(depth):
                av = _pm_view(ada_w[i])
                rt = rowt[i % 2]
                k.dma("sp", lambda e: e.dma_start(out=bb[:], in_=ada_b[i:i + 1, :].partition_broadcast(2)), writes=[bb])
                for j in range(6):
                    wj = wb[n % 2]
                    n += 1
                    k.dma("sp", lambda e: e.dma_start(out=wj[:], in_=av[:, :, j * 1024:(j + 1) * 1024]), writes=[wj])
                    p = self.psum()
                    for h in range(2):
                        for c in range(8):
                            k.op("pe", lambda e: e.matmul(p[0:2, h * 512:(h + 1) * 512], lhsT=S2[:, c, :], rhs=wj[:, c, h * 512:(h + 1) * 512],
                                                          start=(c == 0), stop=(c == 7)), reads=[S2, wj], writes=[p])
                    k.op("dve", lambda e: e.tensor_tensor(out=rt[:, j * 1024:(j + 1) * 1024], in0=p[0:2, :], in1=bb[:, j * 1024:(j + 1) * 1024],
                                                          op=ALU.add), reads=[p, bb], writes=[rt])
                k.dma("sp", lambda e: e.dma_start(out=rows_d[i], in_=rt[:]), reads=[rt])
            self.k.barrier()

    def ada_load(self, st, row_ap, norm_g_i, tag):
        k = self.k
        m = [self.sb(st, f"mod{tag}_{j}", [128, 1024], F32) for j in range(6)]
        for j in range(6):
            k.dma("sp", lambda e: e.dma_start(out=m[j][:], in_=row_ap[0:1, j * 1024:(j + 1) * 1024].partition_broadcast(128)), writes=[m[j]])
        with ExitStack() as ls:
            gb = [self.sb(ls, f"adag{tag}_{j}", [128, 1024], F32) for j in range(2)]
            for (jsc, gi) in ((1, 0), (4, 1)):
                k.dma("sp", lambda e: e.dma_start(out=gb[gi][:], in_=norm_g_i[gi:gi + 1, :].partition_broadcast(128)), writes=[gb[gi]])
                k.op("dve", lambda e: e.scalar_tensor_tensor(out=m[jsc][:], in0=m[jsc][:], scalar=1.0, in1=gb[gi][:],
                                                             op0=ALU.add, op1=ALU.mult), reads=[m[jsc], gb[gi]], writes=[m[jsc]])
            self.k.barrier()
        sh1, a1, g1, sh2, a2, g2 = m
        return a1, sh1, g1, a2, sh2, g2

    def ada_phase(self, st, sil_rep, ada_w_i, ada_b_i, norm_g_i, tag):
        k = self.k
        m = [self.sb(st, f"mod{tag}_{j}", [128, 1024], F32) for j in range(6)]
        av = _pm_view(ada_w_i)
        with ExitStack() as ls:
            if not isinstance(sil_rep, Buf):
                sil_rep = self.make_sil_rep(ls, "silrep" + tag, sil_rep)
            wb = [self.sb(ls, f"adaw{tag}_{j}", [128, 8, 1024], F32) for j in range(2)]
            bb = [self.sb(ls, f"adab{tag}_{j}", [128, 1024], F32) for j in range(2)]
            for j in range(6):
                wj, bj = wb[j % 2], bb[j % 2]
                k.dma("sp", lambda e: e.dma_start(out=wj[:], in_=av[:, :, j * 1024:(j + 1) * 1024]), writes=[wj])
                k.dma("sp", lambda e: e.dma_start(out=bj[:], in_=ada_b_i[0:1, j * 1024:(j + 1) * 1024].partition_broadcast(128)),
                      writes=[bj])
                p = self.psum()
                for h in range(2):
                    for c in range(8):
                        k.op("pe", lambda e: e.matmul(p[:, h * 512:(h + 1) * 512], lhsT=sil_rep[:, c, :],
                                                      rhs=wj[:, c, h * 512:(h + 1) * 512], start=(c == 0), stop=(c == 7)),
                             reads=[sil_rep, wj], writes=[p])
                k.op("dve", lambda e: e.tensor_tensor(out=m[j][:], in0=p[:], in1=bj[:], op=ALU.add),
                     reads=[p, bj], writes=[m[j]])
            for (jsc, gi) in ((1, 0), (4, 1)):
                gb = bb[gi]
                k.dma("sp", lambda e: e.dma_start(out=gb[:], in_=norm_g_i[gi:gi + 1, :].partition_broadcast(128)), writes=[gb])
                k.op("dve", lambda e: e.scalar_tensor_tensor(out=m[jsc][:], in0=m[jsc][:], scalar=1.0, in1=gb[:],
                                                             op0=ALU.add, op1=ALU.mult),
                     reads=[m[jsc], gb], writes=[m[jsc]])
            self.k.barrier()
        sh1, a1, g1, sh2, a2, g2 = m
        return a1, sh1, g1, a2, sh2, g2

    def norm_stats_a(self, xt, tmp, rows=128):
        k = self.k
        ss, rs = self.small()
        R = slice(0, rows)
        k.op("act", lambda e: e.activation(out=tmp[R, :], in_=xt[R, :], func=AF.Square, accum_out=ss[R, :]),
             reads=[xt], writes=[tmp, ss])
        return (ss, rs)

    def norm_stats_b(self, pr, rows=128):
        k = self.k
        ss, rs = pr
        R = slice(0, rows)
        k.op("dve", lambda e: e.tensor_scalar(out=ss[R, :], in0=ss[R, :], scalar1=1.0 / D, scalar2=EPS, op0=ALU.mult, op1=ALU.add),
             reads=[ss], writes=[ss])
        k.op("pool", lambda e: e.tensor_tensor(out=rs[R, :], in0=ss[R, :], in1=self.mhalf[R, 0:1], op=ALU.pow),
             reads=[ss, self.mhalf], writes=[rs])
        return rs

    def norm_stats(self, xt, tmp, rows=128):
        return self.norm_stats_b(self.norm_stats_a(xt, tmp, rows), rows)

    def norm_apply(self, xt, rs, A, B, tmp, out_h, rows=128):
        k = self.k
        R = slice(0, rows)
        k.op("dve", lambda e: e.scalar_tensor_tensor(out=tmp[R, :], in0=xt[R, :], scalar=rs[R, 0:1], in1=A[R, :],
                                                     op0=ALU.mult, op1=ALU.mult),
             reads=[xt, rs, A], writes=[tmp])
        k.op("dve", lambda e: e.tensor_tensor(out=out_h[R, :], in0=tmp[R, :], in1=B[R, :], op=ALU.add),
             reads=[tmp, B], writes=[out_h])

    def norm_tile(self, xt, A, B, tmp, out_h, rows=128):
        rs = self.norm_stats(xt, tmp, rows)
        self.norm_apply(xt, rs, A, B, tmp, out_h, rows)

    def transpose_bf_to(self, src, dst, col0, rows=128):
        k = self.k
        p = self.psum()
        pv = pbf(p)[:, 0:1024].rearrange("p (c t) -> p c t", c=8)
        for c in range(8):
            k.op("pe", lambda e: e.transpose(pv[:, c, 0:rows], src[0:rows, c * 128:(c + 1) * 128], self.identb[0:rows, 0:rows]),
                 reads=[src, self.identb], writes=[p])
        k.op("act", lambda e: e.copy(out=dst[:, :, col0:col0 + rows], in_=pv[:, :, 0:rows]), reads=[p], writes=[dst])

    def phase_a(self, st, src_ap, T, A1, B1, tag, hT=None):
        k = self.k
        if hT is None:
            hT = self.sb(st, f"hT{tag}", [128, 8, T], BF16)
        with ExitStack() as ls:
            xts = [self.sb(ls, f"pa_x{i}", [128, 1024], F32) for i in range(3)]
            tmps = [self.sb(ls, f"pa_t{i}", [128, 1024], F32) for i in range(3)]
            hbs = [self.sb(ls, f"pa_h{i}", [128, 1024], BF16) for i in range(3)]
            NTT = T // 128
            rss = {}

            def st1(tt):
                xt, tmp = xts[tt % 3], tmps[tt % 3]
                k.dma("sp", lambda e: e.dma_start(out=xt[:], in_=src_ap[tt * 128:(tt + 1) * 128, :]), writes=[xt])
                rss[tt] = self.norm_stats_a(xt, tmp)

            def st1b(tt):
                rss[tt] = self.norm_stats_b(rss[tt])

            def st2(tt):
                self.norm_apply(xts[tt % 3], rss.pop(tt), A1, B1, tmps[tt % 3], hbs[tt % 3])

            def st3(tt):
                self.transpose_bf_to(hbs[tt % 3], hT, tt * 128)

            for step in range(NTT + 2):
                if step < NTT:
                    st1(step)
                if 0 <= step - 1 < NTT:
                    st2(step - 1)
                if step < NTT:
                    st1b(step)
                if 0 <= step - 2 < NTT:
                    st3(step - 2)
            self.k.barrier()
        return hT

    def load_w_bf(self, dst, w_ap2d, col0, ncols):
        v = _pm_view(w_ap2d)
        self.k.dma("pool", lambda e: e.dma_start(out=dst[:, :, 0:ncols], in_=v[:, :, col0:col0 + ncols]), writes=[dst])

    def phase_b_conv(self, hT, T, w_in, cw_pm, zT_d, tag):
        k = self.k
        TB = min(512, T)
        with ExitStack() as ls:
            win = self.sb(ls, "cv_win", [128, 8, 3072], BF16)
            for q in range(3):
                v = _pm_view(w_in)
                k.dma("pool", lambda e: e.dma_start(out=win[:, :, q * 1024:(q + 1) * 1024], in_=v[:, :, q * 1024:(q + 1) * 1024]),
                      writes=[win])
            cw = self.sb(ls, "cv_cw", [128, 8, 3], F32)
            k.dma("sp", lambda e: e.dma_start(out=cw[:], in_=cw_pm), writes=[cw])
            cv = self.sb(ls, "cv_cv", [128, T + 2], F32)
            gb = self.sb(ls, "cv_gb", [128, T], BF16)
            acc = self.sb(ls, "cv_acc", [128, T], F32)
            vt = [self.sb(ls, f"cv_vt{i}", [128, TB], F32) for i in range(2)]
            zr = [self.sb(ls, f"cv_zr{i}", [128, T], BF16) for i in range(1)]
            k.op("dve", lambda e: e.memset(cv[:, 0:1], 0.0), writes=[cv])
            k.op("dve", lambda e: e.memset(cv[:, T + 1:T + 2], 0.0), writes=[cv])
            for j in range(8):
                for tb in range(T // TB):
                    ts = slice(tb * TB, (tb + 1) * TB)
                    pb_, pc_, pv_ = self.psum(), self.psum(), self.psum()
                    for q, p in ((0, pb_), (1, pc_), (2, pv_)):
                        for c in range(8):
                            k.op("pe", lambda e: e.matmul(p[:, 0:TB], lhsT=win[:, c, q * 1024 + j * 128:q * 1024 + (j + 1) * 128],
                                                          rhs=hT[:, c, ts], start=(c == 0), stop=(c == 7)),
                                 reads=[win, hT], writes=[p])
                    v_ = vt[tb % 2]
                    k.op("act", lambda e: e.copy(out=v_[:, 0:TB], in_=pv_[:, 0:TB]), reads=[pv_], writes=[v_])
                    k.op("dve", lambda e: e.tensor_tensor(out=cv[:, 1 + tb * TB:1 + (tb + 1) * TB], in0=pc_[:, 0:TB], in1=v_[:, 0:TB],
                                                          op=ALU.mult), reads=[pc_, v_], writes=[cv])
                    k.op("act", lambda e: e.copy(out=gb[:, ts], in_=pb_[:, 0:TB]), reads=[pb_], writes=[gb])
                z = zr[0]
                k.op("dve", lambda e: e.tensor_scalar(out=acc[:], in0=cv[:, 0:T], scalar1=cw[:, j, 0:1], scalar2=None, op0=ALU.mult),
                     reads=[cv, cw], writes=[acc])
                k.op("dve", lambda e: e.scalar_tensor_tensor(out=acc[:], in0=cv[:, 1:T + 1], scalar=cw[:, j, 1:2], in1=acc[:],
                                                             op0=ALU.mult, op1=ALU.add), reads=[cv, cw, acc], writes=[acc])
                k.op("dve", lambda e: e.scalar_tensor_tensor(out=acc[:], in0=cv[:, 2:T + 2], scalar=cw[:, j, 2:3], in1=acc[:],
                                                             op0=ALU.mult, op1=ALU.add), reads=[cv, cw, acc], writes=[acc])
                k.op("dve", lambda e: e.tensor_tensor(out=z[:], in0=acc[:], in1=gb[:], op=ALU.mult), reads=[acc, gb], writes=[z])
                k.dma("sp", lambda e: e.dma_start(out=zT_d[j * 128:(j + 1) * 128, :], in_=z[:]), reads=[z])
            self.k.barrier()

    def phase_c(self, T, zT_d, w_out, lat_src, lat_dst, G1, A2, B2, w_router, h2_d, st_out, tag, want_moe=True, z_tok=False):
        k = self.k
        TB = min(512, T)
        affT = self.sb(st_out, f"affT{tag}", [16, T], F32) if want_moe else None
        with ExitStack() as ls:
            wo = self.sb(ls, "pc_wo", [128, 8, 1024], BF16)
            self.load_w_bf(wo, w_out, 0, 1024)
            zts = [self.sb(ls, f"pc_z{i}", [128, 8, TB], BF16) for i in range(2)]
            xts = [self.sb(ls, f"pc_x{i}", [128, 1024], F32) for i in range(3)]
            tmps = [self.sb(ls, f"pc_t{i}", [128, 1024], F32) for i in range(3)]
            h2s = [self.sb(ls, f"pc_h{i}", [128, 1024], F32) for i in range(2)]
            h2bs = [self.sb(ls, f"pc_hb{i}", [128, 1024], BF16) for i in range(2)]
            h2Ts = [self.sb(ls, f"pc_hT{i}", [128, 8, TB], F32) for i in range(2)]
            if want_moe:
                wr = self.sb(ls, "pc_wr", [128, 8, 16], F32)
                k.dma("sp", lambda e: e.dma_start(out=wr[:], in_=_pm_view(w_router)), writes=[wr])
            if not z_tok:
                zv = zT_d.rearrange("(c p) t -> p c t", p=128)
            else:
                zrow = [self.sb(ls, f"pc_zr{i}", [128, 1024], BF16) for i in range(2)]
            NSUB = TB // 128
            tiles = [(tb, sub) for tb in range(T // TB) for sub in range(NSUB)]

            def stA(i):
                tb, sub = tiles[i]
                zt = zts[tb % 2]
                if sub == 0:
                    if not z_tok:
                        k.dma("sp", lambda e: e.dma_start(out=zt[:], in_=zv[:, :, tb * TB:(tb + 1) * TB]), writes=[zt])
                    else:
                        for sb_ in range(NSUB):
                            zr_ = zrow[sb_ % 2]
                            r0 = tb * TB + sb_ * 128
                            k.dma("sp", lambda e: e.dma_start(out=zr_[:], in_=zT_d[r0:r0 + 128, :]), writes=[zr_])
                            self.transpose_bf_to(zr_, zt, sb_ * 128)
                tt = tb * NSUB + sub
                xt, tmp = xts[i % 3], tmps[i % 3]
                rows = slice(tt * 128, (tt + 1) * 128)
                k.dma("sp", lambda e: e.dma_start(out=xt[:], in_=lat_src[rows, :]), writes=[xt])
                p = self.psum()
                for c in range(8):
                    for h in range(2):
                        k.op("pe", lambda e: e.matmul(p[:, h * 512:(h + 1) * 512], lhsT=zt[:, c, sub * 128:(sub + 1) * 128],
                                                      rhs=wo[:, c, h * 512:(h + 1) * 512], start=(c == 0), stop=(c == 7)),
                             reads=[zt, wo], writes=[p])
                k.op("dve", lambda e: e.tensor_tensor(out=tmp[:], in0=p[:], in1=G1[:], op=ALU.mult), reads=[p, G1], writes=[tmp])
                k.op("dve", lambda e: e.tensor_tensor(out=xt[:], in0=tmp[:], in1=xt[:], op=ALU.add), reads=[tmp, xt], writes=[xt])
                k.dma("pool", lambda e: e.dma_start(out=lat_dst[rows, :], in_=xt[:]), reads=[xt])
                if want_moe:
                    return self.norm_stats_a(xt, tmp)
                return None

            def stB(i, rs):
                tb, sub = tiles[i]
                tt = tb * NSUB + sub
                rows = slice(tt * 128, (tt + 1) * 128)
                xt, tmp, h2, h2b = xts[i % 3], tmps[i % 3], h2s[i % 2], h2bs[i % 2]
                self.norm_apply(xt, rs, A2, B2, tmp, h2)
                k.op("act", lambda e: e.copy(out=h2b[:], in_=h2[:]), reads=[h2], writes=[h2b])
                k.dma("pool", lambda e: e.dma_start(out=h2_d[rows, :], in_=h2b[:]), reads=[h2b])

            def stC(i):
                tb, sub = tiles[i]
                h2, h2T = h2s[i % 2], h2Ts[tb % 2]
                p2 = self.psum()
                p2v = p2.t[:].rearrange("p (c t) -> p c t", c=8)
                for c in range(8):
                    k.op("pe", lambda e: e.transpose(p2v[:, c, :], h2[:, c * 128:(c + 1) * 128], self.ident[:]),
                         reads=[h2, self.ident], writes=[p2])
                k.op("act", lambda e: e.copy(out=h2T[:, :, sub * 128:(sub + 1) * 128], in_=p2v), reads=[p2], writes=[h2T])
                if sub == NSUB - 1:
                    p3 = self.psum()
                    for c in range(8):
                        k.op("pe", lambda e: e.matmul(p3[0:16, 0:TB], lhsT=wr[:, c, :], rhs=h2T[:, c, :], start=(c == 0), stop=(c == 7)),
                             reads=[wr, h2T], writes=[p3])
                    k.op("act", lambda e: e.activation(out=affT[:, tb * TB:(tb + 1) * TB], in_=p3[0:16, 0:TB], func=AF.Exp),
                         reads=[p3], writes=[affT])

            NTL = len(tiles)
            rsd = {}
            for step in range(NTL + 2):
                if step < NTL:
                    rsd[step] = stA(step)
                if want_moe and 0 <= step - 1 < NTL:
                    stB(step - 1, rsd.pop(step - 1))
                if want_moe and step < NTL:
                    rsd[step] = self.norm_stats_b(rsd[step])
                if want_moe and 0 <= step - 2 < NTL:
                    stC(step - 2)
            if want_moe:
                rc = self.sb(ls, "pc_rc", [16, TB], F32)
                for tb in range(T // TB):
                    ts = slice(tb * TB, (tb + 1) * TB)
                    p = self.psum()
                    k.op("pe", lambda e: e.matmul(p[0:16, 0:TB], lhsT=self.ones_f[0:16, 0:16], rhs=affT[:, ts], start=True, stop=True),
                         reads=[self.ones_f, affT], writes=[p])
                    k.op("dve", lambda e: e.reciprocal(out=rc[:], in_=p[0:16, 0:TB]), reads=[p], writes=[rc])
                    k.op("dve", lambda e: e.tensor_tensor(out=affT[:, ts], in0=affT[:, ts], in1=rc[:], op=ALU.mult),
                         reads=[affT, rc], writes=[affT])
            self.k.barrier()
        return affT


class Moe(Layers):
    def moe_route(self, T, affT, idx, gsel, n_iter=27, dbg=None):
        k = self.k
        cap = 2 * T // NE
        CW = min(cap, 128)
        NJT = cap // CW
        NT = T // 128
        if True:
            with ExitStack() as ls:
                junk = self.sb(ls, "mo_junk", [16, T], F32)
                mask = self.sb(ls, "mo_mask", [16, T], F32)
                cum = self.sb(ls, "mo_cum", [16, T], F32)
                sv = {n: self.sb(ls, "mo_" + n, [16, 1], F32) for n in ("lo", "hi", "mid", "cnt", "ge", "nge", "t1")}
                lo, hi, mid, cnt, ge, nge, t1 = (sv[n] for n in ("lo", "hi", "mid", "cnt", "ge", "nge", "t1"))
                if T >= 1024:
                    W = T // 8
                    if not hasattr(self, "aff_scr"):
                        self.aff_scr = self.nc.dram_tensor("s_affscr", [16, T], F32).ap()
                        self.aff_scr_buf = Buf(None, "aff_scr")
                    a128 = self.sb(ls, "mo_a128", [128, W], F32)
                    j128 = self.sb(ls, "mo_j128", [128, W], F32)
                    G = self.sb(ls, "mo_G", [16, 128], F32)
                    GT8 = self.sb(ls, "mo_GT8", [128, 16], F32)
                    Bd = self.sb(ls, "mo_Bd", [128, 128], F32)
                    bv = {n_: self.sb(ls, "mo_b" + n_, [128, 1], F32) for n_ in ("lo", "hi", "mid", "cnt", "ge", "nge", "t1")}
                    k.dma("sp", lambda e: e.dma_start(out=self.aff_scr[:, 0:T], in_=affT[:]), reads=[affT], writes=[self.aff_scr_buf])
                    k.dma("sp", lambda e: e.dma_start(out=a128[:], in_=self.aff_scr[:, 0:T].rearrange("e (s j) -> (e s) j", s=8)),
                          reads=[self.aff_scr_buf], writes=[a128])
                    k.op("pool", lambda e: e.iota(G[:], pattern=[[1, 16], [0, 8]], base=0, channel_multiplier=0,
                                                  allow_small_or_imprecise_dtypes=True), writes=[G])
                    k.op("dve", lambda e: e.tensor_scalar(out=G[:], in0=G[:], scalar1=self.pidx[0:16, 0:1], scalar2=None, op0=ALU.is_equal),
                         reads=[G, self.pidx], writes=[G])
                    p = self.psum()
                    k.op("pe", lambda e: e.matmul(p[:, 0:128], lhsT=G[:], rhs=G[:], start=True, stop=True), reads=[G], writes=[p])
                    k.op("dve", lambda e: e.tensor_copy(out=Bd[:], in_=p[:, 0:128]), reads=[p], writes=[Bd])
                    p = self.psum()
                    k.op("pe", lambda e: e.transpose(p[:, 0:16], G[:], self.ident[0:16, 0:16]), reads=[G, self.ident], writes=[p])
                    k.op("dve", lambda e: e.tensor_scalar(out=GT8[:], in0=p[:, 0:16], scalar1=0.125, scalar2=None, op0=ALU.mult),
                         reads=[p], writes=[GT8])
                    blo, bhi, bmid, bcnt, bge, bnge, bt1 = (bv[n_] for n_ in ("lo", "hi", "mid", "cnt", "ge", "nge", "t1"))
                    k.op("dve", lambda e: e.memset(blo[:], 0.0), writes=[blo])
                    k.op("dve", lambda e: e.memset(bhi[:], 1.5), writes=[bhi])
                    for _ in range(n_iter):
                        k.op("dve", lambda e: e.tensor_tensor(out=bmid[:], in0=blo[:], in1=bhi[:], op=ALU.add), reads=[blo, bhi], writes=[bmid])
                        k.op("dve", lambda e: e.tensor_scalar(out=bmid[:], in0=bmid[:], scalar1=0.5, scalar2=None, op0=ALU.mult),
                             reads=[bmid], writes=[bmid])
                        k.op("dve", lambda e: e.tensor_scalar(out=j128[:], in0=a128[:], scalar1=bmid[:, 0:1], scalar2=0.0,
                                                              op0=ALU.is_ge, op1=ALU.add, accum_out=bcnt[:]),
                             reads=[a128, bmid], writes=[j128, bcnt])
                        pt_ = self.psum()
                        k.op("pe", lambda e: e.matmul(pt_[:, 0:1], lhsT=Bd[:], rhs=bcnt[:], start=True, stop=True), reads=[Bd, bcnt], writes=[pt_])
                        k.op("dve", lambda e: e.tensor_scalar(out=bge[:], in0=pt_[:, 0:1], scalar1=float(cap), scalar2=None, op0=ALU.is_ge),
                             reads=[pt_], writes=[bge])
                        k.op("dve", lambda e: e.tensor_scalar(out=bnge[:], in0=pt_[:, 0:1], scalar1=float(cap), scalar2=None, op0=ALU.is_lt),
                             reads=[pt_], writes=[bnge])
                        k.op("dve", lambda e: e.tensor_tensor(out=bt1[:], in0=bmid[:], in1=blo[:], op=ALU.subtract), reads=[bmid, blo], writes=[bt1])
                        k.op("dve", lambda e: e.scalar_tensor_tensor(out=blo[:], in0=bt1[:], scalar=bge[:, 0:1], in1=blo[:],
                                                                     op0=ALU.mult, op1=ALU.add), reads=[bt1, bge, blo], writes=[blo])
                        k.op("dve", lambda e: e.tensor_tensor(out=bt1[:], in0=bmid[:], in1=bhi[:], op=ALU.subtract), reads=[bmid, bhi], writes=[bt1])
                        k.op("dve", lambda e: e.scalar_tensor_tensor(out=bhi[:], in0=bt1[:], scalar=bnge[:, 0:1], in1=bhi[:],
                                                                     op0=ALU.mult, op1=ALU.add), reads=[bt1, bnge, bhi], writes=[bhi])
                    p = self.psum()
                    k.op("pe", lambda e: e.matmul(p[0:16, 0:1], lhsT=GT8[:], rhs=blo[:], start=True, stop=True), reads=[GT8, blo], writes=[p])
                    k.op("dve", lambda e: e.tensor_copy(out=lo[:], in_=p[0:16, 0:1]), reads=[p], writes=[lo])
                else:
                    k.op("dve", lambda e: e.memset(lo[:], 0.0), writes=[lo])
                    k.op("dve", lambda e: e.memset(hi[:], 1.5), writes=[hi])
                    for _ in range(n_iter):
                        k.op("dve", lambda e: e.tensor_tensor(out=mid[:], in0=lo[:], in1=hi[:], op=ALU.add), reads=[lo, hi], writes=[mid])
                        k.op("dve", lambda e: e.tensor_scalar(out=mid[:], in0=mid[:], scalar1=0.5, scalar2=None, op0=ALU.mult),
                             reads=[mid], writes=[mid])
                        k.op("dve", lambda e: e.tensor_scalar(out=junk[:], in0=affT[:], scalar1=mid[:, 0:1], scalar2=0.0,
                                                              op0=ALU.is_ge, op1=ALU.add, accum_out=cnt[:]),
                             reads=[affT, mid], writes=[junk, cnt])
                        k.op("dve", lambda e: e.tensor_scalar(out=ge[:], in0=cnt[:], scalar1=float(cap), scalar2=None, op0=ALU.is_ge),
                             reads=[cnt], writes=[ge])
                        k.op("dve", lambda e: e.tensor_scalar(out=nge[:], in0=cnt[:], scalar1=float(cap), scalar2=None, op0=ALU.is_lt),
                             reads=[cnt], writes=[nge])
                        k.op("dve", lambda e: e.tensor_tensor(out=t1[:], in0=mid[:], in1=lo[:], op=ALU.subtract), reads=[mid, lo], writes=[t1])
                        k.op("dve", lambda e: e.scalar_tensor_tensor(out=lo[:], in0=t1[:], scalar=ge[:, 0:1], in1=lo[:],
                                                                     op0=ALU.mult, op1=ALU.add), reads=[t1, ge, lo], writes=[lo])
                        k.op("dve", lambda e: e.tensor_tensor(out=t1[:], in0=mid[:], in1=hi[:], op=ALU.subtract), reads=[mid, hi], writes=[t1])
                        k.op("dve", lambda e: e.scalar_tensor_tensor(out=hi[:], in0=t1[:], scalar=nge[:, 0:1], in1=hi[:],
                                                                     op0=ALU.mult, op1=ALU.add), reads=[t1, nge, hi], writes=[hi])
                k.op("dve", lambda e: e.tensor_scalar(out=mask[:], in0=affT[:], scalar1=lo[:, 0:1], scalar2=None, op0=ALU.is_ge),
                     reads=[affT, lo], writes=[mask])
                k.op("dve", lambda e: e.memset(junk[:], 1.0), writes=[junk])
                k.op("dve", lambda e: e.tensor_tensor_scan(out=cum[:], data0=junk[:], data1=mask[:], initial=0.0,
                                                           op0=ALU.mult, op1=ALU.add), reads=[junk, mask], writes=[cum])
                k.op("dve", lambda e: e.tensor_tensor(out=cum[:], in0=cum[:], in1=mask[:], op=ALU.mult), reads=[cum, mask], writes=[cum])
                k.op("dve", lambda e: e.tensor_scalar(out=cum[:], in0=cum[:], scalar1=-1.0, scalar2=None, op0=ALU.add),
                     reads=[cum], writes=[cum])
                pmT = self.sb(ls, "mo_pmT", [128, NT, 16], F32)
                gT = self.sb(ls, "mo_gT", [128, NT, 16], F32)
                rT = self.sb(ls, "mo_rT", [128, NT, 16], F32)
                GB = self.sb(ls, "mo_GB", [128, NT, NE, 5], BF16)
                gp = self.sb(ls, "mo_gp", [128, NT, 16], BF16)
                hl = self.sb(ls, "mo_hl", [128, NT, 2], F32)
                for tc in range(NT):
                    cs = slice(tc * 128, (tc + 1) * 128)
                    p = self.psum()
                    k.op("pe", lambda e: e.transpose(p[:, 0:16], cum[:, cs], self.ident[0:16, 0:16]),
                         reads=[cum, self.ident], writes=[p])
                    k.op("pe", lambda e: e.transpose(p[:, 16:32], affT[:, cs], self.ident[0:16, 0:16]),
                         reads=[affT, self.ident], writes=[p])
                    k.op("act", lambda e: e.copy(out=pmT[:, tc, :], in_=p[:, 0:16]), reads=[p], writes=[pmT])
                    k.op("act", lambda e: e.copy(out=gT[:, tc, :], in_=p[:, 16:32]), reads=[p], writes=[gT])
                    k.op("dve", lambda e: e.memset(hl[:, tc, 0:1], float(tc)), writes=[hl])
                k.op("dve", lambda e: e.tensor_copy(out=hl[:, :, 1:2], in_=self.pidx[:, 0:1].unsqueeze(1).to_broadcast([128, NT, 1])),
                     reads=[self.pidx], writes=[hl])
                for piece in range(3):
                    k.op("dve", lambda e: e.tensor_copy(out=gp[:], in_=gT[:]), reads=[gT], writes=[gp])
                    k.op("dve", lambda e: e.tensor_copy(out=GB[:, :, :, 2 + piece], in_=gp[:]), reads=[gp], writes=[GB])
                    if piece < 2:
                        k.op("dve", lambda e: e.tensor_copy(out=rT[:], in_=gp[:]), reads=[gp], writes=[rT])
                        k.op("dve", lambda e: e.tensor_tensor(out=gT[:], in0=gT[:], in1=rT[:], op=ALU.subtract), reads=[gT, rT], writes=[gT])
                for c2 in range(2):
                    k.op("dve", lambda e: e.tensor_copy(out=GB[:, :, :, c2], in_=hl[:, :, c2:c2 + 1].to_broadcast([128, NT, NE])),
                         reads=[hl], writes=[GB])
                ohs = [self.sb(ls, f"mo_oh{i}", [128, cap], BF16) for i in range(4)]
                selS = [self.sb(ls, f"mo_selS{i}", [8, cap], F32) for i in range(2)]
                selT = [self.sb(ls, f"mo_selT{i}", [128, NJT, 5], F32) for i in range(2)]
                n = 0
                for ex in range(NE):
                    p = self.psum()
                    for tc in range(NT):
                        oh = ohs[n % 4]
                        n += 1
                        if True:
                            k.op("dve", lambda e: e.tensor_scalar(out=oh[:], in0=self.iota_h[:, 0:cap], scalar1=pmT[:, tc, ex:ex + 1],
                                                                  scalar2=None, op0=ALU.is_equal),
                                 reads=[self.iota_h, pmT], writes=[oh])
                        k.op("pe", lambda e: e.matmul(p[0:5, 0:cap], lhsT=GB[:, tc, ex, :], rhs=oh[:], start=(tc == 0), stop=(tc == NT - 1)),
                             reads=[GB, oh], writes=[p])
                    sS, sT = selS[ex % 2], selT[ex % 2]
                    k.op("act", lambda e: e.copy(out=sS[0:5, :], in_=p[0:5, 0:cap]), reads=[p], writes=[sS])
                    p2 = self.psum()
                    for jt in range(NJT):
                        k.op("pe", lambda e: e.transpose(p2[0:CW, jt * 8:jt * 8 + 5], sS[0:5, jt * CW:(jt + 1) * CW], self.ident[0:5, 0:5]),
                             reads=[sS, self.ident], writes=[p2])
                    p2v = p2.t[:, 0:NJT * 8].rearrange("p (j c) -> p j c", c=8)
                    k.op("act", lambda e: e.copy(out=sT[0:CW, :, :], in_=p2v[0:CW, :, 0:5]), reads=[p2], writes=[sT])
                    cols = slice(ex * NJT, (ex + 1) * NJT)
                    k.op("dve", lambda e: e.scalar_tensor_tensor(out=sT[0:CW, :, 0], in0=sT[0:CW, :, 0], scalar=128.0, in1=sT[0:CW, :, 1],
                                                                 op0=ALU.mult, op1=ALU.add), reads=[sT], writes=[sT])
                    k.op("dve", lambda e: e.tensor_copy(out=idx[0:CW, cols], in_=sT[0:CW, :, 0]), reads=[sT], writes=[idx])
                    k.op("dve", lambda e: e.tensor_tensor(out=sT[0:CW, :, 2], in0=sT[0:CW, :, 2], in1=sT[0:CW, :, 3], op=ALU.add),
                         reads=[sT], writes=[sT])
                    k.op("dve", lambda e: e.tensor_tensor(out=gsel[0:CW, cols], in0=sT[0:CW, :, 2], in1=sT[0:CW, :, 4], op=ALU.add),
                         reads=[sT], writes=[gsel])
                if dbg is not None:
                    k.dma("sp", lambda e: e.dma_start(out=dbg[0], in_=idx[:]), reads=[idx])
                    k.dma("sp", lambda e: e.dma_start(out=dbg[1], in_=gsel[:]), reads=[gsel])
                    k.dma("sp", lambda e: e.dma_start(out=dbg[2], in_=affT[:]), reads=[affT])
                    k.dma("sp", lambda e: e.dma_start(out=dbg[3], in_=cum[:]), reads=[cum])
                self.k.barrier()

    def moe_experts(self, streams, wg_d, wu_d, wd_d):
        k = self.k
        lat_sc = Buf(None, "lat_scatter")
        for s_ in streams:
            T = s_["T"]
            s_["cap"] = 2 * T // NE
            s_["CW"] = min(s_["cap"], 128)
            s_["NJT"] = s_["cap"] // s_["CW"]
            s_["reg"] = self.nc.gpsimd.to_reg(T - 1)
        if True:
            with ExitStack() as ls:
                W = [[self.sb(ls, f"mo_w{n_}{i}", [128, 8, 1024], BF16) for n_ in "gud"] for i in range(2)]
                for si, s_ in enumerate(streams):
                    cap = s_["cap"]
                    s_["xs"] = [self.sb(ls, f"mo_xs{si}_{i}", [128, 1024], BF16) for i in range(4 if cap > 128 else 2)]
                    s_["xsT"] = [self.sb(ls, f"mo_xsT{si}_{i}", [128, 8, cap], BF16) for i in range(2)]
                    s_["hidT"] = [self.sb(ls, f"mo_hidT{si}_{i}", [128, 8, cap], BF16) for i in range(2)]
                    s_["sg"] = [self.sb(ls, f"mo_sg{si}_{i}", [128, cap], F32) for i in range(2)]
                    s_["ys"] = [self.sb(ls, f"mo_ys{si}_{i}", [128, 1024], F32) for i in range(2)]
                    s_["nys"] = 0

                def load_w(ex):
                    for n_, src in enumerate((wg_d, wu_d, wd_d)):
                        self.load_w_bf(W[ex % 2][n_], src[ex], 0, 1024)

                def gather(ex):
                    for s_ in streams:
                        NJT, CW = s_["NJT"], s_["CW"]
                        for jt in range(NJT):
                            col = ex * NJT + jt
                            x_ = s_["xs"][(ex * NJT + jt) % len(s_["xs"])]
                            idx, h2_d, reg = s_["idx"], s_["h2_d"], s_["reg"]
                            k.dma("pool", lambda e: e.indirect_dma_start(
                                out=x_[0:CW, :], out_offset=None, in_=h2_d[:, :],
                                in_offset=bass.IndirectOffsetOnAxis(ap=idx[0:CW, col:col + 1], axis=0),
                                bounds_check=reg, oob_is_err=False), reads=[idx], writes=[x_])

                load_w(0)
                gather(0)
                for ex in range(NE):
                    wg, wu, wd = W[ex % 2]
                    for s_ in streams:
                        NJT, CW = s_["NJT"], s_["CW"]
                        xT = s_["xsT"][ex % 2]
                        for jt in range(NJT):
                            x_ = s_["xs"][(ex * NJT + jt) % len(s_["xs"])]
                            p = self.psum()
                            pv = pbf(p)[:, 0:1024].rearrange("p (c t) -> p c t", c=8)
                            for c in range(8):
                                k.op("pe", lambda e: e.transpose(pv[:, c, 0:CW], x_[0:CW, c * 128:(c + 1) * 128], self.identb[0:CW, 0:CW]),
                                     reads=[x_, self.identb], writes=[p])
                            k.op("act", lambda e: e.copy(out=xT[:, :, jt * CW:(jt + 1) * CW], in_=pv[:, :, 0:CW]), reads=[p], writes=[xT])
                    if ex + 1 < NE:
                        load_w(ex + 1)
                        gather(ex + 1)
                    for s_ in streams:
                        cap, NJT, CW = s_["cap"], s_["NJT"], s_["CW"]
                        xT, hidT, sg = s_["xsT"][ex % 2], s_["hidT"][ex % 2], s_["sg"]
                        for fc in range(8):
                            pg, pu = self.psum(), self.psum()
                            for (p, w) in ((pg, wg), (pu, wu)):
                                for c in range(8):
                                    k.op("pe", lambda e: e.matmul(p[:, 0:cap], lhsT=w[:, c, fc * 128:(fc + 1) * 128], rhs=xT[:, c, :],
                                                                  start=(c == 0), stop=(c == 7)), reads=[w, xT], writes=[p])
                            sg_ = sg[fc % 2]
                            k.op("act", lambda e: e.activation(out=sg_[:], in_=pg[:, 0:cap], func=AF.Silu), reads=[pg], writes=[sg_])
                            k.op("dve", lambda e: e.tensor_tensor(out=hidT[:, fc, :], in0=sg_[:], in1=pu[:, 0:cap], op=ALU.mult),
                                 reads=[sg_, pu], writes=[hidT])
                    for s_ in streams:
                        cap, NJT, CW = s_["cap"], s_["NJT"], s_["CW"]
                        hidT, idx, gsel, G2, lat_d, reg = s_["hidT"][ex % 2], s_["idx"], s_["gsel"], s_["G2"], s_["lat_d"], s_["reg"]
                        for jt in range(NJT):
                            col = ex * NJT + jt
                            p = self.psum()
                            for fc in range(8):
                                for h in range(2):
                                    k.op("pe", lambda e: e.matmul(p[0:CW, h * 512:(h + 1) * 512], lhsT=hidT[:, fc, jt * CW:(jt + 1) * CW],
                                                                  rhs=wd[:, fc, h * 512:(h + 1) * 512], start=(fc == 0), stop=(fc == 7)),
                                         reads=[hidT, wd], writes=[p])
                            y_ = s_["ys"][s_["nys"] % 2]
                            s_["nys"] += 1
                            k.op("dve", lambda e: e.scalar_tensor_tensor(out=y_[0:CW, :], in0=p[0:CW, :], scalar=gsel[0:CW, col:col + 1],
                                                                         in1=G2[0:CW, :], op0=ALU.mult, op1=ALU.mult),
                                 reads=[p, gsel, G2], writes=[y_])
                            k.dma("pool", lambda e: e.indirect_dma_start(
                                out=lat_d[:, :], out_offset=bass.IndirectOffsetOnAxis(ap=idx[0:CW, col:col + 1], axis=0),
                                in_=y_[0:CW, :], in_offset=None, bounds_check=reg, oob_is_err=False, compute_op=ALU.add),
                                reads=[idx, y_], writes=[lat_sc])
                self.k.barrier()

    def moe_phase(self, T, affT, h2_d, lat_d, G2, wg_d, wu_d, wd_d, tag, n_iter=30, dbg=None):
        cap = 2 * T // NE
        NJT = cap // min(cap, 128)
        with ExitStack() as ms:
            idx = self.sb(ms, "mo_idx", [128, NE * NJT], I32)
            gsel = self.sb(ms, "mo_gsel", [128, NE * NJT], F32)
            self.moe_route(T, affT, idx, gsel, n_iter=n_iter, dbg=dbg)
            self.moe_experts([dict(T=T, idx=idx, gsel=gsel, h2_d=h2_d, lat_d=lat_d, G2=G2)], wg_d, wu_d, wd_d)
            self.k.barrier()


class Attn(Moe):
    def phase_b_attn(self, hT, hcT, T, LC, w_qkv, qg_pm, kg_pm, rope_d, rotT_d, zT_d):
        k = self.k
        NK = LC + T
        NKC = NK // 128
        QB = min(512, T)
        with ExitStack() as ls:
            wq = self.sb(ls, "at_wq", [128, 8, 1536], BF16)
            for q in range(3):
                self.k.dma("pool", lambda e: e.dma_start(out=wq[:, :, q * 512:(q + 1) * 512],
                                                        in_=_pm_view(w_qkv)[:, :, q * 512:(q + 1) * 512]), writes=[wq])
            gq = self.sb(ls, "at_gq", [128, 1], F32)
            gk = self.sb(ls, "at_gk", [128, 1], F32)
            k.dma("sp", lambda e: e.dma_start(out=gq[:], in_=qg_pm), writes=[gq])
            k.dma("sp", lambda e: e.dma_start(out=gk[:], in_=kg_pm), writes=[gk])
            rotf = self.sb(ls, "at_rotf", [128, 128], F32)
            rot = self.sb(ls, "at_rot", [128, 128], BF16)
            k.dma("sp", lambda e: e.dma_start(out=rotf[:], in_=rotT_d), writes=[rotf])
            k.op("dve", lambda e: e.tensor_copy(out=rot[:], in_=rotf[:]), reads=[rotf], writes=[rot])
            negc = self.sb(ls, "at_negc", [128, 1], F32)
            ab = self.sb(ls, "at_ab", [128, 2], F32)
            k.op("act", lambda e: e.activation(out=ab[:, 0:1], in_=gq[:], func=AF.Abs), reads=[gq], writes=[ab])
            k.op("act", lambda e: e.activation(out=ab[:, 1:2], in_=gk[:], func=AF.Abs), reads=[gk], writes=[ab])
            p = self.psum()
            k.op("pe", lambda e: e.transpose(p[0:2, 0:128], ab[:, 0:2], self.ident[:]), reads=[ab, self.ident], writes=[p])
            mx = self.sb(ls, "at_mx", [2, 1], F32)
            k.op("dve", lambda e: e.tensor_reduce(out=mx[:], in_=p[0:2, 0:128], axis=AX.X, op=ALU.max), reads=[p], writes=[mx])
            mq = self.sb(ls, "at_mq", [128, 2], F32)
            for i in range(2):
                p = self.psum()
                sel = self.sb(ls, f"at_sel{i}", [2, 128], F32)
                k.op("dve", lambda e: e.memset(sel[:], 0.0), writes=[sel])
                k.op("dve", lambda e: e.tensor_scalar(out=sel[:], in0=self.ones_f[0:2, :], scalar1=self.ident[0:2, i:i + 1], scalar2=None,
                                                      op0=ALU.mult), reads=[self.ones_f, self.ident], writes=[sel])
                k.op("pe", lambda e: e.matmul(p[:, 0:1], lhsT=sel[:], rhs=mx[:], start=True, stop=True), reads=[sel, mx], writes=[p])
                k.op("dve", lambda e: e.tensor_copy(out=mq[:, i:i + 1], in_=p[:, 0:1]), reads=[p], writes=[mq])
            k.op("dve", lambda e: e.scalar_tensor_tensor(out=negc[:], in0=mq[:, 0:1], scalar=-math.sqrt(HD), in1=mq[:, 1:2],
                                                         op0=ALU.mult, op1=ALU.mult), reads=[mq], writes=[negc])

            kT = [self.sb(ls, f"at_kT{i}", [128, NK], BF16) for i in range(NKVH)]
            V = self.sb(ls, "at_V", [128, NKC, NKVH, 129], BF16)
            k.op("dve", lambda e: e.memset(V[:, :, :, 128:129], 1.0), writes=[V])
            sq = [self.sb(ls, "at_sq0", [128, QB], F32)] * 2
            rst = [self.sb(ls, f"at_rst{i}", [128, QB], F32) for i in range(2)]
            xn = [self.sb(ls, f"at_xn{i}", [128, QB], F32) for i in range(2)]
            xnb = [self.sb(ls, f"at_xnb{i}", [128, QB], BF16) for i in range(2)]
            rp = [self.sb(ls, f"at_rp{i}", [128, 2, QB], F32) for i in range(2)]
            tmp = [self.sb(ls, "at_tmp0", [128, QB], F32)] * 2
            cnt = [0]

            def qk_block(srcT, col0, w, wcol, g, dst_ap, rope_blk):
                i = cnt[0] % 2
                cnt[0] += 1
                pq = self.psum()
                for c in range(8):
                    k.op("pe", lambda e: e.matmul(pq[:, 0:w], lhsT=wq[:, c, wcol:wcol + 128], rhs=srcT[:, c, col0:col0 + w],
                                                  start=(c == 0), stop=(c == 7)), reads=[wq, srcT], writes=[pq])
                k.op("act", lambda e: e.activation(out=sq[i][:, 0:w], in_=pq[:, 0:w], func=AF.Square), reads=[pq], writes=[sq[i]])
                k.op("pe", lambda e: e.matmul(pq[:, 512:512 + w], lhsT=self.ones_f[:], rhs=sq[i][:, 0:w], start=True, stop=True),
                     reads=[self.ones_f, sq[i]], writes=[pq])
                k.op("act", lambda e: e.activation(out=rst[i][:, 0:w], in_=pq[:, 512:512 + w], func=AF.Sqrt, bias=self.eps_t[:, 0:1],
                                                   scale=1.0 / HD), reads=[pq, self.eps_t], writes=[rst[i]])
                k.op("dve", lambda e: e.reciprocal(out=rst[i][:, 0:w], in_=rst[i][:, 0:w]), reads=[rst[i]], writes=[rst[i]])
                if rope_blk is None:
                    k.op("dve", lambda e: e.scalar_tensor_tensor(out=dst_ap[0], in0=pq[:, 0:w], scalar=g[:, 0:1], in1=rst[i][:, 0:w],
                                                                 op0=ALU.mult, op1=ALU.mult), reads=[pq, g, rst[i]], writes=[dst_ap[1]])
                    return
                k.op("dve", lambda e: e.scalar_tensor_tensor(out=xn[i][:, 0:w], in0=pq[:, 0:w], scalar=g[:, 0:1], in1=rst[i][:, 0:w],
                                                             op0=ALU.mult, op1=ALU.mult), reads=[pq, g, rst[i]], writes=[xn[i]])
                k.op("act", lambda e: e.copy(out=xnb[i][:, 0:w], in_=xn[i][:, 0:w]), reads=[xn[i]], writes=[xnb[i]])
                pr = self.psum()
                k.op("pe", lambda e: e.matmul(pr[:, 0:w], lhsT=rot[:], rhs=xnb[i][:, 0:w], start=True, stop=True),
                     reads=[rot, xnb[i]], writes=[pr])
                k.op("dve", lambda e: e.tensor_tensor(out=tmp[i][:, 0:w], in0=pr[:, 0:w], in1=rope_blk[:, 1, 0:w], op=ALU.mult),
                     reads=[pr, rope_blk], writes=[tmp[i]])
                k.op("pool", lambda e: e.tensor_tensor(out=xn[i][:, 0:w], in0=xn[i][:, 0:w], in1=rope_blk[:, 0, 0:w], op=ALU.mult),
                     reads=[xn[i], rope_blk], writes=[xn[i]])
                k.op("dve", lambda e: e.tensor_tensor(out=dst_ap[0], in0=xn[i][:, 0:w], in1=tmp[i][:, 0:w], op=ALU.add),
                     reads=[xn[i], tmp[i]], writes=[dst_ap[1]])

            def v_tile(srcT, col0, kc):
                pv = self.psum()
                for c in range(8):
                    k.op("pe", lambda e: e.matmul(pv[:, 0:256], lhsT=srcT[:, c, col0:col0 + 128], rhs=wq[:, c, 1280:1536],
                                                  start=(c == 0), stop=(c == 7)), reads=[wq, srcT], writes=[pv])
                k.op("act", lambda e: e.copy(out=V[:, kc, :, 0:128], in_=pv[:, 0:256].rearrange("p (a d) -> p a d", a=NKVH)),
                     reads=[pv], writes=[V])

            for b0 in range(0, LC, QB):
                w = min(QB, LC - b0)
                for kv in range(NKVH):
                    qk_block(hcT, b0, w, 1024 + kv * 128, gk, (kT[kv][:, b0:b0 + w], kT[kv]), None)
            for t0 in range(0, LC, 128):
                v_tile(hcT, t0, t0 // 128)
            for qb in range(T // QB):
                rb = rp[qb % 2]
                k.dma("sp", lambda e: e.dma_start(out=rb[:, :, 0:QB], in_=rope_d[:, :, qb * QB:(qb + 1) * QB].rearrange("a p t -> p a t")),
                      writes=[rb])
                for kv in range(NKVH):
                    qk_block(hT, qb * QB, QB, 1024 + kv * 128, gk, (kT[kv][:, LC + qb * QB:LC + (qb + 1) * QB], kT[kv]), rb)
            for t0 in range(0, T, 128):
                v_tile(hT, t0, (LC + t0) // 128)
            NS = QB // 128
            qTb = [self.sb(ls, f"at_qT{i}", [128, QB], BF16) for i in range(2)]
            pT = [self.sb(ls, f"at_pT{i}", [128, 2 * QB], BF16) for i in range(3)]
            rden = [self.sb(ls, f"at_rden{i}", [128, 1], F32) for i in range(2)]
            otok = [self.sb(ls, f"at_ot{i}", [128, 1024], BF16) for i in range(2 * NS)]
            scale = 1.0 / math.sqrt(HD)
            groups = [(g0, min(2, NKC - g0)) for g0 in range(0, NKC, 2)]
            n = 0
            items = [(qb, h) for qb in range(T // QB) for h in range(NQH)]

            def prep(i):
                qb, h = items[i]
                rb = rp[qb % 2]
                if h == 0:
                    k.dma("sp", lambda e: e.dma_start(out=rb[:, :, 0:QB], in_=rope_d[:, :, qb * QB:(qb + 1) * QB].rearrange("a p t -> p a t")),
                          writes=[rb])
                qT = qTb[i % 2]
                qk_block(hT, qb * QB, QB, h * 128, gq, (qT[:, 0:QB], qT), rb)

            prep(0)
            for it_, (qb, h) in enumerate(items):
                if True:
                    kv = h // (NQH // NKVH)
                    qT = qTb[it_ % 2]
                    if it_ + 1 < len(items):
                        prep(it_ + 1)
                    accs = [self.ps[0], self.ps[1]]

                    def emit_s(gi):
                        g0, ng = groups[gi]
                        st_ = self.ps[2 + (gi % 2)]
                        for j in range(ng):
                            kc = g0 + j
                            k.op("pe", lambda e: e.matmul(st_[:, j * 512:j * 512 + QB], lhsT=kT[kv][:, kc * 128:(kc + 1) * 128], rhs=qT[:, 0:QB],
                                                          start=True, stop=True), reads=[kT[kv], qT], writes=[st_])
                        return st_

                    sts = {0: emit_s(0)}
                    for gi, (g0, ng) in enumerate(groups):
                        if gi + 1 < len(groups):
                            sts[gi + 1] = emit_s(gi + 1)
                        st_ = sts.pop(gi)
                        pt = pT[n % 3]
                        n += 1
                        if QB == 512:
                            k.op("act", lambda e: e.activation(out=pt[:, 0:ng * 512], in_=st_[:, 0:ng * 512], func=AF.Exp, bias=negc[:, 0:1], scale=scale),
                                 reads=[st_, negc], writes=[pt])
                        else:
                            for j in range(ng):
                                k.op("act", lambda e: e.activation(out=pt[:, j * QB:(j + 1) * QB], in_=st_[:, j * 512:j * 512 + QB], func=AF.Exp,
                                                                   bias=negc[:, 0:1], scale=scale), reads=[st_, negc], writes=[pt])
                        for j in range(ng):
                            kc = g0 + j
                            for s_ in range(NS):
                                a_ = accs[s_ // 2]
                                c0 = (s_ % 2) * 512
                                k.op("pe", lambda e: e.matmul(a_[:, c0:c0 + 129], lhsT=pt[:, j * QB + s_ * 128:j * QB + (s_ + 1) * 128],
                                                              rhs=V[:, kc, kv, :], start=(kc == 0), stop=(kc == NKC - 1)),
                                     reads=[pt, V], writes=[a_])
                    for s_ in range(NS):
                        a_ = accs[s_ // 2]
                        c0 = (s_ % 2) * 512
                        rd = rden[s_ % 2]
                        ot = otok[(qb % 2) * NS + s_]
                        k.op("dve", lambda e: e.reciprocal(out=rd[:], in_=a_[:, c0 + 128:c0 + 129]), reads=[a_], writes=[rd])
                        k.op("dve", lambda e: e.tensor_scalar(out=ot[:, h * 128:(h + 1) * 128], in0=a_[:, c0:c0 + 128], scalar1=rd[:, 0:1],
                                                              scalar2=None, op0=ALU.mult), reads=[a_, rd], writes=[ot])
                for s_ in range(NS if h == NQH - 1 else 0):
                    ot = otok[(qb % 2) * NS + s_]
                    r0 = qb * QB + s_ * 128
                    k.dma("sp", lambda e: e.dma_start(out=zT_d[r0:r0 + 128, :], in_=ot[:]), reads=[ot])
            self.k.barrier()


TWO_PI = 2.0 * math.pi


class Hyena(Attn):
    def hy_filter(self, L, cst, wts, scr, fft=None):
        k = self.k
        NLT = L // 128
        NFT = NLT
        LB = min(512, L)
        with ExitStack() as fs:
            rS = [self.sb(fs, f"hf_rS{o}", [128, 1024], F32) for o in range(2)]
            with ExitStack() as ls:
                zT = self.sb(ls, "hf_zT", [33, L], F32)
                w1 = self.sb(ls, "hf_w1", [33, 64], F32)
                w2 = self.sb(ls, "hf_w2", [64, 64], F32)
                w3 = self.sb(ls, "hf_w3", [64, 4096], F32)
                sm = self.sb(ls, "hf_sm", [64, 4], F32)
                fb = self.sb(ls, "hf_fb", [64, 2], F32)
                a1T = self.sb(ls, "hf_a1T", [64, L], F32)
                a2T = self.sb(ls, "hf_a2T", [64, L + 1], F32)
                arg = self.sb(ls, "hf_arg", [64, LB], F32)
                m1 = self.sb(ls, "hf_m1", [64, LB], F32)
                k.dma("sp", lambda e: e.dma_start(out=zT[:], in_=cst["zT"]), writes=[zT])
                k.dma("sp", lambda e: e.dma_start(out=w1[:], in_=wts["w1"]), writes=[w1])
                k.dma("sp", lambda e: e.dma_start(out=w2[:], in_=wts["w2"]), writes=[w2])
                k.dma("sp", lambda e: e.dma_start(out=w3[:], in_=wts["w3"]), writes=[w3])
                k.dma("sp", lambda e: e.dma_start(out=sm[:, 0:1], in_=wts["b1"]), writes=[sm])
                k.dma("sp", lambda e: e.dma_start(out=sm[:, 1:2], in_=wts["b2"]), writes=[sm])
                k.dma("sp", lambda e: e.dma_start(out=sm[:, 2:3], in_=wts["freq"]), writes=[sm])
                k.op("dve", lambda e: e.tensor_scalar(out=fb[:], in0=sm[:, 0:2], scalar1=sm[:, 2:3], scalar2=None, op0=ALU.mult),
                     reads=[sm], writes=[fb])
                k.op("dve", lambda e: e.memset(a2T[:, L:L + 1], 0.0), writes=[a2T])
                for (wl, kdim, src, dst, bi) in ((w1, 33, zT, a1T, 0), (w2, 64, a1T, a2T, 1)):
                    for b0 in range(0, L, LB):
                        p = self.psum()
                        k.op("pe", lambda e: e.matmul(p[0:64, 0:LB], lhsT=wl[0:kdim, :], rhs=src[0:kdim, b0:b0 + LB], start=True, stop=True),
                             reads=[wl, src], writes=[p])
                        k.op("dve", lambda e: e.tensor_scalar(out=arg[:], in0=p[0:64, 0:LB], scalar1=sm[:, 2:3], scalar2=fb[:, bi:bi + 1],
                                                              op0=ALU.mult, op1=ALU.add), reads=[p, sm, fb], writes=[arg])
                        k.op("dve", lambda e: e.tensor_scalar(out=m1[:], in0=arg[:], scalar1=math.pi, scalar2=-TWO_PI,
                                                              op0=ALU.is_gt, op1=ALU.mult), reads=[arg], writes=[m1])
                        k.op("dve", lambda e: e.tensor_tensor(out=m1[:], in0=m1[:], in1=arg[:], op=ALU.add), reads=[m1, arg], writes=[m1])
                        k.op("dve", lambda e: e.tensor_scalar(out=arg[:], in0=arg[:], scalar1=-math.pi, scalar2=TWO_PI,
                                                              op0=ALU.is_lt, op1=ALU.mult), reads=[arg], writes=[arg])
                        k.op("dve", lambda e: e.tensor_tensor(out=arg[:], in0=m1[:], in1=arg[:], op=ALU.add), reads=[m1, arg], writes=[arg])
                        k.op("act", lambda e: e.activation(out=dst[:, b0:b0 + LB], in_=arg[:], func=AF.Sin), reads=[arg], writes=[dst])
                a2b = self.sb(ls, "hf_a2b", [64, L + 1], BF16)
                w3b = self.sb(ls, "hf_w3b", [64, 4096], BF16)
                k.op("dve", lambda e: e.tensor_copy(out=a2b[:], in_=a2T[:]), reads=[a2T], writes=[a2b])
                k.op("act", lambda e: e.copy(out=w3b[:], in_=w3[:]), reads=[w3], writes=[w3b])
                dl = self.sb(ls, "hf_dl", [128, 1024], F32)
                tl = self.sb(ls, "hf_tl", [128, NLT, 2], F32)
                k.dma("sp", lambda e: e.dma_start(out=dl[:], in_=cst["delta"][0:1, :].partition_broadcast(128)), writes=[dl])
                k.dma("sp", lambda e: e.dma_start(out=tl[:], in_=cst["tl"]), writes=[tl])
                dec = [[self.sb(ls, f"hf_dec{i}{j}", [128, 1024], F32) for j in range(2)] for i in range(2)]
                fbb = [[self.sb(ls, f"hf_f{i}{j}", [128, 1024], F32) for j in range(2)] for i in range(2)]
                abb = [[self.sb(ls, f"hf_a{i}{j}", [128, 1024], BF16) for j in range(2)] for i in range(2)]
                kpm = [self.sb(ls, f"hf_kpm{i}", [128, 1024], BF16) for i in range(4)]
                Sps = self.ps[3]
                nps = 0
                pend = []

                def flush_one():
                    a_s, lt_s = pend.pop(0)
                    for h in range(2):
                        k.op("pe", lambda e: e.matmul(Sps[:, h * 512:(h + 1) * 512], lhsT=self.ones_b[:], rhs=a_s[:, h * 512:(h + 1) * 512],
                                                      start=(lt_s == 0), stop=(lt_s == NLT - 1)), reads=[self.ones_b, a_s], writes=[Sps])

                for o in range(2):
                    for lt in range(NLT):
                        par = lt % 2
                        for dr in (0, 1):
                            p = self.ps[nps % 3]
                            nps += 1
                            c0 = dr * 2048 + o * 1024
                            for h in range(2):
                                k.op("pe", lambda e: e.matmul(p[:, h * 512:(h + 1) * 512], lhsT=a2b[:, lt * 128 + dr:lt * 128 + dr + 128],
                                                              rhs=w3b[:, c0 + h * 512:c0 + (h + 1) * 512], start=True, stop=True),
                                     reads=[a2b, w3b], writes=[p])
                            d_, f_, a_ = dec[par][dr], fbb[par][dr], abb[par][dr]
                            k.op("act", lambda e: e.activation(out=d_[:], in_=dl[:], func=AF.Exp, scale=tl[:, lt, dr:dr + 1]),
                                 reads=[dl, tl], writes=[d_])
                            k.op("dve", lambda e: e.tensor_tensor(out=f_[:], in0=p[:], in1=d_[:], op=ALU.mult),
                                 reads=[p, d_], writes=[f_])
                            k.op("act", lambda e: e.activation(out=a_[:], in_=f_[:], func=AF.Abs), reads=[f_], writes=[a_])
                        f0, f1, a0, a1 = fbb[par][0], fbb[par][1], abb[par][0], abb[par][1]
                        kp_, km_ = kpm[par * 2], kpm[par * 2 + 1]
                        k.op("dve", lambda e: e.tensor_tensor(out=kp_[:], in0=f0[:], in1=f1[:], op=ALU.add), reads=[f0, f1], writes=[kp_])
                        k.op("dve", lambda e: e.tensor_tensor(out=km_[:], in0=f0[:], in1=f1[:], op=ALU.subtract), reads=[f0, f1], writes=[km_])
                        k.dma("sp", lambda e: e.dma_start(out=scr["kp"][o, lt * 128:(lt + 1) * 128, :], in_=kp_[:]), reads=[kp_])
                        k.dma("sp", lambda e: e.dma_start(out=scr["km"][o, lt * 128:(lt + 1) * 128, :], in_=km_[:]), reads=[km_])
                        k.op("pool", lambda e: e.tensor_tensor(out=a0[:], in0=a0[:], in1=a1[:], op=ALU.add), reads=[a0, a1], writes=[a0])
                        pend.append((a0, lt))
                        if len(pend) > 1:
                            flush_one()
                    while pend:
                        flush_one()
                    k.op("dve", lambda e: e.reciprocal(out=rS[o][:], in_=Sps[:]), reads=[Sps], writes=[rS[o]])
                self.k.barrier()
            if fft is not None:
                with ExitStack() as ls:
                    W1 = self.sb(ls, "hf_W1", [64, 128], BF16)
                    Tf = self.sb(ls, "hf_Tf", [128, 4, 64, 128], BF16)
                    sk = self.sb(ls, "hf_skf", [128, 1024], F32)
                    k.dma("sp", lambda e: e.dma_start(out=W1[:], in_=fft["W1"]), writes=[W1])
                    k.dma("sp", lambda e: e.dma_start(out=Tf[:], in_=fft["Tf"]), writes=[Tf])
                    for o in range(2):
                        k.dma("sp", lambda e: e.dma_start(out=sk[:], in_=wts["skip"][o:o + 1, :].partition_broadcast(128)), writes=[sk])
                        self.fft_stage1(scr["kp"][o], fft["Gp"], W1)
                        self.fft_stage1(scr["km"][o], fft["Gm"], W1)
                        self.fft_filter_stage2(fft["Gp"], fft["Gm"], Tf, rS[o], sk, fft["KA"][o], fft["KB"][o])
                    self.k.barrier()
                return
            with ExitStack() as ls:
                src = self.sb(ls, "hf_src", [128, NLT, 1024], BF16)
                mb = [self.sb(ls, f"hf_mb{i}", [128, NLT, 128], BF16) for i in range(2)]
                ph = self.sb(ls, "hf_ph", [128, NFT, 2], F32)
                sk = self.sb(ls, "hf_sk", [128, 1024], F32)
                At = [self.sb(ls, f"hf_At{i}", [128, 1024], F32) for i in range(2)]
                t1 = self.sb(ls, "hf_t1", [128, 1024], F32)
                t2 = self.sb(ls, "hf_t2", [128, 1024], F32)
                Ko = [self.sb(ls, f"hf_Ko{i}", [128, 1024], BF16) for i in range(4)]
                k.dma("sp", lambda e: e.dma_start(out=ph[:], in_=cst["ph"]), writes=[ph])
                for o in range(2):
                    k.dma("sp", lambda e: e.dma_start(out=sk[:], in_=wts["skip"][o:o + 1, :].partition_broadcast(128)), writes=[sk])
                    for pas in range(2):
                        sd = scr["kp"] if pas == 0 else scr["km"]
                        k.dma("sp", lambda e: e.dma_start(out=src[:], in_=sd[o].rearrange("(c p) n -> p c n", p=128)), writes=[src])
                        for ft in range(NFT):
                            m = mb[ft % 2]
                            k.dma("sp", lambda e: e.dma_start(out=m[:], in_=cst["Mb"][pas, ft]), writes=[m])
                            p = self.psum()
                            for h in range(2):
                                for c in range(NLT):
                                    k.op("pe", lambda e: e.matmul(p[:, h * 512:(h + 1) * 512], lhsT=m[:, c, :], rhs=src[:, c, h * 512:(h + 1) * 512],
                                                                  start=(c == 0), stop=(c == NLT - 1)), reads=[m, src], writes=[p])
                            rows = slice(ft * 128, (ft + 1) * 128)
                            if pas == 0:
                                a_ = At[ft % 2]
                                k.op("act", lambda e: e.copy(out=a_[:], in_=p[:]), reads=[p], writes=[a_])
                                k.dma("sp", lambda e: e.dma_start(out=scr["A"][rows, :], in_=a_[:]), reads=[a_])
                            else:
                                a_ = At[ft % 2]
                                kc_, ks_ = Ko[(ft % 2) * 2], Ko[(ft % 2) * 2 + 1]
                                cph, sph = ph[:, ft, 0:1], ph[:, ft, 1:2]
                                k.dma("sp", lambda e: e.dma_start(out=a_[:], in_=scr["A"][rows, :]), writes=[a_])
                                k.op("act", lambda e: e.activation(out=t1[:], in_=a_[:], func=AF.Copy, scale=cph),
                                     reads=[a_, ph], writes=[t1])
                                k.op("dve", lambda e: e.scalar_tensor_tensor(out=t1[:], in0=p[:], scalar=sph, in1=t1[:], op0=ALU.mult, op1=ALU.add),
                                     reads=[p, ph, t1], writes=[t1])
                                k.op("pool", lambda e: e.tensor_tensor(out=t1[:], in0=t1[:], in1=rS[o][:], op=ALU.mult), reads=[t1, rS[o]], writes=[t1])
                                k.op("pool", lambda e: e.tensor_tensor(out=kc_[:], in0=t1[:], in1=sk[:], op=ALU.add), reads=[t1, sk], writes=[kc_])
                                k.op("act", lambda e: e.activation(out=t2[:], in_=a_[:], func=AF.Copy, scale=sph),
                                     reads=[a_, ph], writes=[t2])
                                k.op("dve", lambda e: e.scalar_tensor_tensor(out=t2[:], in0=p[:], scalar=cph, in1=t2[:], op0=ALU.mult, op1=ALU.subtract),
                                     reads=[p, ph, t2], writes=[t2])
                                k.op("dve", lambda e: e.tensor_tensor(out=ks_[:], in0=t2[:], in1=rS[o][:], op=ALU.mult), reads=[t2, rS[o]], writes=[ks_])
                                k.dma("sp", lambda e: e.dma_start(out=scr["K"][o, 0, rows, :], in_=kc_[:]), reads=[kc_])
                                k.dma("sp", lambda e: e.dma_start(out=scr["K"][o, 1, rows, :], in_=ks_[:]), reads=[ks_])
                        self.k.barrier()
                self.k.barrier()
            self.k.barrier()

    def hy_proj(self, hT, T, w_in, cw_pm, src, g_d, z_d=None):
        k = self.k
        TB = min(512, T)
        NLT = T // 128
        with ExitStack() as ls:
            cw = self.sb(ls, "hp_cw", [128, 24, 3], F32)
            k.dma("sp", lambda e: e.dma_start(out=cw[:], in_=cw_pm), writes=[cw])
            wq = [self.sb(ls, f"hp_w{i}", [128, 8, 128], BF16) for i in range(2)]
            u = self.sb(ls, "hp_u", [128, T + 2], F32)
            acc = self.sb(ls, "hp_acc", [128, T], F32)
            row = self.sb(ls, "hp_row", [128, T], BF16)
            k.op("dve", lambda e: e.memset(u[:, 0:1], 0.0), writes=[u])
            k.op("dve", lambda e: e.memset(u[:, T + 1:T + 2], 0.0), writes=[u])
            n = 0
            for q in (1, 2, 0):
                for j in range(8):
                    w = wq[n % 2]
                    n += 1
                    self.load_w_bf(w, w_in, q * 1024 + j * 128, 128)
                    for tb in range(T // TB):
                        p = self.psum()
                        for c in range(8):
                            k.op("pe", lambda e: e.matmul(p[:, 0:TB], lhsT=w[:, c, :], rhs=hT[:, c, tb * TB:(tb + 1) * TB],
                                                          start=(c == 0), stop=(c == 7)), reads=[w, hT], writes=[p])
                        k.op("act", lambda e: e.copy(out=u[:, 1 + tb * TB:1 + (tb + 1) * TB], in_=p[:, 0:TB]), reads=[p], writes=[u])
                    ci = q * 8 + j
                    k.op("dve", lambda e: e.tensor_scalar(out=acc[:], in0=u[:, 0:T], scalar1=cw[:, ci, 0:1], scalar2=None, op0=ALU.mult),
                         reads=[u, cw], writes=[acc])
                    k.op("dve", lambda e: e.scalar_tensor_tensor(out=acc[:], in0=u[:, 1:T + 1], scalar=cw[:, ci, 1:2], in1=acc[:],
                                                                 op0=ALU.mult, op1=ALU.add), reads=[u, cw, acc], writes=[acc])
                    k.op("dve", lambda e: e.scalar_tensor_tensor(out=row[:], in0=u[:, 2:T + 2], scalar=cw[:, ci, 2:3], in1=acc[:],
                                                                 op0=ALU.mult, op1=ALU.add), reads=[u, cw, acc], writes=[row])
                    for l0 in range(0, NLT, 8):
                        nl = min(8, NLT - l0)
                        p = self.psum()
                        pv = pbf(p)[:, 0:1024].rearrange("p (c t) -> p c t", c=8)
                        for i in range(nl):
                            lt = l0 + i
                            k.op("pe", lambda e: e.transpose(pv[:, i, :], row[:, lt * 128:(lt + 1) * 128], self.identb[:]),
                                 reads=[row, self.identb], writes=[p])
                        k.op("act", lambda e: e.copy(out=src[:, l0:l0 + nl, j * 128:(j + 1) * 128], in_=pv[:, 0:nl, :]), reads=[p], writes=[src])
                if q != 0:
                    k.dma("sp", lambda e: e.dma_start(out=g_d[q - 1].rearrange("(c p) n -> p c n", p=128), in_=src[:]), reads=[src])
                elif z_d is not None:
                    k.dma("sp", lambda e: e.dma_start(out=z_d.rearrange("(c p) n -> p c n", p=128), in_=src[:]), reads=[src])
            self.k.barrier()

    def hy_conv(self, L, src, cst, K_d, g_d, Y_d, z_d, o):
        k = self.k
        NLT = L // 128
        NFT = NLT
        with ExitStack() as ls:
            mb = [self.sb(ls, f"hc_mb{i}", [128, NLT, 128], BF16) for i in range(4)]
            Kt = [self.sb(ls, f"hc_K{i}", [128, 1024], BF16) for i in range(4)]
            tt_ = [self.sb(ls, f"hc_t{i}", [128, 1024], F32) for i in range(4)]
            Yo = [self.sb(ls, f"hc_Y{i}", [128, 1024], BF16) for i in range(4)]
            for ft in range(NFT):
                mc, ms = mb[(ft % 2) * 2], mb[(ft % 2) * 2 + 1]
                kc_, ks_ = Kt[(ft % 2) * 2], Kt[(ft % 2) * 2 + 1]
                rows = slice(ft * 128, (ft + 1) * 128)
                k.dma("sp", lambda e: e.dma_start(out=mc[:], in_=cst["Mb"][0, ft]), writes=[mc])
                k.dma("sp", lambda e: e.dma_start(out=ms[:], in_=cst["Mb"][1, ft]), writes=[ms])
                k.dma("sp", lambda e: e.dma_start(out=kc_[:], in_=K_d[o, 0, rows, :]), writes=[kc_])
                k.dma("sp", lambda e: e.dma_start(out=ks_[:], in_=K_d[o, 1, rows, :]), writes=[ks_])
                pc, ps_ = self.psum(), self.psum()
                for (p, m) in ((pc, mc), (ps_, ms)):
                    for h in range(2):
                        for c in range(NLT):
                            k.op("pe", lambda e: e.matmul(p[:, h * 512:(h + 1) * 512], lhsT=m[:, c, :], rhs=src[:, c, h * 512:(h + 1) * 512],
                                                          start=(c == 0), stop=(c == NLT - 1)), reads=[m, src], writes=[p])
                yc, ys = Yo[(ft % 2) * 2], Yo[(ft % 2) * 2 + 1]
                t1, t2, t3, t4 = tt_
                k.op("dve", lambda e: e.tensor_tensor(out=t1[:], in0=pc[:], in1=kc_[:], op=ALU.mult), reads=[pc, kc_], writes=[t1])
                k.op("dve", lambda e: e.tensor_tensor(out=t2[:], in0=ps_[:], in1=ks_[:], op=ALU.mult), reads=[ps_, ks_], writes=[t2])
                k.op("pool", lambda e: e.tensor_tensor(out=yc[:], in0=t1[:], in1=t2[:], op=ALU.subtract), reads=[t1, t2], writes=[yc])
                k.op("dve", lambda e: e.tensor_tensor(out=t3[:], in0=pc[:], in1=ks_[:], op=ALU.mult), reads=[pc, ks_], writes=[t3])
                k.op("dve", lambda e: e.tensor_tensor(out=t4[:], in0=ps_[:], in1=kc_[:], op=ALU.mult), reads=[ps_, kc_], writes=[t4])
                k.op("pool", lambda e: e.tensor_tensor(out=ys[:], in0=t3[:], in1=t4[:], op=ALU.add), reads=[t3, t4], writes=[ys])
                k.dma("sp", lambda e: e.dma_start(out=Y_d[0, rows, :], in_=yc[:]), reads=[yc])
                k.dma("sp", lambda e: e.dma_start(out=Y_d[1, rows, :], in_=ys[:]), reads=[ys])
            self.k.barrier()
            srcv = src.t[:].rearrange("p c n -> p (c n)")
            Ych = srcv[:, 0:NFT * 512].rearrange("p (c n) -> p c n", n=512)
            Ysh = srcv[:, NFT * 512:2 * NFT * 512].rearrange("p (c n) -> p c n", n=512)
            gt = [self.sb(ls, f"hc_g{i}", [128, 512], BF16) for i in range(2)]
            zo = [self.sb(ls, f"hc_z{i}", [128, 512], BF16) for i in range(2)]
            for hh in range(2):
                cs = slice(hh * 512, (hh + 1) * 512)
                k.dma("sp", lambda e: e.dma_start(out=Ych, in_=Y_d[0].rearrange("(c p) n -> p c n", p=128)[:, :, cs]), writes=[src])
                k.dma("sp", lambda e: e.dma_start(out=Ysh, in_=Y_d[1].rearrange("(c p) n -> p c n", p=128)[:, :, cs]), writes=[src])
                for tt in range(NLT):
                    mc, ms = mb[(tt % 2) * 2], mb[(tt % 2) * 2 + 1]
                    rows = slice(tt * 128, (tt + 1) * 128)
                    g_ = gt[tt % 2]
                    z_ = zo[tt % 2]
                    k.dma("sp", lambda e: e.dma_start(out=mc[:], in_=cst["Mb"][0, tt]), writes=[mc])
                    k.dma("sp", lambda e: e.dma_start(out=ms[:], in_=cst["Mb"][1, tt]), writes=[ms])
                    k.dma("sp", lambda e: e.dma_start(out=g_[:], in_=g_d[o, rows, cs]), writes=[g_])
                    p = self.psum()
                    for (m, Yh, first) in ((mc, Ych, True), (ms, Ysh, False)):
                        for c in range(NFT):
                            k.op("pe", lambda e: e.matmul(p[:, 0:512], lhsT=m[:, c, :], rhs=Yh[:, c, :],
                                                          start=(first and c == 0), stop=((not first) and c == NFT - 1)),
                                 reads=[m, src], writes=[p])
                    k.op("dve", lambda e: e.scalar_tensor_tensor(out=z_[:], in0=p[:, 0:512], scalar=1.0 / L, in1=g_[:],
                                                                 op0=ALU.mult, op1=ALU.mult), reads=[p, g_], writes=[z_])
                    k.dma("sp", lambda e: e.dma_start(out=z_d[rows, cs], in_=z_[:]), reads=[z_])
                self.k.barrier()
            self.k.barrier()


def hy_consts(L, with_M=True):
    import ml_dtypes
    N = 2 * L
    f32 = np.float32
    t = np.linspace(0.0, 1.0, L, dtype=f32)[:, None]
    bands = np.linspace(1e-4, 15.0, 16, dtype=f32)[None, :]
    ang = f32(2.0 * math.pi / L) * np.arange(L, dtype=f32)[:, None] * bands
    z = np.concatenate([t, np.cos(ang), -np.sin(ang)], axis=-1).astype(f32)
    zT = np.ascontiguousarray(z.T)
    NT_ = L // 128
    fi = np.arange(L, dtype=np.float64)
    th = 2.0 * np.pi * (fi + 0.5) / N
    angm = th[:, None] * (fi[None, :] + 0.5)
    Mb = np.empty((2, NT_, 128, NT_, 128) if with_M else (1,), dtype=ml_dtypes.bfloat16)
    for i, fn in enumerate((np.cos, np.sin) if with_M else ()):
        M = fn(angm).astype(f32)
        Mb[i] = M.reshape(NT_, 128, NT_, 128).transpose(0, 3, 2, 1).astype(ml_dtypes.bfloat16)
    ph = np.stack([np.cos(th / 2), np.sin(th / 2)], -1).astype(f32)
    ph = np.ascontiguousarray(ph.reshape(NT_, 128, 2).transpose(1, 0, 2))
    tl_ = np.linspace(0.0, 1.0, L, dtype=f32).astype(np.float64)
    tl1 = np.concatenate([tl_[1:], [L / (L - 1.0)]])
    tl = np.stack([-tl_, -tl1], -1).astype(f32)
    tl = np.ascontiguousarray(tl.reshape(NT_, 128, 2).transpose(1, 0, 2))
    delta = np.abs(np.linspace(math.log(1e-2) / 1.5, math.log(1e-2) / 0.3, D, dtype=f32)).reshape(1, D).astype(f32)
    return dict(zT=zT, Mb=Mb, ph=ph, tl=tl, delta=delta)


DEPTH = 4
GRID_W = 64


def rope_tables_np(T):
    f32 = np.float32
    rows = T // GRID_W
    row = np.repeat(np.arange(rows, dtype=f32), GRID_W)
    col = np.tile(np.arange(GRID_W, dtype=f32), rows)
    inv = (f32(10000.0) ** (-np.arange(0, 64, 2, dtype=f32) / f32(64))).astype(f32)

    def axis_angles(pos):
        a = pos[:, None] * inv[None, :]
        return np.concatenate([a, a], axis=-1)

    ang = np.concatenate([axis_angles(row), axis_angles(col)], axis=-1).astype(f32)
    return np.ascontiguousarray(np.stack([np.cos(ang).T, np.sin(ang).T]).astype(f32))


def rot_T_np():
    R = np.zeros((128, 128), np.float32)
    for base in (0, 64):
        for i in range(32):
            R[base + i, base + i + 32] = -1.0
            R[base + 32 + i, base + i] = 1.0
    return np.ascontiguousarray(R.T)


def build_model(T, LC, depth=DEPTH):
    nc = bass.Bass("TRN2", target_bir_lowering=False)

    def dt(n, s, d=F32, kind="ExternalInput"):
        return nc.dram_tensor(n, s, d, kind=kind).ap()

    nA = len(range(0, depth, 3))
    nB = len(range(1, depth, 3))
    nC = len(range(2, depth, 3))
    last_attn = max(list(range(2, depth, 3)), default=-1)
    I = {}
    I["x"] = dt("x", [T, D]); I["ctx"] = dt("ctx", [LC, D]); I["c_pm"] = dt("c_pm", [128, 8]); I["cc_pm"] = dt("cc_pm", [128, 8])
    I["ada_w"] = dt("ada_w", [depth, D, 6 * D]); I["ada_b"] = dt("ada_b", [depth, 6 * D]); I["norm_g"] = dt("norm_g", [depth, 2, D])
    I["sc_w_in"] = dt("sc_w_in", [nA, D, 3 * D]); I["sc_cw_pm"] = dt("sc_cw_pm", [nA, 128, 8, 3]); I["sc_w_out"] = dt("sc_w_out", [nA, D, D])
    if nB:
        I["hy_w_in"] = dt("hy_w_in", [nB, D, 3 * D]); I["hy_cw_pm"] = dt("hy_cw_pm", [nB, 128, 24, 3])
        I["hy_f_w1"] = dt("hy_f_w1", [nB, 33, 64]); I["hy_f_b1"] = dt("hy_f_b1", [nB, 64, 1]); I["hy_f_w2"] = dt("hy_f_w2", [nB, 64, 64])
        I["hy_f_b2"] = dt("hy_f_b2", [nB, 64, 1]); I["hy_f_w3"] = dt("hy_f_w3", [nB, 64, 4 * D]); I["hy_sin_freq"] = dt("hy_sin_freq", [nB, 64, 1])
        I["hy_skip"] = dt("hy_skip", [nB, 2, D]); I["hy_w_out"] = dt("hy_w_out", [nB, D, D])
        cst = {}
        use_fft = (T == 4096)
        for tag, L in (("l", T), ("c", LC)):
            n_ = L // 128
            cst[tag] = dict(zT=dt(f"hz_{tag}", [33, L]), ph=dt(f"hp_{tag}", [128, n_, 2]),
                            tl=dt(f"ht_{tag}", [128, n_, 2]), delta=dt(f"hd_{tag}", [1, D]))
            if not (use_fft and tag == "l"):
                cst[tag]["Mb"] = dt(f"hM_{tag}", [2, n_, 128, n_, 128], BF16)
        if use_fft:
            fc = dict(W1=dt("fW1", [64, 128], BF16), W4=dt("fW4", [128, 64], BF16), Td=dt("fTd", [128, 3, 64, 128], BF16),
                      Tf=dt("fTf", [128, 4, 64, 128], BF16))
    if nC:
        I["at_w_qkv"] = dt("at_w_qkv", [nC, D, 1536]); I["at_qg"] = dt("at_qg", [nC, 128, 1]); I["at_kg"] = dt("at_kg", [nC, 128, 1])
        I["at_w_o"] = dt("at_w_o", [nC, D, D]); I["rope"] = dt("rope", [2, 128, T]); I["rotT"] = dt("rotT", [128, 128])
    I["moe_router"] = dt("moe_router", [depth, D, NE]); I["moe_w_gate"] = dt("moe_w_gate", [depth, NE, D, D])
    I["moe_w_up"] = dt("moe_w_up", [depth, NE, D, D]); I["moe_w_down"] = dt("moe_w_down", [depth, NE, D, D])
    out = dt("out", [T, D], kind="ExternalOutput")
    cxs = dt("s_cxs", [LC, D], F32, "Internal")
    zT_d = dt("s_zT", [D, T], BF16, "Internal")
    h2_d = dt("s_h2", [T, D], BF16, "Internal")
    ztok_d = dt("s_ztok", [T, D], BF16, "Internal")
    rows_d = dt("s_adarows", [depth, 2, 6 * D], F32, "Internal")
    h2c_d = dt("s_h2c", [LC, D], BF16, "Internal")
    if nB:
        hs = dict(kp=dt("s_kp", [2, T, D], BF16, "Internal"), km=dt("s_km", [2, T, D], BF16, "Internal"), A=dt("s_A", [T, D], F32, "Internal"))
        Kl = dt("s_Kl", [2, 2, T, D], BF16, "Internal"); Kc_ = dt("s_Kc", [2, 2, LC, D], BF16, "Internal")
        g_d = dt("s_g", [2, T, D], BF16, "Internal"); Y_d = dt("s_Y", [2, T, D], BF16, "Internal")
        z1_d = dt("s_z1", [T, D], BF16, "Internal"); z2_d = dt("s_z2", [T, D], BF16, "Internal")
        if use_fft:
            Gp = dt("s_Gp", [2, 64, 64, D], BF16, "Internal"); Gm = dt("s_Gm", [2, 64, 64, D], BF16, "Internal")
            Zd = dt("s_Zd", [2, 64, 64, D], BF16, "Internal")
            KA = dt("s_KA", [2, 64, 128, D], BF16, "Internal"); KB = dt("s_KB", [2, 64, 128, D], BF16, "Internal")
            zt_d = dt("s_zt", [T, D], BF16, "Internal")

    P = HyFFT(nc, T, LC)
    with ExitStack() as st:
        P.setup_small(st)
        rep = {"l": 0, "c": 1}
        P.ada_precompute(I["c_pm"], I["cc_pm"], I["ada_w"], I["ada_b"], depth, rows_d)
        for i in range(depth):
            kind, j = i % 3, i // 3
            need_ctx, upd_ctx = i <= last_attn, i < last_attn
            streams = []
            if upd_ctx:
                streams.append(("c", LC, I["ctx"] if i == 0 else cxs, cxs))
            streams.append(("l", T, I["x"] if i == 0 else out, out))
            aw, ab, ng = I["ada_w"][i], I["ada_b"][i:i + 1, :], I["norm_g"][i]
            if kind == 1:
                wts = dict(w1=I["hy_f_w1"][j], b1=I["hy_f_b1"][j], w2=I["hy_f_w2"][j], b2=I["hy_f_b2"][j], w3=I["hy_f_w3"][j],
                           freq=I["hy_sin_freq"][j], skip=I["hy_skip"][j])
                Ks = {}
                for (tag, L, _, _) in streams:
                    scr = dict(kp=hs["kp"][:, 0:L, :], km=hs["km"][:, 0:L, :], A=hs["A"][0:L, :], K=(Kl if tag == "l" else Kc_))
                    if use_fft and tag == "l":
                        P.hy_filter(L, cst[tag], wts, scr, fft=dict(W1=fc["W1"], Tf=fc["Tf"], Gp=Gp, Gm=Gm, KA=KA, KB=KB))
                    else:
                        P.hy_filter(L, cst[tag], wts, scr)
                    Ks[tag] = scr["K"]
            lay = ExitStack()
            pers = {}
            for (tag, L, _, _) in streams:
                cap_ = 2 * L // NE
                njt_ = cap_ // min(cap_, 128)
                pers[tag] = dict(idx=P.sb(lay, f"idx{tag}", [128, NE * njt_], I32), gsel=P.sb(lay, f"gs{tag}", [128, NE * njt_], F32))
                if tag != streams[-1][0]:
                    pers[tag]["G2"] = P.sb(lay, f"G2{tag}", [128, 1024], F32)
            for (tag, L, src_ap, dst_ap) in streams:
                with ExitStack() as s1:
                    A1, B1, G1, A2, B2, G2 = P.ada_load(s1, rows_d[i, rep[tag]:rep[tag] + 1, :], ng, tag)
                    if kind == 0:
                        with ExitStack() as s2:
                            hT = P.phase_a(s2, src_ap, L, A1, B1, tag)
                            P.phase_b_conv(hT, L, I["sc_w_in"][j], I["sc_cw_pm"][j], zT_d[:, 0:L], tag)
                            P.end_phase(s2)
                        zsrc, wo, ztok = zT_d[:, 0:L], I["sc_w_out"][j], False
                    elif kind == 1 and use_fft and tag == "l":
                        n_ = L // 128
                        with ExitStack() as s2:
                            src = P.sb(s2, "hy_src", [128, n_, 1024], BF16)
                            with ExitStack() as s2b:
                                hT = P.phase_a(s2b, src_ap, L, A1, B1, tag)
                                P.hy_proj(hT, L, I["hy_w_in"][j], I["hy_cw_pm"][j], src, g_d[:, 0:L, :], z_d=zt_d)
                                P.end_phase(s2b)
                            P.end_phase(s2)
                        with ExitStack() as s2:
                            W1 = P.sb(s2, "fW1", [64, 128], BF16); W4 = P.sb(s2, "fW4", [128, 64], BF16)
                            Td = P.sb(s2, "fTd", [128, 3, 64, 128], BF16)
                            P.k.dma("sp", lambda e: e.dma_start(out=W1[:], in_=fc["W1"]), writes=[W1])
                            P.k.dma("sp", lambda e: e.dma_start(out=W4[:], in_=fc["W4"]), writes=[W4])
                            P.k.dma("sp", lambda e: e.dma_start(out=Td[:], in_=fc["Td"]), writes=[Td])
                            P.fft_conv(zt_d, Gp, Zd, W1, W4, Td, KA[0], KB[0], g_d[0], z1_d)
                            P.fft_conv(z1_d, Gp, Zd, W1, W4, Td, KA[1], KB[1], g_d[1], z2_d)
                            P.end_phase(s2)
                        zsrc, wo, ztok = z2_d[0:L, :], I["hy_w_out"][j], True
                    elif kind == 1:
                        n_ = L // 128
                        with ExitStack() as s2:
                            src = P.sb(s2, "hy_src", [128, n_, 1024], BF16)
                            with ExitStack() as s2b:
                                hT = P.phase_a(s2b, src_ap, L, A1, B1, tag)
                                P.hy_proj(hT, L, I["hy_w_in"][j], I["hy_cw_pm"][j], src, g_d[:, 0:L, :])
                                P.end_phase(s2b)
                            P.hy_conv(L, src, cst[tag], Ks[tag], g_d[:, 0:L, :], Y_d[:, 0:L, :], z1_d[0:L, :], 0)
                            P.k.dma("sp", lambda e: e.dma_start(out=src[:], in_=z1_d[0:L, :].rearrange("(c p) n -> p c n", p=128)), writes=[src])
                            P.hy_conv(L, src, cst[tag], Ks[tag], g_d[:, 0:L, :], Y_d[:, 0:L, :], z2_d[0:L, :], 1)
                            P.end_phase(s2)
                        zsrc, wo, ztok = z2_d[0:L, :], I["hy_w_out"][j], True
                    else:
                        with ExitStack() as s2:
                            hcT = P.sb(s2, "hcT", [128, 8, LC], BF16)
                            with ExitStack() as sc:
                                cA1, cB1, *_ = P.ada_load(sc, rows_d[i, 1:2, :], ng, "c")
                                P.phase_a(sc, I["ctx"] if i == 0 else cxs, LC, cA1, cB1, "c", hT=hcT)
                                P.end_phase(sc)
                            hT = P.phase_a(s2, src_ap, L, A1, B1, tag)
                            P.phase_b_attn(hT, hcT, L, LC, I["at_w_qkv"][j], I["at_qg"][j], I["at_kg"][j], I["rope"], I["rotT"], ztok_d)
                            P.end_phase(s2)
                        zsrc, wo, ztok = ztok_d, I["at_w_o"][j], True
                    h2x = (h2_d if tag == "l" else h2c_d)[0:L, :]
                    with ExitStack() as s3:
                        affT = P.phase_c(L, zsrc, wo, src_ap, dst_ap, G1, A2, B2, I["moe_router"][i], h2x, s3, tag, z_tok=ztok)
                        P.moe_route(L, affT, pers[tag]["idx"], pers[tag]["gsel"])
                        P.end_phase(s3)
                    pers[tag].update(T=L, h2_d=h2x, lat_d=dst_ap)
                    if tag != streams[-1][0]:
                        P.k.op("dve", lambda e: e.tensor_copy(out=pers[tag]["G2"][:], in_=G2[:]), reads=[G2], writes=[pers[tag]["G2"]])
                    else:
                        pers[tag]["G2"] = G2
                        P.moe_experts([pers[t_] for (t_, _, _, _) in streams], I["moe_w_gate"][i], I["moe_w_up"][i], I["moe_w_down"][i])
                    P.end_phase(s1)
            P.end_phase(lay)
    P.finish()
    return nc


def host_inputs(inp, b, T, LC, depth=DEPTH):
    f = lambda a: np.ascontiguousarray(np.asarray(a, dtype=np.float32))
    pm = lambda v: np.ascontiguousarray(np.asarray(v, np.float32).reshape(8, 128).T)
    m = {}
    m["x"] = f(inp["x"][b]); m["ctx"] = f(inp["ctx"][b]); m["c_pm"] = pm(inp["c"][b]); m["cc_pm"] = pm(inp["c_ctx"])
    for n in ("ada_w", "ada_b", "norm_g", "sc_w_in", "sc_w_out", "moe_router", "moe_w_gate", "moe_w_up", "moe_w_down"):
        m[n] = f(inp[n])
    sc = np.asarray(inp["sc_conv"], np.float32)
    m["sc_cw_pm"] = np.ascontiguousarray(sc.reshape(sc.shape[0], 3, 8, 128).transpose(0, 3, 2, 1))
    if depth > 1:
        for n in ("hy_w_in", "hy_f_w1", "hy_f_w2", "hy_f_w3", "hy_skip", "hy_w_out"):
            m[n] = f(inp[n])
        hc_ = np.asarray(inp["hy_conv"], np.float32)
        m["hy_cw_pm"] = np.ascontiguousarray(hc_.reshape(hc_.shape[0], 3, 24, 128).transpose(0, 3, 2, 1))
        for n in ("hy_f_b1", "hy_f_b2", "hy_sin_freq"):
            a = np.asarray(inp[n], np.float32)
            m[n] = np.ascontiguousarray(a.reshape(a.shape[0], 64, 1))
    if depth > 2:
        m["at_w_qkv"] = f(inp["at_w_qkv"]); m["at_w_o"] = f(inp["at_w_o"])
        for n, s in (("at_qg", "at_q_g"), ("at_kg", "at_k_g")):
            a = np.asarray(inp[s], np.float32)
            m[n] = np.ascontiguousarray(a.reshape(a.shape[0], 128, 1))
    return m


_CONST_CACHE = {}


def const_inputs(T, LC, depth=DEPTH):
    key = (T, LC, depth)
    if key not in _CONST_CACHE:
        m = {}
        if depth > 1:
            for tag, L in (("l", T), ("c", LC)):
                fft_l = (T == 4096 and tag == "l")
                hc = hy_consts(L, with_M=not fft_l)
                m[f"hz_{tag}"] = hc["zT"]; m[f"hp_{tag}"] = hc["ph"]; m[f"ht_{tag}"] = hc["tl"]; m[f"hd_{tag}"] = hc["delta"]
                if not fft_l:
                    m[f"hM_{tag}"] = hc["Mb"]
            if T == 4096:
                m.update(hy_fft_consts())
        if depth > 2:
            m["rope"] = rope_tables_np(T); m["rotT"] = rot_T_np()
        _CONST_CACHE[key] = m
    return _CONST_CACHE[key]


def kernel(**inputs):
    B, T, _ = inputs["x"].shape
    LC = inputs["ctx"].shape[1]
    depth = inputs["ada_w"].shape[0]
    nc = build_model(T, LC, depth)
    cm = const_inputs(T, LC, depth)
    in_maps = []
    for b in range(B):
        m = host_inputs(inputs, b, T, LC, depth)
        m.update(cm)
        in_maps.append(m)
    res = run_bass_kernel_spmd(nc, in_maps, core_ids=list(range(B)))
    return np.stack([np.asarray(r["out"], dtype=np.float32) for r in res.results], axis=0)


def hy_fft_consts():
    import ml_dtypes
    L = 4096
    N = 2 * L
    a = np.arange(64)[:, None]
    fa = np.arange(64)[None, :]
    al = 2 * np.pi * (fa + 0.5) * a / 128.0
    W1 = np.concatenate([np.cos(al), -np.sin(al)], 1)
    W4 = np.concatenate([np.cos(al).T, -np.sin(al).T], 0) * (2.0 / N)
    b = np.arange(64)[:, None]
    fb = np.arange(32)[None, :]
    names = ("T2", "T2s", "T3", "MA1", "MA2", "MB1", "MB2")
    Ms = {n: np.zeros((64, 128, 128)) for n in names}
    for f_a in range(64):
        fD = f_a + 128 * fb
        fM = 127 - f_a + 128 * fb
        pD = 2 * np.pi * (fD + 0.5) * (b + 0.5) / N
        pM = 2 * np.pi * (fM + 0.5) * (b + 0.5) / N
        hD = np.pi * (fD + 0.5) / N
        hM = np.pi * (fM + 0.5) / N
        cD, sD, cM, sM = np.cos(pD), np.sin(pD), np.cos(pM), np.sin(pM)

        def put(name, blk_re, blk_im, ro, mi):
            c0 = ro * 64 + mi * 32
            Ms[name][f_a, 0:64, c0:c0 + 32] += blk_re
            Ms[name][f_a, 64:128, c0:c0 + 32] += blk_im

        ReD, ImD, ReM, ImM = (cD, sD), (-sD, cD), (cM, -sM), (-sM, -cM)
        neg = lambda t: (-t[0], -t[1])
        put("T2", *ReD, 0, 0); put("T2", *ImD, 1, 0); put("T2", *ReM, 0, 1); put("T2", *ImM, 1, 1)
        put("T2s", *neg(ImD), 0, 0); put("T2s", *ReD, 1, 0); put("T2s", *neg(ImM), 0, 1); put("T2s", *ReM, 1, 1)
        for ro in (0, 1):
            for (Re_, Im_, h_, mi) in ((ReD, ImD, hD, 0), (ReM, ImM, hM, 1)):
                ch, sh = np.cos(h_), np.sin(h_)
                put("MA1", Re_[0] * ch, Re_[1] * ch, ro, mi); put("MA2", -Im_[0] * sh, -Im_[1] * sh, ro, mi)
                put("MB1", Re_[0] * sh, Re_[1] * sh, ro, mi); put("MB2", Im_[0] * ch, Im_[1] * ch, ro, mi)

        def put3(ri, mi, blk, ro):
            r0 = ri * 64 + mi * 32
            Ms["T3"][f_a, r0:r0 + 32, ro * 64:ro * 64 + 64] += blk.T

        put3(0, 0, cD, 0); put3(1, 0, -sD, 0); put3(0, 1, cM, 0); put3(1, 1, -sM, 0)
        put3(0, 0, sD, 1); put3(1, 0, cD, 1); put3(0, 1, -sM, 1); put3(1, 1, -cM, 1)
    bf = lambda x: np.ascontiguousarray(x.astype(np.float32).astype(ml_dtypes.bfloat16))
    out = {"fW1": bf(W1), "fW4": bf(W4)}
    out["fTd"] = bf(np.stack([Ms["T2"], Ms["T2s"], Ms["T3"]], 0).transpose(2, 0, 1, 3))
    out["fTf"] = bf(np.stack([Ms["MA1"], Ms["MA2"], Ms["MB1"], Ms["MB2"]], 0).transpose(2, 0, 1, 3))
    return out


class HyFFT(Hyena):
    def fft_stage1(self, src_d, Gd, W1, ncols=1024):
        k = self.k
        BBS = 8
        xv = src_d.rearrange("(a b) c -> a b c", b=64)
        gv = Gd.rearrange("r f b c -> (r f) b c")
        with ExitStack() as ls:
            xt = [self.sb(ls, f"f1_x{i}", [64, BBS, ncols], BF16) for i in range(2)]
            gt = [self.sb(ls, f"f1_g{i}", [128, ncols], BF16) for i in range(6)]
            n = 0
            for bb in range(64 // BBS):
                x_ = xt[bb % 2]
                k.dma("sp", lambda e: e.dma_start(out=x_[:], in_=xv[:, bb * BBS:(bb + 1) * BBS, :]), writes=[x_])
                for bi in range(BBS):
                    b = bb * BBS + bi
                    p = self.psum()
                    for h in range(ncols // 512):
                        k.op("pe", lambda e: e.matmul(p[:, h * 512:(h + 1) * 512], lhsT=W1[:], rhs=x_[:, bi, h * 512:(h + 1) * 512],
                                                      start=True, stop=True), reads=[W1, x_], writes=[p])
                    g_ = gt[n % 6]
                    n += 1
                    k.op("act", lambda e: e.copy(out=g_[:], in_=p[:, 0:ncols]), reads=[p], writes=[g_])
                    k.dma("pool", lambda e: e.dma_start(out=gv[:, b, :], in_=g_[:]), reads=[g_])
            self.k.barrier()

    def _load_gblock(self, dst, Gd, fa0, nfa):
        for r in range(2):
            self.k.dma("sp", lambda e: e.dma_start(out=dst[r * 64:(r + 1) * 64, 0:nfa, :],
                                                   in_=Gd[r, fa0:fa0 + nfa, :, :].rearrange("f b c -> b f c")), writes=[dst])

    def fft_filter_stage2(self, Gp, Gm, Tf, rS, sk, KA_d, KB_d):
        k = self.k
        FBS = 4
        with ExitStack() as ls:
            gp = [self.sb(ls, f"f2_gp{i}", [128, FBS, 1024], BF16) for i in range(2)]
            gm = [self.sb(ls, f"f2_gm{i}", [128, FBS, 1024], BF16) for i in range(2)]
            tmp = [self.sb(ls, f"f2_t{i}", [128, 1024], F32) for i in range(2)]
            ko = [self.sb(ls, f"f2_k{i}", [128, 1024], BF16) for i in range(4)]
            for fb_ in range(64 // FBS):
                gp_, gm_ = gp[fb_ % 2], gm[fb_ % 2]
                self._load_gblock(gp_, Gp, fb_ * FBS, FBS)
                self._load_gblock(gm_, Gm, fb_ * FBS, FBS)
                for fi in range(FBS):
                    fa = fb_ * FBS + fi
                    pA, pB = self.psum(), self.psum()
                    for (p, m1, m2) in ((pA, 0, 1), (pB, 2, 3)):
                        for h in range(2):
                            hs = slice(h * 512, (h + 1) * 512)
                            k.op("pe", lambda e: e.matmul(p[:, hs], lhsT=Tf[:, m1, fa, :], rhs=gp_[:, fi, hs], start=True, stop=False),
                                 reads=[Tf, gp_], writes=[p])
                            k.op("pe", lambda e: e.matmul(p[:, hs], lhsT=Tf[:, m2, fa, :], rhs=gm_[:, fi, hs], start=False, stop=True),
                                 reads=[Tf, gm_], writes=[p])
                    t_ = tmp[fa % 2]
                    ka, kb = ko[(fa % 2) * 2], ko[(fa % 2) * 2 + 1]
                    k.op("dve", lambda e: e.tensor_tensor(out=t_[:], in0=pA[:], in1=rS[:], op=ALU.mult), reads=[pA, rS], writes=[t_])
                    k.op("pool", lambda e: e.tensor_tensor(out=ka[:], in0=t_[:], in1=sk[:], op=ALU.add), reads=[t_, sk], writes=[ka])
                    k.op("dve", lambda e: e.tensor_tensor(out=kb[:], in0=pB[:], in1=rS[:], op=ALU.mult), reads=[pB, rS], writes=[kb])
                    k.dma("pool", lambda e: e.dma_start(out=KA_d[fa], in_=ka[:]), reads=[ka])
                    k.dma("pool", lambda e: e.dma_start(out=KB_d[fa], in_=kb[:]), reads=[kb])
            self.k.barrier()

    def fft_conv(self, src_d, Gd, Zd, W1, W4, Td, KA_d, KB_d, gate_d, z_d):
        k = self.k
        self.fft_stage1(src_d, Gd, W1)
        FBS = 4
        with ExitStack() as ls:
            gb = [self.sb(ls, f"f3_g{i}", [128, FBS, 1024], BF16) for i in range(2)]
            kk = [self.sb(ls, f"f3_k{i}", [128, 1024], BF16) for i in range(4)]
            t1 = [self.sb(ls, f"f3_t1{i}", [128, 1024], F32) for i in range(2)]
            t2 = [self.sb(ls, f"f3_t2{i}", [128, 1024], F32) for i in range(2)]
            Y = [self.sb(ls, f"f3_Y{i}", [128, 1024], BF16) for i in range(2)]
            zt = [self.sb(ls, f"f3_z{i}", [128, 1024], BF16) for i in range(2)]
            pUs = {}

            def stX(fa):
                fb_, fi = fa // FBS, fa % FBS
                g_ = gb[fb_ % 2]
                if fi == 0:
                    self._load_gblock(g_, Gd, fb_ * FBS, FBS)
                ka, kb = kk[(fa % 2) * 2], kk[(fa % 2) * 2 + 1]
                k.dma("sp", lambda e: e.dma_start(out=ka[:], in_=KA_d[fa]), writes=[ka])
                k.dma("sp", lambda e: e.dma_start(out=kb[:], in_=KB_d[fa]), writes=[kb])
                pU, pS = self.psum(), self.psum()
                for (p, m) in ((pU, 0), (pS, 1)):
                    for h in range(2):
                        hs = slice(h * 512, (h + 1) * 512)
                        k.op("pe", lambda e: e.matmul(p[:, hs], lhsT=Td[:, m, fa, :], rhs=g_[:, fi, hs], start=True, stop=True),
                             reads=[Td, g_], writes=[p])
                a_, b_, y_ = t1[fa % 2], t2[fa % 2], Y[fa % 2]
                k.op("dve", lambda e: e.tensor_tensor(out=a_[:], in0=pU[:], in1=ka[:], op=ALU.mult), reads=[pU, ka], writes=[a_])
                k.op("dve", lambda e: e.tensor_tensor(out=b_[:], in0=pS[:], in1=kb[:], op=ALU.mult), reads=[pS, kb], writes=[b_])
                k.op("pool", lambda e: e.tensor_tensor(out=y_[:], in0=a_[:], in1=b_[:], op=ALU.add), reads=[a_, b_], writes=[y_])
                pUs[fa] = pU

            def stZ(fa):
                y_, z_ = Y[fa % 2], zt[fa % 2]
                pZ = pUs.pop(fa)
                for h in range(2):
                    hs = slice(h * 512, (h + 1) * 512)
                    k.op("pe", lambda e: e.matmul(pZ[:, hs], lhsT=Td[:, 2, fa, :], rhs=y_[:, hs], start=True, stop=True),
                         reads=[Td, y_], writes=[pZ])
                k.op("act", lambda e: e.copy(out=z_[:], in_=pZ[:]), reads=[pZ], writes=[z_])
                for r in range(2):
                    k.dma("pool", lambda e: e.dma_start(out=Zd[r, fa, :, :], in_=z_[r * 64:(r + 1) * 64, :]), reads=[z_])

            stX(0)
            for fa in range(64):
                if fa + 1 < 64:
                    stX(fa + 1)
                stZ(fa)
            self.k.barrier()
        BBS = 8
        with ExitStack() as ls:
            zz = [self.sb(ls, f"f4_z{i}", [128, BBS, 1024], BF16) for i in range(2)]
            gg = [self.sb(ls, f"f4_g{i}", [64, BBS, 1024], BF16) for i in range(2)]
            oo = [self.sb(ls, f"f4_o{i}", [64, BBS, 1024], BF16) for i in range(2)]
            zv = Zd.rearrange("r f b c -> (r f) b c")
            gv = gate_d.rearrange("(a b) c -> a b c", b=64)
            ov = z_d.rearrange("(a b) c -> a b c", b=64)
            for bb in range(64 // BBS):
                z_, g_, o_ = zz[bb % 2], gg[bb % 2], oo[bb % 2]
                bs = slice(bb * BBS, (bb + 1) * BBS)
                k.dma("sp", lambda e: e.dma_start(out=z_[:], in_=zv[:, bs, :]), writes=[z_])
                k.dma("sp", lambda e: e.dma_start(out=g_[:], in_=gv[:, bs, :]), writes=[g_])
                for bi in range(BBS):
                    p = self.psum()
                    for h in range(2):
                        hs = slice(h * 512, (h + 1) * 512)
                        k.op("pe", lambda e: e.matmul(p[0:64, hs], lhsT=W4[:], rhs=z_[:, bi, hs], start=True, stop=True),
                             reads=[W4, z_], writes=[p])
                    k.op("dve", lambda e: e.tensor_tensor(out=o_[:, bi, :], in0=p[0:64, :], in1=g_[:, bi, :], op=ALU.mult),
                         reads=[p, g_], writes=[o_])
                k.dma("pool", lambda e: e.dma_start(out=ov[:, bs, :], in_=o_[:]), reads=[o_])
            self.k.barrier()
```

```python
import math
from contextlib import ExitStack

import numpy as np
import concourse.bass as bass
import concourse.mybir as mybir
from concourse.bass_utils import run_bass_kernel_spmd

F32 = mybir.dt.float32
BF16 = mybir.dt.bfloat16
I32 = mybir.dt.int32
U32 = mybir.dt.uint32
ALU = mybir.AluOpType
AF = mybir.ActivationFunctionType
AX = mybir.AxisListType


class Buf:
    def __init__(self, t=None, name=""):
        self.t = t
        self.name = name
        self.w = None
        self.r = {}

    def __getitem__(self, k):
        return self.t[k]


class K:
    EPOCH = 28000
    NDMA = 12

    def __init__(self, nc):
        self.nc = nc
        self.stack = ExitStack()
        self.eng = dict(pe=nc.tensor, act=nc.scalar, dve=nc.vector, pool=nc.gpsimd, sp=nc.sync)
        self.sems = {}
        self.nsem = 0
        self.cur = {}
        self.waited = {e: {} for e in self.eng}
        self.own = {e: set() for e in self.eng}
        self.last = {}
        for e in ("pe", "act", "dve", "pool"):
            self.cur[e] = [self._new_sem(e), 0]
            self.own[e].add(self.cur[e][0])
        self.dq = {}
        self.dqi = {}
        for q in ("sp", "pool", "act"):
            self.dq[q] = [[self._new_sem("d" + q), 0] for _ in range(self.NDMA)]
            self.dqi[q] = 0
        self.same_engine_sync = True

    def _new_sem(self, name):
        key = self.nsem
        self.nsem += 1
        self.sems[key] = self.stack.enter_context(self.nc.semaphore(f"s{key}_{name}"))
        return key

    def _wait(self, e, evs):
        need = {}
        for ev in evs:
            if ev is None:
                continue
            k, v = ev
            if v > need.get(k, 0):
                need[k] = v
        for k, v in need.items():
            if e == "pe" and k in self.own["pe"]:
                continue
            if (not self.same_engine_sync) and k in self.own[e]:
                continue
            if self.waited[e].get(k, 0) >= v:
                continue
            self.eng[e].wait_ge(self.sems[k], v)
            self.waited[e][k] = v

    def _deps(self, reads, writes):
        evs = []
        for b in reads:
            evs.append(b.w)
        for b in writes:
            evs.append(b.w)
            evs.extend(b.r.items())
        return evs

    def _mark(self, ev, reads, writes):
        k, v = ev
        for b in reads:
            if v > b.r.get(k, 0):
                b.r[k] = v
        for b in writes:
            b.w = ev
            b.r = {}

    def op(self, e, fn, reads=(), writes=()):
        self._wait(e, self._deps(reads, writes))
        ins = fn(self.eng[e])
        c = self.cur[e]
        c[1] += 1
        ins.then_inc(self.sems[c[0]], 1)
        ev = (c[0], c[1])
        self.last[e] = ev
        if c[1] >= self.EPOCH:
            self.cur[e] = [self._new_sem(e), 0]
            self.own[e].add(self.cur[e][0])
        self._mark(ev, reads, writes)
        return ev

    def dma(self, q, fn, reads=(), writes=()):
        ring = self.dq[q]
        slot = ring[self.dqi[q]]
        self.dqi[q] = (self.dqi[q] + 1) % len(ring)
        evs = self._deps(reads, writes)
        if slot[1] > 0:
            evs.append((slot[0], slot[1]))
        self._wait(q, evs)
        if slot[1] >= self.EPOCH:
            slot[0] = self._new_sem("d" + q)
            slot[1] = 0
        ins = fn(self.eng[q])
        slot[1] += 16
        ins.then_inc(self.sems[slot[0]], 16)
        ev = (slot[0], slot[1])
        self._mark(ev, reads, writes)
        return ev

    def all_events(self):
        evs = []
        for e, ev in self.last.items():
            evs.append(ev)
        for q, ring in self.dq.items():
            for s in ring:
                if s[1] > 0:
                    evs.append((s[0], s[1]))
        return evs

    def barrier(self, engines=("pe", "act", "dve", "pool", "sp")):
        evs = self.all_events()
        for e in engines:
            self._wait(e, evs)

    def close(self):
        self.stack.close()


D = 1024
NCH = 8
NE = 16
HD = 128
NQH = 8
NKVH = 2
EPS = 1e-6


class Prog:
    def __init__(self, nc, T, LC):
        self.nc = nc
        self.k = K(nc)
        self.T = T
        self.LC = LC
        self.gs = ExitStack()
        k = self.k
        self.ident = self.sb(self.gs, "ident", [128, 128], F32)
        self.identb = self.sb(self.gs, "identb", [128, 128], BF16)
        self.iota_row = self.sb(self.gs, "iota_row", [128, 512], F32)
        self.pidx = self.sb(self.gs, "pidx", [128, 1], F32)
        self.iota_h = self.sb(self.gs, "iota_h", [128, 512], mybir.dt.float16)
        self.ones_f = self.sb(self.gs, "ones_f", [128, 128], F32)
        self.ones_b = self.sb(self.gs, "ones_b", [128, 128], BF16)
        k.op("pool", lambda e: e.iota(self.iota_row[:], pattern=[[1, 512]], base=0, channel_multiplier=0,
                                      allow_small_or_imprecise_dtypes=True), writes=[self.iota_row])
        k.op("pool", lambda e: e.iota(self.pidx[:], pattern=[[0, 1]], base=0, channel_multiplier=1,
                                      allow_small_or_imprecise_dtypes=True), writes=[self.pidx])
        k.op("dve", lambda e: e.tensor_scalar(out=self.ident[:], in0=self.iota_row[:, 0:128], scalar1=self.pidx[:, 0:1],
                                              scalar2=None, op0=ALU.is_equal),
             reads=[self.iota_row, self.pidx], writes=[self.ident])
        k.op("dve", lambda e: e.tensor_copy(out=self.identb[:], in_=self.ident[:]), reads=[self.ident], writes=[self.identb])
        k.op("dve", lambda e: e.tensor_copy(out=self.iota_h[:], in_=self.iota_row[:]), reads=[self.iota_row], writes=[self.iota_h])
        k.op("dve", lambda e: e.memset(self.ones_f[:], 1.0), writes=[self.ones_f])
        k.op("dve", lambda e: e.memset(self.ones_b[:], 1.0), writes=[self.ones_b])
        self.ps = [Buf(self.gs.enter_context(nc.psum_tensor(f"ps{i}", [128, 1024], F32)), f"ps{i}") for i in range(4)]
        self.psi = 0

    def sb(self, st, name, shape, dt):
        self._uid = getattr(self, "_uid", 0) + 1
        name = f"{name}_{self._uid}"
        return Buf(st.enter_context(self.nc.sbuf_tensor(name, shape, dt)), name)

    def psum(self):
        p = self.ps[self.psi]
        self.psi = (self.psi + 1) % len(self.ps)
        return p

    def end_phase(self, st):
        self.k.barrier()
        st.close()

    def finish(self):
        self.k.barrier()
        self.gs.close()
        self.k.close()


def pbf(p):
    return p.t[:].bitcast(BF16)


def _pm_view(ap2d):
    return ap2d.rearrange("(c p) n -> p c n", p=128)


class Layers(Prog):
    def setup_small(self, st):
        k = self.k
        self.eps_t = self.sb(st, "eps_t", [128, 1], F32)
        k.op("dve", lambda e: e.memset(self.eps_t[:], EPS), writes=[self.eps_t])
        self.mhalf = self.sb(st, "mhalf", [128, 1], F32)
        k.op("dve", lambda e: e.memset(self.mhalf[:], -0.5), writes=[self.mhalf])
        self.sm = [[self.sb(st, f"sm{i}_{j}", [128, 1], F32) for j in range(2)] for i in range(4)]
        self.smi = 0

    def small(self):
        s = self.sm[self.smi]
        self.smi = (self.smi + 1) % len(self.sm)
        return s

    def make_sil_rep(self, st, name, vec_pm_ap):
        k = self.k
        v = self.sb(st, name + "_v", [128, 8], F32)
        sg = self.sb(st, name + "_sg", [128, 8], F32)
        rep = self.sb(st, name + "_rep", [128, 8, 128], F32)
        k.dma("sp", lambda e: e.dma_start(out=v[:], in_=vec_pm_ap), writes=[v])
        k.op("act", lambda e: e.activation(out=sg[:], in_=v[:], func=AF.Silu), reads=[v], writes=[sg])
        for c in range(8):
            k.op("dve", lambda e: e.tensor_copy(out=rep[:, c, :], in_=sg[:, c:c + 1].to_broadcast([128, 128])),
                 reads=[sg], writes=[rep])
        return rep

    def ada_precompute(self, c_pm, cc_pm, ada_w, ada_b, depth, rows_d):
        k = self.k
        with ExitStack() as ls:
            S2 = self.sb(ls, "ap_S2", [128, 8, 2], F32)
            for si, vec in enumerate((c_pm, cc_pm)):
                v = self.sb(ls, f"ap_v{si}", [128, 8], F32)
                sg = self.sb(ls, f"ap_sg{si}", [128, 8], F32)
                k.dma("sp", lambda e: e.dma_start(out=v[:], in_=vec), writes=[v])
                k.op("act", lambda e: e.activation(out=sg[:], in_=v[:], func=AF.Silu), reads=[v], writes=[sg])
                k.op("dve", lambda e: e.tensor_copy(out=S2[:, :, si], in_=sg[:]), reads=[sg], writes=[S2])
            wb = [self.sb(ls, f"ap_w{j}", [128, 8, 1024], F32) for j in range(2)]
            bb = self.sb(ls, "ap_b", [2, 6 * D], F32)
            rowt = [self.sb(ls, f"ap_r{j}", [2, 6 * D], F32) for j in range(2)]
            n = 0
            for i in range(depth):
                av = _pm_view(ada_w[i])
                rt = rowt[i % 2]
                k.dma("sp", lambda e: e.dma_start(out=bb[:], in_=ada_b[i:i + 1, :].partition_broadcast(2)), writes=[bb])
                for j in range(6):
                    wj = wb[n % 2]
                    n += 1
                    k.dma("sp", lambda e: e.dma_start(out=wj[:], in_=av[:, :, j * 1024:(j + 1) * 1024]), writes=[wj])
                    p = self.psum()
                    for h in range(2):
                        for c in range(8):
                            k.op("pe", lambda e: e.matmul(p[0:2, h * 512:(h + 1) * 512], lhsT=S2[:, c, :], rhs=wj[:, c, h * 512:(h + 1) * 512],
                                                          start=(c == 0), stop=(c == 7)), reads=[S2, wj], writes=[p])
                    k.op("dve", lambda e: e.tensor_tensor(out=rt[:, j * 1024:(j + 1) * 1024], in0=p[0:2, :], in1=bb[:, j * 1024:(j + 1) * 1024],
                                                          op=ALU.add), reads=[p, bb], writes=[rt])
                k.dma("sp", lambda e: e.dma_start(out=rows_d[i], in_=rt[:]), reads=[rt])
            self.k.barrier()

    def ada_load(self, st, row_ap, norm_g_i, tag):
        k = self.k
        m = [self.sb(st, f"mod{tag}_{j}", [128, 1024], F32) for j in range(6)]
        for j in range(6):
            k.dma("sp", lambda e: e.dma_start(out=m[j][:], in_=row_ap[0:1, j * 1024:(j + 1) * 1024].partition_broadcast(128)), writes=[m[j]])
        with ExitStack() as ls:
            gb = [self.sb(ls, f"adag{tag}_{j}", [128, 1024], F32) for j in range(2)]
            for (jsc, gi) in ((1, 0), (4, 1)):
                k.dma("sp", lambda e: e.dma_start(out=gb[gi][:], in_=norm_g_i[gi:gi + 1, :].partition_broadcast(128)), writes=[gb[gi]])
                k.op("dve", lambda e: e.scalar_tensor_tensor(out=m[jsc][:], in0=m[jsc][:], scalar=1.0, in1=gb[gi][:],
                                                             op0=ALU.add, op1=ALU.mult), reads=[m[jsc], gb[gi]], writes=[m[jsc]])
            self.k.barrier()
        sh1, a1, g1, sh2, a2, g2 = m
        return a1, sh1, g1, a2, sh2, g2

    def ada_phase(self, st, sil_rep, ada_w_i, ada_b_i, norm_g_i, tag):
        k = self.k
        m = [self.sb(st, f"mod{tag}_{j}", [128, 1024], F32) for j in range(6)]
        av = _pm_view(ada_w_i)
        with ExitStack() as ls:
            if not isinstance(sil_rep, Buf):
                sil_rep = self.make_sil_rep(ls, "silrep" + tag, sil_rep)
            wb = [self.sb(ls, f"adaw{tag}_{j}", [128, 8, 1024], F32) for j in range(2)]
            bb = [self.sb(ls, f"adab{tag}_{j}", [128, 1024], F32) for j in range(2)]
            for j in range(6):
                wj, bj = wb[j % 2], bb[j % 2]
                k.dma("sp", lambda e: e.dma_start(out=wj[:], in_=av[:, :, j * 1024:(j + 1) * 1024]), writes=[wj])
                k.dma("sp", lambda e: e.dma_start(out=bj[:], in_=ada_b_i[0:1, j * 1024:(j + 1) * 1024].partition_broadcast(128)),
                      writes=[bj])
                p = self.psum()
                for h in range(2):
                    for c in range(8):
                        k.op("pe", lambda e: e.matmul(p[:, h * 512:(h + 1) * 512], lhsT=sil_rep[:, c, :],
                                                      rhs=wj[:, c, h * 512:(h + 1) * 512], start=(c == 0), stop=(c == 7)),
                             reads=[sil_rep, wj], writes=[p])
                k.op("dve", lambda e: e.tensor_tensor(out=m[j][:], in0=p[:], in1=bj[:], op=ALU.add),
                     reads=[p, bj], writes=[m[j]])
            for (jsc, gi) in ((1, 0), (4, 1)):
                gb = bb[gi]
                k.dma("sp", lambda e: e.dma_start(out=gb[:], in_=norm_g_i[gi:gi + 1, :].partition_broadcast(128)), writes=[gb])
                k.op("dve", lambda e: e.scalar_tensor_tensor(out=m[jsc][:], in0=m[jsc][:], scalar=1.0, in1=gb[:],
                                                             op0=ALU.add, op1=ALU.mult),
                     reads=[m[jsc], gb], writes=[m[jsc]])
            self.k.barrier()
        sh1, a1, g1, sh2, a2, g2 = m
        return a1, sh1, g1, a2, sh2, g2

    def norm_stats_a(self, xt, tmp, rows=128):
        k = self.k
        ss, rs = self.small()
        R = slice(0, rows)
        k.op("act", lambda e: e.activation(out=tmp[R, :], in_=xt[R, :], func=AF.Square, accum_out=ss[R, :]),
             reads=[xt], writes=[tmp, ss])
        return (ss, rs)

    def norm_stats_b(self, pr, rows=128):
        k = self.k
        ss, rs = pr
        R = slice(0, rows)
        k.op("dve", lambda e: e.tensor_scalar(out=ss[R, :], in0=ss[R, :], scalar1=1.0 / D, scalar2=EPS, op0=ALU.mult, op1=ALU.add),
             reads=[ss], writes=[ss])
        k.op("pool", lambda e: e.tensor_tensor(out=rs[R, :], in0=ss[R, :], in1=self.mhalf[R, 0:1], op=ALU.pow),
             reads=[ss, self.mhalf], writes=[rs])
        return rs

    def norm_stats(self, xt, tmp, rows=128):
        return self.norm_stats_b(self.norm_stats_a(xt, tmp, rows), rows)

    def norm_apply(self, xt, rs, A, B, tmp, out_h, rows=128):
        k = self.k
        R = slice(0, rows)
        k.op("dve", lambda e: e.scalar_tensor_tensor(out=tmp[R, :], in0=xt[R, :], scalar=rs[R, 0:1], in1=A[R, :],
                                                     op0=ALU.mult, op1=ALU.mult),
             reads=[xt, rs, A], writes=[tmp])
        k.op("dve", lambda e: e.tensor_tensor(out=out_h[R, :], in0=tmp[R, :], in1=B[R, :], op=ALU.add),
             reads=[tmp, B], writes=[out_h])

    def norm_tile(self, xt, A, B, tmp, out_h, rows=128):
        rs = self.norm_stats(xt, tmp, rows)
        self.norm_apply(xt, rs, A, B, tmp, out_h, rows)

    def transpose_bf_to(self, src, dst, col0, rows=128):
        k = self.k
        p = self.psum()
        pv = pbf(p)[:, 0:1024].rearrange("p (c t) -> p c t", c=8)
        for c in range(8):
            k.op("pe", lambda e: e.transpose(pv[:, c, 0:rows], src[0:rows, c * 128:(c + 1) * 128], self.identb[0:rows, 0:rows]),
                 reads=[src, self.identb], writes=[p])
        k.op("act", lambda e: e.copy(out=dst[:, :, col0:col0 + rows], in_=pv[:, :, 0:rows]), reads=[p], writes=[dst])

    def phase_a(self, st, src_ap, T, A1, B1, tag, hT=None):
        k = self.k
        if hT is None:
            hT = self.sb(st, f"hT{tag}", [128, 8, T], BF16)
        with ExitStack() as ls:
            xts = [self.sb(ls, f"pa_x{i}", [128, 1024], F32) for i in range(3)]
            tmps = [self.sb(ls, f"pa_t{i}", [128, 1024], F32) for i in range(3)]
            hbs = [self.sb(ls, f"pa_h{i}", [128, 1024], BF16) for i in range(3)]
            NTT = T // 128
            rss = {}

            def st1(tt):
                xt, tmp = xts[tt % 3], tmps[tt % 3]
                k.dma("sp", lambda e: e.dma_start(out=xt[:], in_=src_ap[tt * 128:(tt + 1) * 128, :]), writes=[xt])
                rss[tt] = self.norm_stats_a(xt, tmp)

            def st1b(tt):
                rss[tt] = self.norm_stats_b(rss[tt])

            def st2(tt):
                self.norm_apply(xts[tt % 3], rss.pop(tt), A1, B1, tmps[tt % 3], hbs[tt % 3])

            def st3(tt):
                self.transpose_bf_to(hbs[tt % 3], hT, tt * 128)

            for step in range(NTT + 2):
                if step < NTT:
                    st1(step)
                if 0 <= step - 1 < NTT:
                    st2(step - 1)
                if step < NTT:
                    st1b(step)
                if 0 <= step - 2 < NTT:
                    st3(step - 2)
            self.k.barrier()
        return hT

    def load_w_bf(self, dst, w_ap2d, col0, ncols):
        v = _pm_view(w_ap2d)
        self.k.dma("pool", lambda e: e.dma_start(out=dst[:, :, 0:ncols], in_=v[:, :, col0:col0 + ncols]), writes=[dst])

    def phase_b_conv(self, hT, T, w_in, cw_pm, zT_d, tag):
        k = self.k
        TB = min(512, T)
        with ExitStack() as ls:
            win = self.sb(ls, "cv_win", [128, 8, 3072], BF16)
            for q in range(3):
                v = _pm_view(w_in)
                k.dma("pool", lambda e: e.dma_start(out=win[:, :, q * 1024:(q + 1) * 1024], in_=v[:, :, q * 1024:(q + 1) * 1024]),
                      writes=[win])
            cw = self.sb(ls, "cv_cw", [128, 8, 3], F32)
            k.dma("sp", lambda e: e.dma_start(out=cw[:], in_=cw_pm), writes=[cw])
            cv = self.sb(ls, "cv_cv", [128, T + 2], F32)
            gb = self.sb(ls, "cv_gb", [128, T], BF16)
            acc = self.sb(ls, "cv_acc", [128, T], F32)
            vt = [self.sb(ls, f"cv_vt{i}", [128, TB], F32) for i in range(2)]
            zr = [self.sb(ls, f"cv_zr{i}", [128, T], BF16) for i in range(1)]
            k.op("dve", lambda e: e.memset(cv[:, 0:1], 0.0), writes=[cv])
            k.op("dve", lambda e: e.memset(cv[:, T + 1:T + 2], 0.0), writes=[cv])
            for j in range(8):
                for tb in range(T // TB):
                    ts = slice(tb * TB, (tb + 1) * TB)
                    pb_, pc_, pv_ = self.psum(), self.psum(), self.psum()
                    for q, p in ((0, pb_), (1, pc_), (2, pv_)):
                        for c in range(8):
                            k.op("pe", lambda e: e.matmul(p[:, 0:TB], lhsT=win[:, c, q * 1024 + j * 128:q * 1024 + (j + 1) * 128],
                                                          rhs=hT[:, c, ts], start=(c == 0), stop=(c == 7)),
                                 reads=[win, hT], writes=[p])
                    v_ = vt[tb % 2]
                    k.op("act", lambda e: e.copy(out=v_[:, 0:TB], in_=pv_[:, 0:TB]), reads=[pv_], writes=[v_])
                    k.op("dve", lambda e: e.tensor_tensor(out=cv[:, 1 + tb * TB:1 + (tb + 1) * TB], in0=pc_[:, 0:TB], in1=v_[:, 0:TB],
                                                          op=ALU.mult), reads=[pc_, v_], writes=[cv])
                    k.op("act", lambda e: e.copy(out=gb[:, ts], in_=pb_[:, 0:TB]), reads=[pb_], writes=[gb])
                z = zr[0]
                k.op("dve", lambda e: e.tensor_scalar(out=acc[:], in0=cv[:, 0:T], scalar1=cw[:, j, 0:1], scalar2=None, op0=ALU.mult),
                     reads=[cv, cw], writes=[acc])
                k.op("dve", lambda e: e.scalar_tensor_tensor(out=acc[:], in0=cv[:, 1:T + 1], scalar=cw[:, j, 1:2], in1=acc[:],
                                                             op0=ALU.mult, op1=ALU.add), reads=[cv, cw, acc], writes=[acc])
                k.op("dve", lambda e: e.scalar_tensor_tensor(out=acc[:], in0=cv[:, 2:T + 2], scalar=cw[:, j, 2:3], in1=acc[:],
                                                             op0=ALU.mult, op1=ALU.add), reads=[cv, cw, acc], writes=[acc])
                k.op("dve", lambda e: e.tensor_tensor(out=z[:], in0=acc[:], in1=gb[:], op=ALU.mult), reads=[acc, gb], writes=[z])
                k.dma("sp", lambda e: e.dma_start(out=zT_d[j * 128:(j + 1) * 128, :], in_=z[:]), reads=[z])
            self.k.barrier()

    def phase_c(self, T, zT_d, w_out, lat_src, lat_dst, G1, A2, B2, w_router, h2_d, st_out, tag, want_moe=True, z_tok=False):
        k = self.k
        TB = min(512, T)
        affT = self.sb(st_out, f"affT{tag}", [16, T], F32) if want_moe else None
        with ExitStack() as ls:
            wo = self.sb(ls, "pc_wo", [128, 8, 1024], BF16)
            self.load_w_bf(wo, w_out, 0, 1024)
            zts = [self.sb(ls, f"pc_z{i}", [128, 8, TB], BF16) for i in range(2)]
            xts = [self.sb(ls, f"pc_x{i}", [128, 1024], F32) for i in range(3)]
            tmps = [self.sb(ls, f"pc_t{i}", [128, 1024], F32) for i in range(3)]
            h2s = [self.sb(ls, f"pc_h{i}", [128, 1024], F32) for i in range(2)]
            h2bs = [self.sb(ls, f"pc_hb{i}", [128, 1024], BF16) for i in range(2)]
            h2Ts = [self.sb(ls, f"pc_hT{i}", [128, 8, TB], F32) for i in range(2)]
            if want_moe:
                wr = self.sb(ls, "pc_wr", [128, 8, 16], F32)
                k.dma("sp", lambda e: e.dma_start(out=wr[:], in_=_pm_view(w_router)), writes=[wr])
            if not z_tok:
                zv = zT_d.rearrange("(c p) t -> p c t", p=128)
            else:
                zrow = [self.sb(ls, f"pc_zr{i}", [128, 1024], BF16) for i in range(2)]
            NSUB = TB // 128
            tiles = [(tb, sub) for tb in range(T // TB) for sub in range(NSUB)]

            def stA(i):
                tb, sub = tiles[i]
                zt = zts[tb % 2]
                if sub == 0:
                    if not z_tok:
                        k.dma("sp", lambda e: e.dma_start(out=zt[:], in_=zv[:, :, tb * TB:(tb + 1) * TB]), writes=[zt])
                    else:
                        for sb_ in range(NSUB):
                            zr_ = zrow[sb_ % 2]
                            r0 = tb * TB + sb_ * 128
                            k.dma("sp", lambda e: e.dma_start(out=zr_[:], in_=zT_d[r0:r0 + 128, :]), writes=[zr_])
                            self.transpose_bf_to(zr_, zt, sb_ * 128)
                tt = tb * NSUB + sub
                xt, tmp = xts[i % 3], tmps[i % 3]
                rows = slice(tt * 128, (tt + 1) * 128)
                k.dma("sp", lambda e: e.dma_start(out=xt[:], in_=lat_src[rows, :]), writes=[xt])
                p = self.psum()
                for c in range(8):
                    for h in range(2):
                        k.op("pe", lambda e: e.matmul(p[:, h * 512:(h + 1) * 512], lhsT=zt[:, c, sub * 128:(sub + 1) * 128],
                                                      rhs=wo[:, c, h * 512:(h + 1) * 512], start=(c == 0), stop=(c == 7)),
                             reads=[zt, wo], writes=[p])
                k.op("dve", lambda e: e.tensor_tensor(out=tmp[:], in0=p[:], in1=G1[:], op=ALU.mult), reads=[p, G1], writes=[tmp])
                k.op("dve", lambda e: e.tensor_tensor(out=xt[:], in0=tmp[:], in1=xt[:], op=ALU.add), reads=[tmp, xt], writes=[xt])
                k.dma("pool", lambda e: e.dma_start(out=lat_dst[rows, :], in_=xt[:]), reads=[xt])
                if want_moe:
                    return self.norm_stats_a(xt, tmp)
                return None

            def stB(i, rs):
                tb, sub = tiles[i]
                tt = tb * NSUB + sub
                rows = slice(tt * 128, (tt + 1) * 128)
                xt, tmp, h2, h2b = xts[i % 3], tmps[i % 3], h2s[i % 2], h2bs[i % 2]
                self.norm_apply(xt, rs, A2, B2, tmp, h2)
                k.op("act", lambda e: e.copy(out=h2b[:], in_=h2[:]), reads=[h2], writes=[h2b])
                k.dma("pool", lambda e: e.dma_start(out=h2_d[rows, :], in_=h2b[:]), reads=[h2b])

            def stC(i):
                tb, sub = tiles[i]
                h2, h2T = h2s[i % 2], h2Ts[tb % 2]
                p2 = self.psum()
                p2v = p2.t[:].rearrange("p (c t) -> p c t", c=8)
                for c in range(8):
                    k.op("pe", lambda e: e.transpose(p2v[:, c, :], h2[:, c * 128:(c + 1) * 128], self.ident[:]),
                         reads=[h2, self.ident], writes=[p2])
                k.op("act", lambda e: e.copy(out=h2T[:, :, sub * 128:(sub + 1) * 128], in_=p2v), reads=[p2], writes=[h2T])
                if sub == NSUB - 1:
                    p3 = self.psum()
                    for c in range(8):
                        k.op("pe", lambda e: e.matmul(p3[0:16, 0:TB], lhsT=wr[:, c, :], rhs=h2T[:, c, :], start=(c == 0), stop=(c == 7)),
                             reads=[wr, h2T], writes=[p3])
                    k.op("act", lambda e: e.activation(out=affT[:, tb * TB:(tb + 1) * TB], in_=p3[0:16, 0:TB], func=AF.Exp),
                         reads=[p3], writes=[affT])

            NTL = len(tiles)
            rsd = {}
            for step in range(NTL + 2):
                if step < NTL:
                    rsd[step] = stA(step)
                if want_moe and 0 <= step - 1 < NTL:
                    stB(step - 1, rsd.pop(step - 1))
                if want_moe and step < NTL:
                    rsd[step] = self.norm_stats_b(rsd[step])
                if want_moe and 0 <= step - 2 < NTL:
                    stC(step - 2)
            if want_moe:
                rc = self.sb(ls, "pc_rc", [16, TB], F32)
                for tb in range(T // TB):
                    ts = slice(tb * TB, (tb + 1) * TB)
                    p = self.psum()
                    k.op("pe", lambda e: e.matmul(p[0:16, 0:TB], lhsT=self.ones_f[0:16, 0:16], rhs=affT[:, ts], start=True, stop=True),
                         reads=[self.ones_f, affT], writes=[p])
                    k.op("dve", lambda e: e.reciprocal(out=rc[:], in_=p[0:16, 0:TB]), reads=[p], writes=[rc])
                    k.op("dve", lambda e: e.tensor_tensor(out=affT[:, ts], in0=affT[:, ts], in1=rc[:], op=ALU.mult),
                         reads=[affT, rc], writes=[affT])
            self.k.barrier()
        return affT


class Moe(Layers):
    def moe_route(self, T, affT, idx, gsel, n_iter=27, dbg=None):
        k = self.k
        cap = 2 * T // NE
        CW = min(cap, 128)
        NJT = cap // CW
        NT = T // 128
        if True:
            with ExitStack() as ls:
                junk = self.sb(ls, "mo_junk", [16, T], F32)
                mask = self.sb(ls, "mo_mask", [16, T], F32)
                cum = self.sb(ls, "mo_cum", [16, T], F32)
                sv = {n: self.sb(ls, "mo_" + n, [16, 1], F32) for n in ("lo", "hi", "mid", "cnt", "ge", "nge", "t1")}
                lo, hi, mid, cnt, ge, nge, t1 = (sv[n] for n in ("lo", "hi", "mid", "cnt", "ge", "nge", "t1"))
                if T >= 1024:
                    W = T // 8
                    if not hasattr(self, "aff_scr"):
                        self.aff_scr = self.nc.dram_tensor("s_affscr", [16, T], F32).ap()
                        self.aff_scr_buf = Buf(None, "aff_scr")
                    a128 = self.sb(ls, "mo_a128", [128, W], F32)
                    j128 = self.sb(ls, "mo_j128", [128, W], F32)
                    G = self.sb(ls, "mo_G", [16, 128], F32)
                    GT8 = self.sb(ls, "mo_GT8", [128, 16], F32)
                    Bd = self.sb(ls, "mo_Bd", [128, 128], F32)
                    bv = {n_: self.sb(ls, "mo_b" + n_, [128, 1], F32) for n_ in ("lo", "hi", "mid", "cnt", "ge", "nge", "t1")}
                    k.dma("sp", lambda e: e.dma_start(out=self.aff_scr[:, 0:T], in_=affT[:]), reads=[affT], writes=[self.aff_scr_buf])
                    k.dma("sp", lambda e: e.dma_start(out=a128[:], in_=self.aff_scr[:, 0:T].rearrange("e (s j) -> (e s) j", s=8)),
                          reads=[self.aff_scr_buf], writes=[a128])
                    k.op("pool", lambda e: e.iota(G[:], pattern=[[1, 16], [0, 8]], base=0, channel_multiplier=0,
                                                  allow_small_or_imprecise_dtypes=True), writes=[G])
                    k.op("dve", lambda e: e.tensor_scalar(out=G[:], in0=G[:], scalar1=self.pidx[0:16, 0:1], scalar2=None, op0=ALU.is_equal),
                         reads=[G, self.pidx], writes=[G])
                    p = self.psum()
                    k.op("pe", lambda e: e.matmul(p[:, 0:128], lhsT=G[:], rhs=G[:], start=True, stop=True), reads=[G], writes=[p])
                    k.op("dve", lambda e: e.tensor_copy(out=Bd[:], in_=p[:, 0:128]), reads=[p], writes=[Bd])
                    p = self.psum()
                    k.op("pe", lambda e: e.transpose(p[:, 0:16], G[:], self.ident[0:16, 0:16]), reads=[G, self.ident], writes=[p])
                    k.op("dve", lambda e: e.tensor_scalar(out=GT8[:], in0=p[:, 0:16], scalar1=0.125, scalar2=None, op0=ALU.mult),
                         reads=[p], writes=[GT8])
                    blo, bhi, bmid, bcnt, bge, bnge, bt1 = (bv[n_] for n_ in ("lo", "hi", "mid", "cnt", "ge", "nge", "t1"))
                    k.op("dve", lambda e: e.memset(blo[:], 0.0), writes=[blo])
                    k.op("dve", lambda e: e.memset(bhi[:], 1.5), writes=[bhi])
                    for _ in range(n_iter):
                        k.op("dve", lambda e: e.tensor_tensor(out=bmid[:], in0=blo[:], in1=bhi[:], op=ALU.add), reads=[blo, bhi], writes=[bmid])
                        k.op("dve", lambda e: e.tensor_scalar(out=bmid[:], in0=bmid[:], scalar1=0.5, scalar2=None, op0=ALU.mult),
                             reads=[bmid], writes=[bmid])
                        k.op("dve", lambda e: e.tensor_scalar(out=j128[:], in0=a128[:], scalar1=bmid[:, 0:1], scalar2=0.0,
                                                              op0=ALU.is_ge, op1=ALU.add, accum_out=bcnt[:]),
                             reads=[a128, bmid], writes=[j128, bcnt])
                        pt_ = self.psum()
                        k.op("pe", lambda e: e.matmul(pt_[:, 0:1], lhsT=Bd[:], rhs=bcnt[:], start=True, stop=True), reads=[Bd, bcnt], writes=[pt_])
                        k.op("dve", lambda e: e.tensor_scalar(out=bge[:], in0=pt_[:, 0:1], scalar1=float(cap), scalar2=None, op0=ALU.is_ge),
                             reads=[pt_], writes=[bge])
                        k.op("dve", lambda e: e.tensor_scalar(out=bnge[:], in0=pt_[:, 0:1], scalar1=float(cap), scalar2=None, op0=ALU.is_lt),
                             reads=[pt_], writes=[bnge])
                        k.op("dve", lambda e: e.tensor_tensor(out=bt1[:], in0=bmid[:], in1=blo[:], op=ALU.subtract), reads=[bmid, blo], writes=[bt1])
                        k.op("dve", lambda e: e.scalar_tensor_tensor(out=blo[:], in0=bt1[:], scalar=bge[:, 0:1], in1=blo[:],
                                                                     op0=ALU.mult, op1=ALU.add), reads=[bt1, bge, blo], writes=[blo])
                        k.op("dve", lambda e: e.tensor_tensor(out=bt1[:], in0=bmid[:], in1=bhi[:], op=ALU.subtract), reads=[bmid, bhi], writes=[bt1])
                        k.op("dve", lambda e: e.scalar_tensor_tensor(out=bhi[:], in0=bt1[:], scalar=bnge[:, 0:1], in1=bhi[:],
                                                                     op0=ALU.mult, op1=ALU.add), reads=[bt1, bnge, bhi], writes=[bhi])
                    p = self.psum()
                    k.op("pe", lambda e: e.matmul(p[0:16, 0:1], lhsT=GT8[:], rhs=blo[:], start=True, stop=True), reads=[GT8, blo], writes=[p])
                    k.op("dve", lambda e: e.tensor_copy(out=lo[:], in_=p[0:16, 0:1]), reads=[p], writes=[lo])
                else:
                    k.op("dve", lambda e: e.memset(lo[:], 0.0), writes=[lo])
                    k.op("dve", lambda e: e.memset(hi[:], 1.5), writes=[hi])
                    for _ in range(n_iter):
                        k.op("dve", lambda e: e.tensor_tensor(out=mid[:], in0=lo[:], in1=hi[:], op=ALU.add), reads=[lo, hi], writes=[mid])
                        k.op("dve", lambda e: e.tensor_scalar(out=mid[:], in0=mid[:], scalar1=0.5, scalar2=None, op0=ALU.mult),
                             reads=[mid], writes=[mid])
                        k.op("dve", lambda e: e.tensor_scalar(out=junk[:], in0=affT[:], scalar1=mid[:, 0:1], scalar2=0.0,
                                                              op0=ALU.is_ge, op1=ALU.add, accum_out=cnt[:]),
                             reads=[affT, mid], writes=[junk, cnt])
                        k.op("dve", lambda e: e.tensor_scalar(out=ge[:], in0=cnt[:], scalar1=float(cap), scalar2=None, op0=ALU.is_ge),
                             reads=[cnt], writes=[ge])
                        k.op("dve", lambda e: e.tensor_scalar(out=nge[:], in0=cnt[:], scalar1=float(cap), scalar2=None, op0=ALU.is_lt),
                             reads=[cnt], writes=[nge])
                        k.op("dve", lambda e: e.tensor_tensor(out=t1[:], in0=mid[:], in1=lo[:], op=ALU.subtract), reads=[mid, lo], writes=[t1])
                        k.op("dve", lambda e: e.scalar_tensor_tensor(out=lo[:], in0=t1[:], scalar=ge[:, 0:1], in1=lo[:],
                                                                     op0=ALU.mult, op1=ALU.add), reads=[t1, ge, lo], writes=[lo])
                        k.op("dve", lambda e: e.tensor_tensor(out=t1[:], in0=mid[:], in1=hi[:], op=ALU.subtract), reads=[mid, hi], writes=[t1])
                        k.op("dve", lambda e: e.scalar_tensor_tensor(out=hi[:], in0=t1[:], scalar=nge[:, 0:1], in1=hi[:],
                                                                     op0=ALU.mult, op1=ALU.add), reads=[t1, nge, hi], writes=[hi])
                k.op("dve", lambda e: e.tensor_scalar(out=mask[:], in0=affT[:], scalar1=lo[:, 0:1], scalar2=None, op0=ALU.is_ge),
                     reads=[affT, lo], writes=[mask])
                k.op("dve", lambda e: e.memset(junk[:], 1.0), writes=[junk])
                k.op("dve", lambda e: e.tensor_tensor_scan(out=cum[:], data0=junk[:], data1=mask[:], initial=0.0,
                                                           op0=ALU.mult, op1=ALU.add), reads=[junk, mask], writes=[cum])
                k.op("dve", lambda e: e.tensor_tensor(out=cum[:], in0=cum[:], in1=mask[:], op=ALU.mult), reads=[cum, mask], writes=[cum])
                k.op("dve", lambda e: e.tensor_scalar(out=cum[:], in0=cum[:], scalar1=-1.0, scalar2=None, op0=ALU.add),
                     reads=[cum], writes=[cum])
                pmT = self.sb(ls, "mo_pmT", [128, NT, 16], F32)
                gT = self.sb(ls, "mo_gT", [128, NT, 16], F32)
                rT = self.sb(ls, "mo_rT", [128, NT, 16], F32)
                GB = self.sb(ls, "mo_GB", [128, NT, NE, 5], BF16)
                gp = self.sb(ls, "mo_gp", [128, NT, 16], BF16)
                hl = self.sb(ls, "mo_hl", [128, NT, 2], F32)
                for tc in range(NT):
                    cs = slice(tc * 128, (tc + 1) * 128)
                    p = self.psum()
                    k.op("pe", lambda e: e.transpose(p[:, 0:16], cum[:, cs], self.ident[0:16, 0:16]),
                         reads=[cum, self.ident], writes=[p])
                    k.op("pe", lambda e: e.transpose(p[:, 16:32], affT[:, cs], self.ident[0:16, 0:16]),
                         reads=[affT, self.ident], writes=[p])
                    k.op("act", lambda e: e.copy(out=pmT[:, tc, :], in_=p[:, 0:16]), reads=[p], writes=[pmT])
                    k.op("act", lambda e: e.copy(out=gT[:, tc, :], in_=p[:, 16:32]), reads=[p], writes=[gT])
                    k.op("dve", lambda e: e.memset(hl[:, tc, 0:1], float(tc)), writes=[hl])
                k.op("dve", lambda e: e.tensor_copy(out=hl[:, :, 1:2], in_=self.pidx[:, 0:1].unsqueeze(1).to_broadcast([128, NT, 1])),
                     reads=[self.pidx], writes=[hl])
                for piece in range(3):
                    k.op("dve", lambda e: e.tensor_copy(out=gp[:], in_=gT[:]), reads=[gT], writes=[gp])
                    k.op("dve", lambda e: e.tensor_copy(out=GB[:, :, :, 2 + piece], in_=gp[:]), reads=[gp], writes=[GB])
                    if piece < 2:
                        k.op("dve", lambda e: e.tensor_copy(out=rT[:], in_=gp[:]), reads=[gp], writes=[rT])
                        k.op("dve", lambda e: e.tensor_tensor(out=gT[:], in0=gT[:], in1=rT[:], op=ALU.subtract), reads=[gT, rT], writes=[gT])
                for c2 in range(2):
                    k.op("dve", lambda e: e.tensor_copy(out=GB[:, :, :, c2], in_=hl[:, :, c2:c2 + 1].to_broadcast([128, NT, NE])),
                         reads=[hl], writes=[GB])
                ohs = [self.sb(ls, f"mo_oh{i}", [128, cap], BF16) for i in range(4)]
                selS = [self.sb(ls, f"mo_selS{i}", [8, cap], F32) for i in range(2)]
                selT = [self.sb(ls, f"mo_selT{i}", [128, NJT, 5], F32) for i in range(2)]
                n = 0
                for ex in range(NE):
                    p = self.psum()
                    for tc in range(NT):
                        oh = ohs[n % 4]
                        n += 1
                        if True:
                            k.op("dve", lambda e: e.tensor_scalar(out=oh[:], in0=self.iota_h[:, 0:cap], scalar1=pmT[:, tc, ex:ex + 1],
                                                                  scalar2=None, op0=ALU.is_equal),
                                 reads=[self.iota_h, pmT], writes=[oh])
                        k.op("pe", lambda e: e.matmul(p[0:5, 0:cap], lhsT=GB[:, tc, ex, :], rhs=oh[:], start=(tc == 0), stop=(tc == NT - 1)),
                             reads=[GB, oh], writes=[p])
                    sS, sT = selS[ex % 2], selT[ex % 2]
                    k.op("act", lambda e: e.copy(out=sS[0:5, :], in_=p[0:5, 0:cap]), reads=[p], writes=[sS])
                    p2 = self.psum()
                    for jt in range(NJT):
                        k.op("pe", lambda e: e.transpose(p2[0:CW, jt * 8:jt * 8 + 5], sS[0:5, jt * CW:(jt + 1) * CW], self.ident[0:5, 0:5]),
                             reads=[sS, self.ident], writes=[p2])
                    p2v = p2.t[:, 0:NJT * 8].rearrange("p (j c) -> p j c", c=8)
                    k.op("act", lambda e: e.copy(out=sT[0:CW, :, :], in_=p2v[0:CW, :, 0:5]), reads=[p2], writes=[sT])
                    cols = slice(ex * NJT, (ex + 1) * NJT)
                    k.op("dve", lambda e: e.scalar_tensor_tensor(out=sT[0:CW, :, 0], in0=sT[0:CW, :, 0], scalar=128.0, in1=sT[0:CW, :, 1],
                                                                 op0=ALU.mult, op1=ALU.add), reads=[sT], writes=[sT])
                    k.op("dve", lambda e: e.tensor_copy(out=idx[0:CW, cols], in_=sT[0:CW, :, 0]), reads=[sT], writes=[idx])
                    k.op("dve", lambda e: e.tensor_tensor(out=sT[0:CW, :, 2], in0=sT[0:CW, :, 2], in1=sT[0:CW, :, 3], op=ALU.add),
                         reads=[sT], writes=[sT])
                    k.op("dve", lambda e: e.tensor_tensor(out=gsel[0:CW, cols], in0=sT[0:CW, :, 2], in1=sT[0:CW, :, 4], op=ALU.add),
                         reads=[sT], writes=[gsel])
                if dbg is not None:
                    k.dma("sp", lambda e: e.dma_start(out=dbg[0], in_=idx[:]), reads=[idx])
                    k.dma("sp", lambda e: e.dma_start(out=dbg[1], in_=gsel[:]), reads=[gsel])
                    k.dma("sp", lambda e: e.dma_start(out=dbg[2], in_=affT[:]), reads=[affT])
                    k.dma("sp", lambda e: e.dma_start(out=dbg[3], in_=cum[:]), reads=[cum])
                self.k.barrier()

    def moe_experts(self, streams, wg_d, wu_d, wd_d):
        k = self.k
        lat_sc = Buf(None, "lat_scatter")
        for s_ in streams:
            T = s_["T"]
            s_["cap"] = 2 * T // NE
            s_["CW"] = min(s_["cap"], 128)
            s_["NJT"] = s_["cap"] // s_["CW"]
            s_["reg"] = self.nc.gpsimd.to_reg(T - 1)
        if True:
            with ExitStack() as ls:
                W = [[self.sb(ls, f"mo_w{n_}{i}", [128, 8, 1024], BF16) for n_ in "gud"] for i in range(2)]
                for si, s_ in enumerate(streams):
                    cap = s_["cap"]
                    s_["xs"] = [self.sb(ls, f"mo_xs{si}_{i}", [128, 1024], BF16) for i in range(4 if cap > 128 else 2)]
                    s_["xsT"] = [self.sb(ls, f"mo_xsT{si}_{i}", [128, 8, cap], BF16) for i in range(2)]
                    s_["hidT"] = [self.sb(ls, f"mo_hidT{si}_{i}", [128, 8, cap], BF16) for i in range(2)]
                    s_["sg"] = [self.sb(ls, f"mo_sg{si}_{i}", [128, cap], F32) for i in range(2)]
                    s_["ys"] = [self.sb(ls, f"mo_ys{si}_{i}", [128, 1024], F32) for i in range(2)]
                    s_["nys"] = 0

                def load_w(ex):
                    for n_, src in enumerate((wg_d, wu_d, wd_d)):
                        self.load_w_bf(W[ex % 2][n_], src[ex], 0, 1024)

                def gather(ex):
                    for s_ in streams:
                        NJT, CW = s_["NJT"], s_["CW"]
                        for jt in range(NJT):
                            col = ex * NJT + jt
                            x_ = s_["xs"][(ex * NJT + jt) % len(s_["xs"])]
                            idx, h2_d, reg = s_["idx"], s_["h2_d"], s_["reg"]
                            k.dma("pool", lambda e: e.indirect_dma_start(
                                out=x_[0:CW, :], out_offset=None, in_=h2_d[:, :],
                                in_offset=bass.IndirectOffsetOnAxis(ap=idx[0:CW, col:col + 1], axis=0),
                                bounds_check=reg, oob_is_err=False), reads=[idx], writes=[x_])

                load_w(0)
                gather(0)
                for ex in range(NE):
                    wg, wu, wd = W[ex % 2]
                    for s_ in streams:
                        NJT, CW = s_["NJT"], s_["CW"]
                        xT = s_["xsT"][ex % 2]
                        for jt in range(NJT):
                            x_ = s_["xs"][(ex * NJT + jt) % len(s_["xs"])]
                            p = self.psum()
                            pv = pbf(p)[:, 0:1024].rearrange("p (c t) -> p c t", c=8)
                            for c in range(8):
                                k.op("pe", lambda e: e.transpose(pv[:, c, 0:CW], x_[0:CW, c * 128:(c + 1) * 128], self.identb[0:CW, 0:CW]),
                                     reads=[x_, self.identb], writes=[p])
                            k.op("act", lambda e: e.copy(out=xT[:, :, jt * CW:(jt + 1) * CW], in_=pv[:, :, 0:CW]), reads=[p], writes=[xT])
                    if ex + 1 < NE:
                        load_w(ex + 1)
                        gather(ex + 1)
                    for s_ in streams:
                        cap, NJT, CW = s_["cap"], s_["NJT"], s_["CW"]
                        xT, hidT, sg = s_["xsT"][ex % 2], s_["hidT"][ex % 2], s_["sg"]
                        for fc in range(8):
                            pg, pu = self.psum(), self.psum()
                            for (p, w) in ((pg, wg), (pu, wu)):
                                for c in range(8):
                                    k.op("pe", lambda e: e.matmul(p[:, 0:cap], lhsT=w[:, c, fc * 128:(fc + 1) * 128], rhs=xT[:, c, :],
                                                                  start=(c == 0), stop=(c == 7)), reads=[w, xT], writes=[p])
                            sg_ = sg[fc % 2]
                            k.op("act", lambda e: e.activation(out=sg_[:], in_=pg[:, 0:cap], func=AF.Silu), reads=[pg], writes=[sg_])
                            k.op("dve", lambda e: e.tensor_tensor(out=hidT[:, fc, :], in0=sg_[:], in1=pu[:, 0:cap], op=ALU.mult),
                                 reads=[sg_, pu], writes=[hidT])
                    for s_ in streams:
                        cap, NJT, CW = s_["cap"], s_["NJT"], s_["CW"]
                        hidT, idx, gsel, G2, lat_d, reg = s_["hidT"][ex % 2], s_["idx"], s_["gsel"], s_["G2"], s_["lat_d"], s_["reg"]
                        for jt in range(NJT):
                            col = ex * NJT + jt
                            p = self.psum()
                            for fc in range(8):
                                for h in range(2):
                                    k.op("pe", lambda e: e.matmul(p[0:CW, h * 512:(h + 1) * 512], lhsT=hidT[:, fc, jt * CW:(jt + 1) * CW],
                                                                  rhs=wd[:, fc, h * 512:(h + 1) * 512], start=(fc == 0), stop=(fc == 7)),
                                         reads=[hidT, wd], writes=[p])
                            y_ = s_["ys"][s_["nys"] % 2]
                            s_["nys"] += 1
                            k.op("dve", lambda e: e.scalar_tensor_tensor(out=y_[0:CW, :], in0=p[0:CW, :], scalar=gsel[0:CW, col:col + 1],
                                                                         in1=G2[0:CW, :], op0=ALU.mult, op1=ALU.mult),
                                 reads=[p, gsel, G2], writes=[y_])
                            k.dma("pool", lambda e: e.indirect_dma_start(
                                out=lat_d[:, :], out_offset=bass.IndirectOffsetOnAxis(ap=idx[0:CW, col:col + 1], axis=0),
                                in_=y_[0:CW, :], in_offset=None, bounds_check=reg, oob_is_err=False, compute_op=ALU.add),
                                reads=[idx, y_], writes=[lat_sc])
                self.k.barrier()

    def moe_phase(self, T, affT, h2_d, lat_d, G2, wg_d, wu_d, wd_d, tag, n_iter=30, dbg=None):
        cap = 2 * T // NE
        NJT = cap // min(cap, 128)
        with ExitStack() as ms:
            idx = self.sb(ms, "mo_idx", [128, NE * NJT], I32)
            gsel = self.sb(ms, "mo_gsel", [128, NE * NJT], F32)
            self.moe_route(T, affT, idx, gsel, n_iter=n_iter, dbg=dbg)
            self.moe_experts([dict(T=T, idx=idx, gsel=gsel, h2_d=h2_d, lat_d=lat_d, G2=G2)], wg_d, wu_d, wd_d)
            self.k.barrier()


class Attn(Moe):
    def phase_b_attn(self, hT, hcT, T, LC, w_qkv, qg_pm, kg_pm, rope_d, rotT_d, zT_d):
        k = self.k
        NK = LC + T
        NKC = NK // 128
        QB = min(512, T)
        with ExitStack() as ls:
            wq = self.sb(ls, "at_wq", [128, 8, 1536], BF16)
            for q in range(3):
                self.k.dma("pool", lambda e: e.dma_start(out=wq[:, :, q * 512:(q + 1) * 512],
                                                        in_=_pm_view(w_qkv)[:, :, q * 512:(q + 1) * 512]), writes=[wq])
            gq = self.sb(ls, "at_gq", [128, 1], F32)
            gk = self.sb(ls, "at_gk", [128, 1], F32)
            k.dma("sp", lambda e: e.dma_start(out=gq[:], in_=qg_pm), writes=[gq])
            k.dma("sp", lambda e: e.dma_start(out=gk[:], in_=kg_pm), writes=[gk])
            rotf = self.sb(ls, "at_rotf", [128, 128], F32)
            rot = self.sb(ls, "at_rot", [128, 128], BF16)
            k.dma("sp", lambda e: e.dma_start(out=rotf[:], in_=rotT_d), writes=[rotf])
            k.op("dve", lambda e: e.tensor_copy(out=rot[:], in_=rotf[:]), reads=[rotf], writes=[rot])
            negc = self.sb(ls, "at_negc", [128, 1], F32)
            ab = self.sb(ls, "at_ab", [128, 2], F32)
            k.op("act", lambda e: e.activation(out=ab[:, 0:1], in_=gq[:], func=AF.Abs), reads=[gq], writes=[ab])
            k.op("act", lambda e: e.activation(out=ab[:, 1:2], in_=gk[:], func=AF.Abs), reads=[gk], writes=[ab])
            p = self.psum()
            k.op("pe", lambda e: e.transpose(p[0:2, 0:128], ab[:, 0:2], self.ident[:]), reads=[ab, self.ident], writes=[p])
            mx = self.sb(ls, "at_mx", [2, 1], F32)
            k.op("dve", lambda e: e.tensor_reduce(out=mx[:], in_=p[0:2, 0:128], axis=AX.X, op=ALU.max), reads=[p], writes=[mx])
            mq = self.sb(ls, "at_mq", [128, 2], F32)
            for i in range(2):
                p = self.psum()
                sel = self.sb(ls, f"at_sel{i}", [2, 128], F32)
                k.op("dve", lambda e: e.memset(sel[:], 0.0), writes=[sel])
                k.op("dve", lambda e: e.tensor_scalar(out=sel[:], in0=self.ones_f[0:2, :], scalar1=self.ident[0:2, i:i + 1], scalar2=None,
                                                      op0=ALU.mult), reads=[self.ones_f, self.ident], writes=[sel])
                k.op("pe", lambda e: e.matmul(p[:, 0:1], lhsT=sel[:], rhs=mx[:], start=True, stop=True), reads=[sel, mx], writes=[p])
                k.op("dve", lambda e: e.tensor_copy(out=mq[:, i:i + 1], in_=p[:, 0:1]), reads=[p], writes=[mq])
            k.op("dve", lambda e: e.scalar_tensor_tensor(out=negc[:], in0=mq[:, 0:1], scalar=-math.sqrt(HD), in1=mq[:, 1:2],
                                                         op0=ALU.mult, op1=ALU.mult), reads=[mq], writes=[negc])

            kT = [self.sb(ls, f"at_kT{i}", [128, NK], BF16) for i in range(NKVH)]
            V = self.sb(ls, "at_V", [128, NKC, NKVH, 129], BF16)
            k.op("dve", lambda e: e.memset(V[:, :, :, 128:129], 1.0), writes=[V])
            sq = [self.sb(ls, "at_sq0", [128, QB], F32)] * 2
            rst = [self.sb(ls, f"at_rst{i}", [128, QB], F32) for i in range(2)]
            xn = [self.sb(ls, f"at_xn{i}", [128, QB], F32) for i in range(2)]
            xnb = [self.sb(ls, f"at_xnb{i}", [128, QB], BF16) for i in range(2)]
            rp = [self.sb(ls, f"at_rp{i}", [128, 2, QB], F32) for i in range(2)]
            tmp = [self.sb(ls, "at_tmp0", [128, QB], F32)] * 2
            cnt = [0]

            def qk_block(srcT, col0, w, wcol, g, dst_ap, rope_blk):
                i = cnt[0] % 2
                cnt[0] += 1
                pq = self.psum()
                for c in range(8):
                    k.op("pe", lambda e: e.matmul(pq[:, 0:w], lhsT=wq[:, c, wcol:wcol + 128], rhs=srcT[:, c, col0:col0 + w],
                                                  start=(c == 0), stop=(c == 7)), reads=[wq, srcT], writes=[pq])
                k.op("act", lambda e: e.activation(out=sq[i][:, 0:w], in_=pq[:, 0:w], func=AF.Square), reads=[pq], writes=[sq[i]])
                k.op("pe", lambda e: e.matmul(pq[:, 512:512 + w], lhsT=self.ones_f[:], rhs=sq[i][:, 0:w], start=True, stop=True),
                     reads=[self.ones_f, sq[i]], writes=[pq])
                k.op("act", lambda e: e.activation(out=rst[i][:, 0:w], in_=pq[:, 512:512 + w], func=AF.Sqrt, bias=self.eps_t[:, 0:1],
                                                   scale=1.0 / HD), reads=[pq, self.eps_t], writes=[rst[i]])
                k.op("dve", lambda e: e.reciprocal(out=rst[i][:, 0:w], in_=rst[i][:, 0:w]), reads=[rst[i]], writes=[rst[i]])
                if rope_blk is None:
                    k.op("dve", lambda e: e.scalar_tensor_tensor(out=dst_ap[0], in0=pq[:, 0:w], scalar=g[:, 0:1], in1=rst[i][:, 0:w],
                                                                 op0=ALU.mult, op1=ALU.mult), reads=[pq, g, rst[i]], writes=[dst_ap[1]])
                    return
                k.op("dve", lambda e: e.scalar_tensor_tensor(out=xn[i][:, 0:w], in0=pq[:, 0:w], scalar=g[:, 0:1], in1=rst[i][:, 0:w],
                                                             op0=ALU.mult, op1=ALU.mult), reads=[pq, g, rst[i]], writes=[xn[i]])
                k.op("act", lambda e: e.copy(out=xnb[i][:, 0:w], in_=xn[i][:, 0:w]), reads=[xn[i]], writes=[xnb[i]])
                pr = self.psum()
                k.op("pe", lambda e: e.matmul(pr[:, 0:w], lhsT=rot[:], rhs=xnb[i][:, 0:w], start=True, stop=True),
                     reads=[rot, xnb[i]], writes=[pr])
                k.op("dve", lambda e: e.tensor_tensor(out=tmp[i][:, 0:w], in0=pr[:, 0:w], in1=rope_blk[:, 1, 0:w], op=ALU.mult),
                     reads=[pr, rope_blk], writes=[tmp[i]])
                k.op("pool", lambda e: e.tensor_tensor(out=xn[i][:, 0:w], in0=xn[i][:, 0:w], in1=rope_blk[:, 0, 0:w], op=ALU.mult),
                     reads=[xn[i], rope_blk], writes=[xn[i]])
                k.op("dve", lambda e: e.tensor_tensor(out=dst_ap[0], in0=xn[i][:, 0:w], in1=tmp[i][:, 0:w], op=ALU.add),
                     reads=[xn[i], tmp[i]], writes=[dst_ap[1]])

            def v_tile(srcT, col0, kc):
                pv = self.psum()
                for c in range(8):
                    k.op("pe", lambda e: e.matmul(pv[:, 0:256], lhsT=srcT[:, c, col0:col0 + 128], rhs=wq[:, c, 1280:1536],
                                                  start=(c == 0), stop=(c == 7)), reads=[wq, srcT], writes=[pv])
                k.op("act", lambda e: e.copy(out=V[:, kc, :, 0:128], in_=pv[:, 0:256].rearrange("p (a d) -> p a d", a=NKVH)),
                     reads=[pv], writes=[V])

            for b0 in range(0, LC, QB):
                w = min(QB, LC - b0)
                for kv in range(NKVH):
                    qk_block(hcT, b0, w, 1024 + kv * 128, gk, (kT[kv][:, b0:b0 + w], kT[kv]), None)
            for t0 in range(0, LC, 128):
                v_tile(hcT, t0, t0 // 128)
            for qb in range(T // QB):
                rb = rp[qb % 2]
                k.dma("sp", lambda e: e.dma_start(out=rb[:, :, 0:QB], in_=rope_d[:, :, qb * QB:(qb + 1) * QB].rearrange("a p t -> p a t")),
                      writes=[rb])
                for kv in range(NKVH):
                    qk_block(hT, qb * QB, QB, 1024 + kv * 128, gk, (kT[kv][:, LC + qb * QB:LC + (qb + 1) * QB], kT[kv]), rb)
            for t0 in range(0, T, 128):
                v_tile(hT, t0, (LC + t0) // 128)
            NS = QB // 128
            qTb = [self.sb(ls, f"at_qT{i}", [128, QB], BF16) for i in range(2)]
            pT = [self.sb(ls, f"at_pT{i}", [128, 2 * QB], BF16) for i in range(3)]
            rden = [self.sb(ls, f"at_rden{i}", [128, 1], F32) for i in range(2)]
            otok = [self.sb(ls, f"at_ot{i}", [128, 1024], BF16) for i in range(2 * NS)]
            scale = 1.0 / math.sqrt(HD)
            groups = [(g0, min(2, NKC - g0)) for g0 in range(0, NKC, 2)]
            n = 0
            items = [(qb, h) for qb in range(T // QB) for h in range(NQH)]

            def prep(i):
                qb, h = items[i]
                rb = rp[qb % 2]
                if h == 0:
                    k.dma("sp", lambda e: e.dma_start(out=rb[:, :, 0:QB], in_=rope_d[:, :, qb * QB:(qb + 1) * QB].rearrange("a p t -> p a t")),
                          writes=[rb])
                qT = qTb[i % 2]
                qk_block(hT, qb * QB, QB, h * 128, gq, (qT[:, 0:QB], qT), rb)

            prep(0)
            for it_, (qb, h) in enumerate(items):
                if True:
                    kv = h // (NQH // NKVH)
                    qT = qTb[it_ % 2]
                    if it_ + 1 < len(items):
                        prep(it_ + 1)
                    accs = [self.ps[0], self.ps[1]]

                    def emit_s(gi):
                        g0, ng = groups[gi]
                        st_ = self.ps[2 + (gi % 2)]
                        for j in range(ng):
                            kc = g0 + j
                            k.op("pe", lambda e: e.matmul(st_[:, j * 512:j * 512 + QB], lhsT=kT[kv][:, kc * 128:(kc + 1) * 128], rhs=qT[:, 0:QB],
                                                          start=True, stop=True), reads=[kT[kv], qT], writes=[st_])
                        return st_

                    sts = {0: emit_s(0)}
                    for gi, (g0, ng) in enumerate(groups):
                        if gi + 1 < len(groups):
                            sts[gi + 1] = emit_s(gi + 1)
                        st_ = sts.pop(gi)
                        pt = pT[n % 3]
                        n += 1
                        if QB == 512:
                            k.op("act", lambda e: e.activation(out=pt[:, 0:ng * 512], in_=st_[:, 0:ng * 512], func=AF.Exp, bias=negc[:, 0:1], scale=scale),
                                 reads=[st_, negc], writes=[pt])
                        else:
                            for j in range(ng):
                                k.op("act", lambda e: e.activation(out=pt[:, j * QB:(j + 1) * QB], in_=st_[:, j * 512:j * 512 + QB], func=AF.Exp,
                                                                   bias=negc[:, 0:1], scale=scale), reads=[st_, negc], writes=[pt])
                        for j in range(ng):
                            kc = g0 + j
                            for s_ in range(NS):
                                a_ = accs[s_ // 2]
                                c0 = (s_ % 2) * 512
                                k.op("pe", lambda e: e.matmul(a_[:, c0:c0 + 129], lhsT=pt[:, j * QB + s_ * 128:j * QB + (s_ + 1) * 128],
                                                              rhs=V[:, kc, kv, :], start=(kc == 0), stop=(kc == NKC - 1)),
                                     reads=[pt, V], writes=[a_])
                    for s_ in range(NS):
                        a_ = accs[s_ // 2]
                        c0 = (s_ % 2) * 512
                        rd = rden[s_ % 2]
                        ot = otok[(qb % 2) * NS + s_]
                        k.op("dve", lambda e: e.reciprocal(out=rd[:], in_=a_[:, c0 + 128:c0 + 129]), reads=[a_], writes=[rd])
                        k.op("dve", lambda e: e.tensor_scalar(out=ot[:, h * 128:(h + 1) * 128], in0=a_[:, c0:c0 + 128], scalar1=rd[:, 0:1],
                                                              scalar2=None, op0=ALU.mult), reads=[a_, rd], writes=[ot])
                for s_ in range(NS if h == NQH - 1 else 0):
                    ot = otok[(qb % 2) * NS + s_]
                    r0 = qb * QB + s_ * 128
                    k.dma("sp", lambda e: e.dma_start(out=zT_d[r0:r0 + 128, :], in_=ot[:]), reads=[ot])
            self.k.barrier()


TWO_PI = 2.0 * math.pi


class Hyena(Attn):
    def hy_filter(self, L, cst, wts, scr, fft=None):
        k = self.k
        NLT = L // 128
        NFT = NLT
        LB = min(512, L)
        with ExitStack() as fs:
            rS = [self.sb(fs, f"hf_rS{o}", [128, 1024], F32) for o in range(2)]
            with ExitStack() as ls:
                zT = self.sb(ls, "hf_zT", [33, L], F32)
                w1 = self.sb(ls, "hf_w1", [33, 64], F32)
                w2 = self.sb(ls, "hf_w2", [64, 64], F32)
                w3 = self.sb(ls, "hf_w3", [64, 4096], F32)
                sm = self.sb(ls, "hf_sm", [64, 4], F32)
                fb = self.sb(ls, "hf_fb", [64, 2], F32)
                a1T = self.sb(ls, "hf_a1T", [64, L], F32)
                a2T = self.sb(ls, "hf_a2T", [64, L + 1], F32)
                arg = self.sb(ls, "hf_arg", [64, LB], F32)
                m1 = self.sb(ls, "hf_m1", [64, LB], F32)
                k.dma("sp", lambda e: e.dma_start(out=zT[:], in_=cst["zT"]), writes=[zT])
                k.dma("sp", lambda e: e.dma_start(out=w1[:], in_=wts["w1"]), writes=[w1])
                k.dma("sp", lambda e: e.dma_start(out=w2[:], in_=wts["w2"]), writes=[w2])
                k.dma("sp", lambda e: e.dma_start(out=w3[:], in_=wts["w3"]), writes=[w3])
                k.dma("sp", lambda e: e.dma_start(out=sm[:, 0:1], in_=wts["b1"]), writes=[sm])
                k.dma("sp", lambda e: e.dma_start(out=sm[:, 1:2], in_=wts["b2"]), writes=[sm])
                k.dma("sp", lambda e: e.dma_start(out=sm[:, 2:3], in_=wts["freq"]), writes=[sm])
                k.op("dve", lambda e: e.tensor_scalar(out=fb[:], in0=sm[:, 0:2], scalar1=sm[:, 2:3], scalar2=None, op0=ALU.mult),
                     reads=[sm], writes=[fb])
                k.op("dve", lambda e: e.memset(a2T[:, L:L + 1], 0.0), writes=[a2T])
                for (wl, kdim, src, dst, bi) in ((w1, 33, zT, a1T, 0), (w2, 64, a1T, a2T, 1)):
                    for b0 in range(0, L, LB):
                        p = self.psum()
                        k.op("pe", lambda e: e.matmul(p[0:64, 0:LB], lhsT=wl[0:kdim, :], rhs=src[0:kdim, b0:b0 + LB], start=True, stop=True),
                             reads=[wl, src], writes=[p])
                        k.op("dve", lambda e: e.tensor_scalar(out=arg[:], in0=p[0:64, 0:LB], scalar1=sm[:, 2:3], scalar2=fb[:, bi:bi + 1],
                                                              op0=ALU.mult, op1=ALU.add), reads=[p, sm, fb], writes=[arg])
                        k.op("dve", lambda e: e.tensor_scalar(out=m1[:], in0=arg[:], scalar1=math.pi, scalar2=-TWO_PI,
                                                              op0=ALU.is_gt, op1=ALU.mult), reads=[arg], writes=[m1])
                        k.op("dve", lambda e: e.tensor_tensor(out=m1[:], in0=m1[:], in1=arg[:], op=ALU.add), reads=[m1, arg], writes=[m1])
                        k.op("dve", lambda e: e.tensor_scalar(out=arg[:], in0=arg[:], scalar1=-math.pi, scalar2=TWO_PI,
                                                              op0=ALU.is_lt, op1=ALU.mult), reads=[arg], writes=[arg])
                        k.op("dve", lambda e: e.tensor_tensor(out=arg[:], in0=m1[:], in1=arg[:], op=ALU.add), reads=[m1, arg], writes=[arg])
                        k.op("act", lambda e: e.activation(out=dst[:, b0:b0 + LB], in_=arg[:], func=AF.Sin), reads=[arg], writes=[dst])
                a2b = self.sb(ls, "hf_a2b", [64, L + 1], BF16)
                w3b = self.sb(ls, "hf_w3b", [64, 4096], BF16)
                k.op("dve", lambda e: e.tensor_copy(out=a2b[:], in_=a2T[:]), reads=[a2T], writes=[a2b])
                k.op("act", lambda e: e.copy(out=w3b[:], in_=w3[:]), reads=[w3], writes=[w3b])
                dl = self.sb(ls, "hf_dl", [128, 1024], F32)
                tl = self.sb(ls, "hf_tl", [128, NLT, 2], F32)
                k.dma("sp", lambda e: e.dma_start(out=dl[:], in_=cst["delta"][0:1, :].partition_broadcast(128)), writes=[dl])
                k.dma("sp", lambda e: e.dma_start(out=tl[:], in_=cst["tl"]), writes=[tl])
                dec = [[self.sb(ls, f"hf_dec{i}{j}", [128, 1024], F32) for j in range(2)] for i in range(2)]
                fbb = [[self.sb(ls, f"hf_f{i}{j}", [128, 1024], F32) for j in range(2)] for i in range(2)]
                abb = [[self.sb(ls, f"hf_a{i}{j}", [128, 1024], BF16) for j in range(2)] for i in range(2)]
                kpm = [self.sb(ls, f"hf_kpm{i}", [128, 1024], BF16) for i in range(4)]
                Sps = self.ps[3]
                nps = 0
                pend = []

                def flush_one():
                    a_s, lt_s = pend.pop(0)
                    for h in range(2):
                        k.op("pe", lambda e: e.matmul(Sps[:, h * 512:(h + 1) * 512], lhsT=self.ones_b[:], rhs=a_s[:, h * 512:(h + 1) * 512],
                                                      start=(lt_s == 0), stop=(lt_s == NLT - 1)), reads=[self.ones_b, a_s], writes=[Sps])

                for o in range(2):
                    for lt in range(NLT):
                        par = lt % 2
                        for dr in (0, 1):
                            p = self.ps[nps % 3]
                            nps += 1
                            c0 = dr * 2048 + o * 1024
                            for h in range(2):
                                k.op("pe", lambda e: e.matmul(p[:, h * 512:(h + 1) * 512], lhsT=a2b[:, lt * 128 + dr:lt * 128 + dr + 128],
                                                              rhs=w3b[:, c0 + h * 512:c0 + (h + 1) * 512], start=True, stop=True),
                                     reads=[a2b, w3b], writes=[p])
                            d_, f_, a_ = dec[par][dr], fbb[par][dr], abb[par][dr]
                            k.op("act", lambda e: e.activation(out=d_[:], in_=dl[:], func=AF.Exp, scale=tl[:, lt, dr:dr + 1]),
                                 reads=[dl, tl], writes=[d_])
                            k.op("dve", lambda e: e.tensor_tensor(out=f_[:], in0=p[:], in1=d_[:], op=ALU.mult),
                                 reads=[p, d_], writes=[f_])
                            k.op("act", lambda e: e.activation(out=a_[:], in_=f_[:], func=AF.Abs), reads=[f_], writes=[a_])
                        f0, f1, a0, a1 = fbb[par][0], fbb[par][1], abb[par][0], abb[par][1]
                        kp_, km_ = kpm[par * 2], kpm[par * 2 + 1]
                        k.op("dve", lambda e: e.tensor_tensor(out=kp_[:], in0=f0[:], in1=f1[:], op=ALU.add), reads=[f0, f1], writes=[kp_])
                        k.op("dve", lambda e: e.tensor_tensor(out=km_[:], in0=f0[:], in1=f1[:], op=ALU.subtract), reads=[f0, f1], writes=[km_])
                        k.dma("sp", lambda e: e.dma_start(out=scr["kp"][o, lt * 128:(lt + 1) * 128, :], in_=kp_[:]), reads=[kp_])
                        k.dma("sp", lambda e: e.dma_start(out=scr["km"][o, lt * 128:(lt + 1) * 128, :], in_=km_[:]), reads=[km_])
                        k.op("pool", lambda e: e.tensor_tensor(out=a0[:], in0=a0[:], in1=a1[:], op=ALU.add), reads=[a0, a1], writes=[a0])
                        pend.append((a0, lt))
                        if len(pend) > 1:
                            flush_one()
                    while pend:
                        flush_one()
                    k.op("dve", lambda e: e.reciprocal(out=rS[o][:], in_=Sps[:]), reads=[Sps], writes=[rS[o]])
                self.k.barrier()
            if fft is not None:
                with ExitStack() as ls:
                    W1 = self.sb(ls, "hf_W1", [64, 128], BF16)
                    Tf = self.sb(ls, "hf_Tf", [128, 4, 64, 128], BF16)
                    sk = self.sb(ls, "hf_skf", [128, 1024], F32)
                    k.dma("sp", lambda e: e.dma_start(out=W1[:], in_=fft["W1"]), writes=[W1])
                    k.dma("sp", lambda e: e.dma_start(out=Tf[:], in_=fft["Tf"]), writes=[Tf])
                    for o in range(2):
                        k.dma("sp", lambda e: e.dma_start(out=sk[:], in_=wts["skip"][o:o + 1, :].partition_broadcast(128)), writes=[sk])
                        self.fft_stage1(scr["kp"][o], fft["Gp"], W1)
                        self.fft_stage1(scr["km"][o], fft["Gm"], W1)
                        self.fft_filter_stage2(fft["Gp"], fft["Gm"], Tf, rS[o], sk, fft["KA"][o], fft["KB"][o])
                    self.k.barrier()
                return
            with ExitStack() as ls:
                src = self.sb(ls, "hf_src", [128, NLT, 1024], BF16)
                mb = [self.sb(ls, f"hf_mb{i}", [128, NLT, 128], BF16) for i in range(2)]
                ph = self.sb(ls, "hf_ph", [128, NFT, 2], F32)
                sk = self.sb(ls, "hf_sk", [128, 1024], F32)
                At = [self.sb(ls, f"hf_At{i}", [128, 1024], F32) for i in range(2)]
                t1 = self.sb(ls, "hf_t1", [128, 1024], F32)
                t2 = self.sb(ls, "hf_t2", [128, 1024], F32)
                Ko = [self.sb(ls, f"hf_Ko{i}", [128, 1024], BF16) for i in range(4)]
                k.dma("sp", lambda e: e.dma_start(out=ph[:], in_=cst["ph"]), writes=[ph])
                for o in range(2):
                    k.dma("sp", lambda e: e.dma_start(out=sk[:], in_=wts["skip"][o:o + 1, :].partition_broadcast(128)), writes=[sk])
                    for pas in range(2):
                        sd = scr["kp"] if pas == 0 else scr["km"]
                        k.dma("sp", lambda e: e.dma_start(out=src[:], in_=sd[o].rearrange("(c p) n -> p c n", p=128)), writes=[src])
                        for ft in range(NFT):
                            m = mb[ft % 2]
                            k.dma("sp", lambda e: e.dma_start(out=m[:], in_=cst["Mb"][pas, ft]), writes=[m])
                            p = self.psum()
                            for h in range(2):
                                for c in range(NLT):
                                    k.op("pe", lambda e: e.matmul(p[:, h * 512:(h + 1) * 512], lhsT=m[:, c, :], rhs=src[:, c, h * 512:(h + 1) * 512],
                                                                  start=(c == 0), stop=(c == NLT - 1)), reads=[m, src], writes=[p])
                            rows = slice(ft * 128, (ft + 1) * 128)
                            if pas == 0:
                                a_ = At[ft % 2]
                                k.op("act", lambda e: e.copy(out=a_[:], in_=p[:]), reads=[p], writes=[a_])
                                k.dma("sp", lambda e: e.dma_start(out=scr["A"][rows, :], in_=a_[:]), reads=[a_])
                            else:
                                a_ = At[ft % 2]
                                kc_, ks_ = Ko[(ft % 2) * 2], Ko[(ft % 2) * 2 + 1]
                                cph, sph = ph[:, ft, 0:1], ph[:, ft, 1:2]
                                k.dma("sp", lambda e: e.dma_start(out=a_[:], in_=scr["A"][rows, :]), writes=[a_])
                                k.op("act", lambda e: e.activation(out=t1[:], in_=a_[:], func=AF.Copy, scale=cph),
                                     reads=[a_, ph], writes=[t1])
                                k.op("dve", lambda e: e.scalar_tensor_tensor(out=t1[:], in0=p[:], scalar=sph, in1=t1[:], op0=ALU.mult, op1=ALU.add),
                                     reads=[p, ph, t1], writes=[t1])
                                k.op("pool", lambda e: e.tensor_tensor(out=t1[:], in0=t1[:], in1=rS[o][:], op=ALU.mult), reads=[t1, rS[o]], writes=[t1])
                                k.op("pool", lambda e: e.tensor_tensor(out=kc_[:], in0=t1[:], in1=sk[:], op=ALU.add), reads=[t1, sk], writes=[kc_])
                                k.op("act", lambda e: e.activation(out=t2[:], in_=a_[:], func=AF.Copy, scale=sph),
                                     reads=[a_, ph], writes=[t2])
                                k.op("dve", lambda e: e.scalar_tensor_tensor(out=t2[:], in0=p[:], scalar=cph, in1=t2[:], op0=ALU.mult, op1=ALU.subtract),
                                     reads=[p, ph, t2], writes=[t2])
                                k.op("dve", lambda e: e.tensor_tensor(out=ks_[:], in0=t2[:], in1=rS[o][:], op=ALU.mult), reads=[t2, rS[o]], writes=[ks_])
                                k.dma("sp", lambda e: e.dma_start(out=scr["K"][o, 0, rows, :], in_=kc_[:]), reads=[kc_])
                                k.dma("sp", lambda e: e.dma_start(out=scr["K"][o, 1, rows, :], in_=ks_[:]), reads=[ks_])
                        self.k.barrier()
                self.k.barrier()
            self.k.barrier()

    def hy_proj(self, hT, T, w_in, cw_pm, src, g_d, z_d=None):
        k = self.k
        TB = min(512, T)
        NLT = T // 128
        with ExitStack() as ls:
            cw = self.sb(ls, "hp_cw", [128, 24, 3], F32)
            k.dma("sp", lambda e: e.dma_start(out=cw[:], in_=cw_pm), writes=[cw])
            wq = [self.sb(ls, f"hp_w{i}", [128, 8, 128], BF16) for i in range(2)]
            u = self.sb(ls, "hp_u", [128, T + 2], F32)
            acc = self.sb(ls, "hp_acc", [128, T + 1], F32)
            k.op("dve", lambda e: e.memset(acc[:, 0:1], 0.0), writes=[acc])
            row = self.sb(ls, "hp_row", [128, T], BF16)
            k.op("dve", lambda e: e.memset(u[:, 0:1], 0.0), writes=[u])
            k.op("dve", lambda e: e.memset(u[:, T + 1:T + 2], 0.0), writes=[u])
            n = 0
            for q in (1, 2, 0):
                for j in range(8):
                    w = wq[n % 2]
                    n += 1
                    self.load_w_bf(w, w_in, q * 1024 + j * 128, 128)
                    for tb in range(T // TB):
                        p = self.psum()
                        for c in range(8):
                            k.op("pe", lambda e: e.matmul(p[:, 0:TB], lhsT=w[:, c, :], rhs=hT[:, c, tb * TB:(tb + 1) * TB],
                                                          start=(c == 0), stop=(c == 7)), reads=[w, hT], writes=[p])
                        ci_ = q * 8 + j
                        k.op("act", lambda e: e.copy(out=u[:, 1 + tb * TB:1 + (tb + 1) * TB], in_=p[:, 0:TB]), reads=[p], writes=[u])
                        if tb == 0:
                            k.op("dve", lambda e: e.memset(acc[:, 0:1], 0.0), writes=[acc])
                        k.op("act", lambda e: e.activation(out=acc[:, 1 + tb * TB:1 + (tb + 1) * TB], in_=p[:, 0:TB], func=AF.Identity,
                                                           scale=cw[:, ci_, 0:1]), reads=[p, cw], writes=[acc])
                    ci = q * 8 + j
                    k.op("dve", lambda e: e.scalar_tensor_tensor(out=acc[:, 0:T], in0=u[:, 1:T + 1], scalar=cw[:, ci, 1:2], in1=acc[:, 0:T],
                                                                 op0=ALU.mult, op1=ALU.add), reads=[u, cw, acc], writes=[acc])
                    k.op("dve", lambda e: e.scalar_tensor_tensor(out=row[:], in0=u[:, 2:T + 2], scalar=cw[:, ci, 2:3], in1=acc[:, 0:T],
                                                                 op0=ALU.mult, op1=ALU.add), reads=[u, cw, acc], writes=[row])
                    for l0 in range(0, NLT, 8):
                        nl = min(8, NLT - l0)
                        p = self.psum()
                        pv = pbf(p)[:, 0:1024].rearrange("p (c t) -> p c t", c=8)
                        for i in range(nl):
                            lt = l0 + i
                            k.op("pe", lambda e: e.transpose(pv[:, i, :], row[:, lt * 128:(lt + 1) * 128], self.identb[:]),
                                 reads=[row, self.identb], writes=[p])
                        k.op("act", lambda e: e.copy(out=src[:, l0:l0 + nl, j * 128:(j + 1) * 128], in_=pv[:, 0:nl, :]), reads=[p], writes=[src])
                if q != 0:
                    k.dma("sp", lambda e: e.dma_start(out=g_d[q - 1].rearrange("(c p) n -> p c n", p=128), in_=src[:]), reads=[src])
                elif z_d is not None:
                    k.dma("sp", lambda e: e.dma_start(out=z_d.rearrange("(c p) n -> p c n", p=128), in_=src[:]), reads=[src])
            self.k.barrier()

    def hy_conv(self, L, src, cst, K_d, g_d, Y_d, z_d, o):
        k = self.k
        NLT = L // 128
        NFT = NLT
        with ExitStack() as ls:
            mb = [self.sb(ls, f"hc_mb{i}", [128, NLT, 128], BF16) for i in range(4)]
            Kt = [self.sb(ls, f"hc_K{i}", [128, 1024], BF16) for i in range(4)]
            tt_ = [self.sb(ls, f"hc_t{i}", [128, 1024], F32) for i in range(4)]
            Yo = [self.sb(ls, f"hc_Y{i}", [128, 1024], BF16) for i in range(4)]
            for ft in range(NFT):
                mc, ms = mb[(ft % 2) * 2], mb[(ft % 2) * 2 + 1]
                kc_, ks_ = Kt[(ft % 2) * 2], Kt[(ft % 2) * 2 + 1]
                rows = slice(ft * 128, (ft + 1) * 128)
                k.dma("sp", lambda e: e.dma_start(out=mc[:], in_=cst["Mb"][0, ft]), writes=[mc])
                k.dma("sp", lambda e: e.dma_start(out=ms[:], in_=cst["Mb"][1, ft]), writes=[ms])
                k.dma("sp", lambda e: e.dma_start(out=kc_[:], in_=K_d[o, 0, rows, :]), writes=[kc_])
                k.dma("sp", lambda e: e.dma_start(out=ks_[:], in_=K_d[o, 1, rows, :]), writes=[ks_])
                pc, ps_ = self.psum(), self.psum()
                for (p, m) in ((pc, mc), (ps_, ms)):
                    for h in range(2):
                        for c in range(NLT):
                            k.op("pe", lambda e: e.matmul(p[:, h * 512:(h + 1) * 512], lhsT=m[:, c, :], rhs=src[:, c, h * 512:(h + 1) * 512],
                                                          start=(c == 0), stop=(c == NLT - 1)), reads=[m, src], writes=[p])
                yc, ys = Yo[(ft % 2) * 2], Yo[(ft % 2) * 2 + 1]
                t1, t2, t3, t4 = tt_
                k.op("dve", lambda e: e.tensor_tensor(out=t1[:], in0=pc[:], in1=kc_[:], op=ALU.mult), reads=[pc, kc_], writes=[t1])
                k.op("dve", lambda e: e.tensor_tensor(out=t2[:], in0=ps_[:], in1=ks_[:], op=ALU.mult), reads=[ps_, ks_], writes=[t2])
                k.op("pool", lambda e: e.tensor_tensor(out=yc[:], in0=t1[:], in1=t2[:], op=ALU.subtract), reads=[t1, t2], writes=[yc])
                k.op("dve", lambda e: e.tensor_tensor(out=t3[:], in0=pc[:], in1=ks_[:], op=ALU.mult), reads=[pc, ks_], writes=[t3])
                k.op("dve", lambda e: e.tensor_tensor(out=t4[:], in0=ps_[:], in1=kc_[:], op=ALU.mult), reads=[ps_, kc_], writes=[t4])
                k.op("pool", lambda e: e.tensor_tensor(out=ys[:], in0=t3[:], in1=t4[:], op=ALU.add), reads=[t3, t4], writes=[ys])
                k.dma("sp", lambda e: e.dma_start(out=Y_d[0, rows, :], in_=yc[:]), reads=[yc])
                k.dma("sp", lambda e: e.dma_start(out=Y_d[1, rows, :], in_=ys[:]), reads=[ys])
            self.k.barrier()
            srcv = src.t[:].rearrange("p c n -> p (c n)")
            Ych = srcv[:, 0:NFT * 512].rearrange("p (c n) -> p c n", n=512)
            Ysh = srcv[:, NFT * 512:2 * NFT * 512].rearrange("p (c n) -> p c n", n=512)
            gt = [self.sb(ls, f"hc_g{i}", [128, 512], BF16) for i in range(2)]
            zo = [self.sb(ls, f"hc_z{i}", [128, 512], BF16) for i in range(2)]
            for hh in range(2):
                cs = slice(hh * 512, (hh + 1) * 512)
                k.dma("sp", lambda e: e.dma_start(out=Ych, in_=Y_d[0].rearrange("(c p) n -> p c n", p=128)[:, :, cs]), writes=[src])
                k.dma("sp", lambda e: e.dma_start(out=Ysh, in_=Y_d[1].rearrange("(c p) n -> p c n", p=128)[:, :, cs]), writes=[src])
                for tt in range(NLT):
                    mc, ms = mb[(tt % 2) * 2], mb[(tt % 2) * 2 + 1]
                    rows = slice(tt * 128, (tt + 1) * 128)
                    g_ = gt[tt % 2]
                    z_ = zo[tt % 2]
                    k.dma("sp", lambda e: e.dma_start(out=mc[:], in_=cst["Mb"][0, tt]), writes=[mc])
                    k.dma("sp", lambda e: e.dma_start(out=ms[:], in_=cst["Mb"][1, tt]), writes=[ms])
                    k.dma("sp", lambda e: e.dma_start(out=g_[:], in_=g_d[o, rows, cs]), writes=[g_])
                    p = self.psum()
                    for (m, Yh, first) in ((mc, Ych, True), (ms, Ysh, False)):
                        for c in range(NFT):
                            k.op("pe", lambda e: e.matmul(p[:, 0:512], lhsT=m[:, c, :], rhs=Yh[:, c, :],
                                                          start=(first and c == 0), stop=((not first) and c == NFT - 1)),
                                 reads=[m, src], writes=[p])
                    k.op("dve", lambda e: e.scalar_tensor_tensor(out=z_[:], in0=p[:, 0:512], scalar=1.0 / L, in1=g_[:],
                                                                 op0=ALU.mult, op1=ALU.mult), reads=[p, g_], writes=[z_])
                    k.dma("sp", lambda e: e.dma_start(out=z_d[rows, cs], in_=z_[:]), reads=[z_])
                self.k.barrier()
            self.k.barrier()


def hy_consts(L, with_M=True):
    import ml_dtypes
    N = 2 * L
    f32 = np.float32
    t = np.linspace(0.0, 1.0, L, dtype=f32)[:, None]
    bands = np.linspace(1e-4, 15.0, 16, dtype=f32)[None, :]
    ang = f32(2.0 * math.pi / L) * np.arange(L, dtype=f32)[:, None] * bands
    z = np.concatenate([t, np.cos(ang), -np.sin(ang)], axis=-1).astype(f32)
    zT = np.ascontiguousarray(z.T)
    NT_ = L // 128
    fi = np.arange(L, dtype=np.float64)
    th = 2.0 * np.pi * (fi + 0.5) / N
    angm = th[:, None] * (fi[None, :] + 0.5)
    Mb = np.empty((2, NT_, 128, NT_, 128) if with_M else (1,), dtype=ml_dtypes.bfloat16)
    for i, fn in enumerate((np.cos, np.sin) if with_M else ()):
        M = fn(angm).astype(f32)
        Mb[i] = M.reshape(NT_, 128, NT_, 128).transpose(0, 3, 2, 1).astype(ml_dtypes.bfloat16)
    ph = np.stack([np.cos(th / 2), np.sin(th / 2)], -1).astype(f32)
    ph = np.ascontiguousarray(ph.reshape(NT_, 128, 2).transpose(1, 0, 2))
    tl_ = np.linspace(0.0, 1.0, L, dtype=f32).astype(np.float64)
    tl1 = np.concatenate([tl_[1:], [L / (L - 1.0)]])
    tl = np.stack([-tl_, -tl1], -1).astype(f32)
    tl = np.ascontiguousarray(tl.reshape(NT_, 128, 2).transpose(1, 0, 2))
    delta = np.abs(np.linspace(math.log(1e-2) / 1.5, math.log(1e-2) / 0.3, D, dtype=f32)).reshape(1, D).astype(f32)
    return dict(zT=zT, Mb=Mb, ph=ph, tl=tl, delta=delta)


DEPTH = 4
GRID_W = 64


def rope_tables_np(T):
    f32 = np.float32
    rows = T // GRID_W
    row = np.repeat(np.arange(rows, dtype=f32), GRID_W)
    col = np.tile(np.arange(GRID_W, dtype=f32), rows)
    inv = (f32(10000.0) ** (-np.arange(0, 64, 2, dtype=f32) / f32(64))).astype(f32)

    def axis_angles(pos):
        a = pos[:, None] * inv[None, :]
        return np.concatenate([a, a], axis=-1)

    ang = np.concatenate([axis_angles(row), axis_angles(col)], axis=-1).astype(f32)
    return np.ascontiguousarray(np.stack([np.cos(ang).T, np.sin(ang).T]).astype(f32))


def rot_T_np():
    R = np.zeros((128, 128), np.float32)
    for base in (0, 64):
        for i in range(32):
            R[base + i, base + i + 32] = -1.0
            R[base + 32 + i, base + i] = 1.0
    return np.ascontiguousarray(R.T)


def build_model(T, LC, depth=DEPTH):
    nc = bass.Bass("TRN2", target_bir_lowering=False)

    def dt(n, s, d=F32, kind="ExternalInput"):
        return nc.dram_tensor(n, s, d, kind=kind).ap()

    nA = len(range(0, depth, 3))
    nB = len(range(1, depth, 3))
    nC = len(range(2, depth, 3))
    last_attn = max(list(range(2, depth, 3)), default=-1)
    I = {}
    I["x"] = dt("x", [T, D]); I["ctx"] = dt("ctx", [LC, D]); I["c_pm"] = dt("c_pm", [128, 8]); I["cc_pm"] = dt("cc_pm", [128, 8])
    I["ada_w"] = dt("ada_w", [depth, D, 6 * D]); I["ada_b"] = dt("ada_b", [depth, 6 * D]); I["norm_g"] = dt("norm_g", [depth, 2, D])
    I["sc_w_in"] = dt("sc_w_in", [nA, D, 3 * D]); I["sc_cw_pm"] = dt("sc_cw_pm", [nA, 128, 8, 3]); I["sc_w_out"] = dt("sc_w_out", [nA, D, D])
    if nB:
        I["hy_w_in"] = dt("hy_w_in", [nB, D, 3 * D]); I["hy_cw_pm"] = dt("hy_cw_pm", [nB, 128, 24, 3])
        I["hy_f_w1"] = dt("hy_f_w1", [nB, 33, 64]); I["hy_f_b1"] = dt("hy_f_b1", [nB, 64, 1]); I["hy_f_w2"] = dt("hy_f_w2", [nB, 64, 64])
        I["hy_f_b2"] = dt("hy_f_b2", [nB, 64, 1]); I["hy_f_w3"] = dt("hy_f_w3", [nB, 64, 4 * D]); I["hy_sin_freq"] = dt("hy_sin_freq", [nB, 64, 1])
        I["hy_skip"] = dt("hy_skip", [nB, 2, D]); I["hy_w_out"] = dt("hy_w_out", [nB, D, D])
        cst = {}
        use_fft = (T == 4096)
        for tag, L in (("l", T), ("c", LC)):
            n_ = L // 128
            cst[tag] = dict(zT=dt(f"hz_{tag}", [33, L]), ph=dt(f"hp_{tag}", [128, n_, 2]),
                            tl=dt(f"ht_{tag}", [128, n_, 2]), delta=dt(f"hd_{tag}", [1, D]))
            if not (use_fft and tag == "l"):
                cst[tag]["Mb"] = dt(f"hM_{tag}", [2, n_, 128, n_, 128], BF16)
        if use_fft:
            fc = dict(W1=dt("fW1", [64, 128], BF16), W4=dt("fW4", [128, 64], BF16), Td=dt("fTd", [128, 3, 64, 128], BF16),
                      Tf=dt("fTf", [128, 4, 64, 128], BF16))
    if nC:
        I["at_w_qkv"] = dt("at_w_qkv", [nC, D, 1536]); I["at_qg"] = dt("at_qg", [nC, 128, 1]); I["at_kg"] = dt("at_kg", [nC, 128, 1])
        I["at_w_o"] = dt("at_w_o", [nC, D, D]); I["rope"] = dt("rope", [2, 128, T]); I["rotT"] = dt("rotT", [128, 128])
    I["moe_router"] = dt("moe_router", [depth, D, NE]); I["moe_w_gate"] = dt("moe_w_gate", [depth, NE, D, D])
    I["moe_w_up"] = dt("moe_w_up", [depth, NE, D, D]); I["moe_w_down"] = dt("moe_w_down", [depth, NE, D, D])
    out = dt("out", [T, D], kind="ExternalOutput")
    cxs = dt("s_cxs", [LC, D], F32, "Internal")
    zT_d = dt("s_zT", [D, T], BF16, "Internal")
    h2_d = dt("s_h2", [T, D], BF16, "Internal")
    ztok_d = dt("s_ztok", [T, D], BF16, "Internal")
    rows_d = dt("s_adarows", [depth, 2, 6 * D], F32, "Internal")
    h2c_d = dt("s_h2c", [LC, D], BF16, "Internal")
    if nB:
        hs = dict(kp=dt("s_kp", [2, T, D], BF16, "Internal"), km=dt("s_km", [2, T, D], BF16, "Internal"), A=dt("s_A", [T, D], F32, "Internal"))
        Kl = dt("s_Kl", [2, 2, T, D], BF16, "Internal"); Kc_ = dt("s_Kc", [2, 2, LC, D], BF16, "Internal")
        g_d = dt("s_g", [2, T, D], BF16, "Internal"); Y_d = dt("s_Y", [2, T, D], BF16, "Internal")
        z1_d = dt("s_z1", [T, D], BF16, "Internal"); z2_d = dt("s_z2", [T, D], BF16, "Internal")
        if use_fft:
            Gp = dt("s_Gp", [2, 64, 64, D], BF16, "Internal"); Gm = dt("s_Gm", [2, 64, 64, D], BF16, "Internal")
            Zd = dt("s_Zd", [2, 64, 64, D], BF16, "Internal")
            KA = dt("s_KA", [2, 64, 128, D], BF16, "Internal"); KB = dt("s_KB", [2, 64, 128, D], BF16, "Internal")
            zt_d = dt("s_zt", [T, D], BF16, "Internal")

    P = HyFFT(nc, T, LC)
    with ExitStack() as st:
        P.setup_small(st)
        rep = {"l": 0, "c": 1}
        P.ada_precompute(I["c_pm"], I["cc_pm"], I["ada_w"], I["ada_b"], depth, rows_d)
        for i in range(depth):
            kind, j = i % 3, i // 3
            need_ctx, upd_ctx = i <= last_attn, i < last_attn
            streams = []
            if upd_ctx:
                streams.append(("c", LC, I["ctx"] if i == 0 else cxs, cxs))
            streams.append(("l", T, I["x"] if i == 0 else out, out))
            aw, ab, ng = I["ada_w"][i], I["ada_b"][i:i + 1, :], I["norm_g"][i]
            if kind == 1:
                wts = dict(w1=I["hy_f_w1"][j], b1=I["hy_f_b1"][j], w2=I["hy_f_w2"][j], b2=I["hy_f_b2"][j], w3=I["hy_f_w3"][j],
                           freq=I["hy_sin_freq"][j], skip=I["hy_skip"][j])
                Ks = {}
                for (tag, L, _, _) in streams:
                    scr = dict(kp=hs["kp"][:, 0:L, :], km=hs["km"][:, 0:L, :], A=hs["A"][0:L, :], K=(Kl if tag == "l" else Kc_))
                    if use_fft and tag == "l":
                        P.hy_filter(L, cst[tag], wts, scr, fft=dict(W1=fc["W1"], Tf=fc["Tf"], Gp=Gp, Gm=Gm, KA=KA, KB=KB))
                    else:
                        P.hy_filter(L, cst[tag], wts, scr)
                    Ks[tag] = scr["K"]
            lay = ExitStack()
            pers = {}
            for (tag, L, _, _) in streams:
                cap_ = 2 * L // NE
                njt_ = cap_ // min(cap_, 128)
                pers[tag] = dict(idx=P.sb(lay, f"idx{tag}", [128, NE * njt_], I32), gsel=P.sb(lay, f"gs{tag}", [128, NE * njt_], F32))
                if tag != streams[-1][0]:
                    pers[tag]["G2"] = P.sb(lay, f"G2{tag}", [128, 1024], F32)
            for (tag, L, src_ap, dst_ap) in streams:
                with ExitStack() as s1:
                    A1, B1, G1, A2, B2, G2 = P.ada_load(s1, rows_d[i, rep[tag]:rep[tag] + 1, :], ng, tag)
                    if kind == 0:
                        with ExitStack() as s2:
                            hT = P.phase_a(s2, src_ap, L, A1, B1, tag)
                            P.phase_b_conv(hT, L, I["sc_w_in"][j], I["sc_cw_pm"][j], zT_d[:, 0:L], tag)
                            P.end_phase(s2)
                        zsrc, wo, ztok = zT_d[:, 0:L], I["sc_w_out"][j], False
                    elif kind == 1 and use_fft and tag == "l":
                        n_ = L // 128
                        with ExitStack() as s2:
                            src = P.sb(s2, "hy_src", [128, n_, 1024], BF16)
                            with ExitStack() as s2b:
                                hT = P.phase_a(s2b, src_ap, L, A1, B1, tag)
                                P.hy_proj(hT, L, I["hy_w_in"][j], I["hy_cw_pm"][j], src, g_d[:, 0:L, :], z_d=zt_d)
                                P.end_phase(s2b)
                            P.end_phase(s2)
                        with ExitStack() as s2:
                            W1 = P.sb(s2, "fW1", [64, 128], BF16); W4 = P.sb(s2, "fW4", [128, 64], BF16)
                            Td = P.sb(s2, "fTd", [128, 3, 64, 128], BF16)
                            P.k.dma("sp", lambda e: e.dma_start(out=W1[:], in_=fc["W1"]), writes=[W1])
                            P.k.dma("sp", lambda e: e.dma_start(out=W4[:], in_=fc["W4"]), writes=[W4])
                            P.k.dma("sp", lambda e: e.dma_start(out=Td[:], in_=fc["Td"]), writes=[Td])
                            P.fft_conv(zt_d, Gp, Zd, W1, W4, Td, KA[0], KB[0], g_d[0], z1_d)
                            P.fft_conv(z1_d, Gp, Zd, W1, W4, Td, KA[1], KB[1], g_d[1], z2_d)
                            P.end_phase(s2)
                        zsrc, wo, ztok = z2_d[0:L, :], I["hy_w_out"][j], True
                    elif kind == 1:
                        n_ = L // 128
                        with ExitStack() as s2:
                            src = P.sb(s2, "hy_src", [128, n_, 1024], BF16)
                            with ExitStack() as s2b:
                                hT = P.phase_a(s2b, src_ap, L, A1, B1, tag)
                                P.hy_proj(hT, L, I["hy_w_in"][j], I["hy_cw_pm"][j], src, g_d[:, 0:L, :])
                                P.end_phase(s2b)
                            P.hy_conv(L, src, cst[tag], Ks[tag], g_d[:, 0:L, :], Y_d[:, 0:L, :], z1_d[0:L, :], 0)
                            P.k.dma("sp", lambda e: e.dma_start(out=src[:], in_=z1_d[0:L, :].rearrange("(c p) n -> p c n", p=128)), writes=[src])
                            P.hy_conv(L, src, cst[tag], Ks[tag], g_d[:, 0:L, :], Y_d[:, 0:L, :], z2_d[0:L, :], 1)
                            P.end_phase(s2)
                        zsrc, wo, ztok = z2_d[0:L, :], I["hy_w_out"][j], True
                    else:
                        with ExitStack() as s2:
                            hcT = P.sb(s2, "hcT", [128, 8, LC], BF16)
                            with ExitStack() as sc:
                                cA1, cB1, *_ = P.ada_load(sc, rows_d[i, 1:2, :], ng, "c")
                                P.phase_a(sc, I["ctx"] if i == 0 else cxs, LC, cA1, cB1, "c", hT=hcT)
                                P.end_phase(sc)
                            hT = P.phase_a(s2, src_ap, L, A1, B1, tag)
                            P.phase_b_attn(hT, hcT, L, LC, I["at_w_qkv"][j], I["at_qg"][j], I["at_kg"][j], I["rope"], I["rotT"], ztok_d)
                            P.end_phase(s2)
                        zsrc, wo, ztok = ztok_d, I["at_w_o"][j], True
                    h2x = (h2_d if tag == "l" else h2c_d)[0:L, :]
                    with ExitStack() as s3:
                        affT = P.phase_c(L, zsrc, wo, src_ap, dst_ap, G1, A2, B2, I["moe_router"][i], h2x, s3, tag, z_tok=ztok)
                        P.moe_route(L, affT, pers[tag]["idx"], pers[tag]["gsel"])
                        P.end_phase(s3)
                    pers[tag].update(T=L, h2_d=h2x, lat_d=dst_ap)
                    if tag != streams[-1][0]:
                        P.k.op("dve", lambda e: e.tensor_copy(out=pers[tag]["G2"][:], in_=G2[:]), reads=[G2], writes=[pers[tag]["G2"]])
                    else:
                        pers[tag]["G2"] = G2
                        P.moe_experts([pers[t_] for (t_, _, _, _) in streams], I["moe_w_gate"][i], I["moe_w_up"][i], I["moe_w_down"][i])
                    P.end_phase(s1)
            P.end_phase(lay)
    P.finish()
    return nc


def host_inputs(inp, b, T, LC, depth=DEPTH):
    f = lambda a: np.ascontiguousarray(np.asarray(a, dtype=np.float32))
    pm = lambda v: np.ascontiguousarray(np.asarray(v, np.float32).reshape(8, 128).T)
    m = {}
    m["x"] = f(inp["x"][b]); m["ctx"] = f(inp["ctx"][b]); m["c_pm"] = pm(inp["c"][b]); m["cc_pm"] = pm(inp["c_ctx"])
    for n in ("ada_w", "ada_b", "norm_g", "sc_w_in", "sc_w_out", "moe_router", "moe_w_gate", "moe_w_up", "moe_w_down"):
        m[n] = f(inp[n])
    sc = np.asarray(inp["sc_conv"], np.float32)
    m["sc_cw_pm"] = np.ascontiguousarray(sc.reshape(sc.shape[0], 3, 8, 128).transpose(0, 3, 2, 1))
    if depth > 1:
        for n in ("hy_w_in", "hy_f_w1", "hy_f_w2", "hy_f_w3", "hy_skip", "hy_w_out"):
            m[n] = f(inp[n])
        hc_ = np.asarray(inp["hy_conv"], np.float32)
        m["hy_cw_pm"] = np.ascontiguousarray(hc_.reshape(hc_.shape[0], 3, 24, 128).transpose(0, 3, 2, 1))
        for n in ("hy_f_b1", "hy_f_b2", "hy_sin_freq"):
            a = np.asarray(inp[n], np.float32)
            m[n] = np.ascontiguousarray(a.reshape(a.shape[0], 64, 1))
    if depth > 2:
        m["at_w_qkv"] = f(inp["at_w_qkv"]); m["at_w_o"] = f(inp["at_w_o"])
        for n, s in (("at_qg", "at_q_g"), ("at_kg", "at_k_g")):
            a = np.asarray(inp[s], np.float32)
            m[n] = np.ascontiguousarray(a.reshape(a.shape[0], 128, 1))
    return m


_CONST_CACHE = {}


def const_inputs(T, LC, depth=DEPTH):
    key = (T, LC, depth)
    if key not in _CONST_CACHE:
        m = {}
        if depth > 1:
            for tag, L in (("l", T), ("c", LC)):
                fft_l = (T == 4096 and tag == "l")
                hc = hy_consts(L, with_M=not fft_l)
                m[f"hz_{tag}"] = hc["zT"]; m[f"hp_{tag}"] = hc["ph"]; m[f"ht_{tag}"] = hc["tl"]; m[f"hd_{tag}"] = hc["delta"]
                if not fft_l:
                    m[f"hM_{tag}"] = hc["Mb"]
            if T == 4096:
                m.update(hy_fft_consts())
        if depth > 2:
            m["rope"] = rope_tables_np(T); m["rotT"] = rot_T_np()
        _CONST_CACHE[key] = m
    return _CONST_CACHE[key]


def kernel(**inputs):
    B, T, _ = inputs["x"].shape
    LC = inputs["ctx"].shape[1]
    depth = inputs["ada_w"].shape[0]
    nc = build_model(T, LC, depth)
    cm = const_inputs(T, LC, depth)
    in_maps = []
    for b in range(B):
        m = host_inputs(inputs, b, T, LC, depth)
        m.update(cm)
        in_maps.append(m)
    res = run_bass_kernel_spmd(nc, in_maps, core_ids=list(range(B)))
    return np.stack([np.asarray(r["out"], dtype=np.float32) for r in res.results], axis=0)


def hy_fft_consts():
    import ml_dtypes
    L = 4096
    N = 2 * L
    a = np.arange(64)[:, None]
    fa = np.arange(64)[None, :]
    al = 2 * np.pi * (fa + 0.5) * a / 128.0
    W1 = np.concatenate([np.cos(al), -np.sin(al)], 1)
    W4 = np.concatenate([np.cos(al).T, -np.sin(al).T], 0) * (2.0 / N)
    b = np.arange(64)[:, None]
    fb = np.arange(32)[None, :]
    names = ("T2", "T2s", "T3", "MA1", "MA2", "MB1", "MB2")
    Ms = {n: np.zeros((64, 128, 128)) for n in names}
    for f_a in range(64):
        fD = f_a + 128 * fb
        fM = 127 - f_a + 128 * fb
        pD = 2 * np.pi * (fD + 0.5) * (b + 0.5) / N
        pM = 2 * np.pi * (fM + 0.5) * (b + 0.5) / N
        hD = np.pi * (fD + 0.5) / N
        hM = np.pi * (fM + 0.5) / N
        cD, sD, cM, sM = np.cos(pD), np.sin(pD), np.cos(pM), np.sin(pM)

        def put(name, blk_re, blk_im, ro, mi):
            c0 = ro * 64 + mi * 32
            Ms[name][f_a, 0:64, c0:c0 + 32] += blk_re
            Ms[name][f_a, 64:128, c0:c0 + 32] += blk_im

        ReD, ImD, ReM, ImM = (cD, sD), (-sD, cD), (cM, -sM), (-sM, -cM)
        neg = lambda t: (-t[0], -t[1])
        put("T2", *ReD, 0, 0); put("T2", *ImD, 1, 0); put("T2", *ReM, 0, 1); put("T2", *ImM, 1, 1)
        put("T2s", *neg(ImD), 0, 0); put("T2s", *ReD, 1, 0); put("T2s", *neg(ImM), 0, 1); put("T2s", *ReM, 1, 1)
        for ro in (0, 1):
            for (Re_, Im_, h_, mi) in ((ReD, ImD, hD, 0), (ReM, ImM, hM, 1)):
                ch, sh = np.cos(h_), np.sin(h_)
                put("MA1", Re_[0] * ch, Re_[1] * ch, ro, mi); put("MA2", -Im_[0] * sh, -Im_[1] * sh, ro, mi)
                put("MB1", Re_[0] * sh, Re_[1] * sh, ro, mi); put("MB2", Im_[0] * ch, Im_[1] * ch, ro, mi)

        def put3(ri, mi, blk, ro):
            r0 = ri * 64 + mi * 32
            Ms["T3"][f_a, r0:r0 + 32, ro * 64:ro * 64 + 64] += blk.T

        put3(0, 0, cD, 0); put3(1, 0, -sD, 0); put3(0, 1, cM, 0); put3(1, 1, -sM, 0)
        put3(0, 0, sD, 1); put3(1, 0, cD, 1); put3(0, 1, -sM, 1); put3(1, 1, -cM, 1)
    bf = lambda x: np.ascontiguousarray(x.astype(np.float32).astype(ml_dtypes.bfloat16))
    out = {"fW1": bf(W1), "fW4": bf(W4)}
    out["fTd"] = bf(np.stack([Ms["T2"], Ms["T2s"], Ms["T3"]], 0).transpose(2, 0, 1, 3))
    out["fTf"] = bf(np.stack([Ms["MA1"], Ms["MA2"], Ms["MB1"], Ms["MB2"]], 0).transpose(2, 0, 1, 3))
    return out


class HyFFT(Hyena):
    def fft_stage1(self, src_d, Gd, W1, ncols=1024):
        k = self.k
        BBS = 8
        xv = src_d.rearrange("(a b) c -> a b c", b=64)
        gv = Gd.rearrange("r f b c -> (r f) b c")
        with ExitStack() as ls:
            xt = [self.sb(ls, f"f1_x{i}", [64, BBS, ncols], BF16) for i in range(2)]
            gt = [self.sb(ls, f"f1_g{i}", [128, ncols], BF16) for i in range(6)]
            n = 0
            for bb in range(64 // BBS):
                x_ = xt[bb % 2]
                k.dma("sp", lambda e: e.dma_start(out=x_[:], in_=xv[:, bb * BBS:(bb + 1) * BBS, :]), writes=[x_])
                for bi in range(BBS):
                    b = bb * BBS + bi
                    p = self.psum()
                    for h in range(ncols // 512):
                        k.op("pe", lambda e: e.matmul(p[:, h * 512:(h + 1) * 512], lhsT=W1[:], rhs=x_[:, bi, h * 512:(h + 1) * 512],
                                                      start=True, stop=True), reads=[W1, x_], writes=[p])
                    g_ = gt[n % 6]
                    n += 1
                    k.op("act", lambda e: e.copy(out=g_[:], in_=p[:, 0:ncols]), reads=[p], writes=[g_])
                    k.dma("pool", lambda e: e.dma_start(out=gv[:, b, :], in_=g_[:]), reads=[g_])
            self.k.barrier()

    def _load_gblock(self, dst, Gd, fa0, nfa):
        for r in range(2):
            self.k.dma("sp", lambda e: e.dma_start(out=dst[r * 64:(r + 1) * 64, 0:nfa, :],
                                                   in_=Gd[r, fa0:fa0 + nfa, :, :].rearrange("f b c -> b f c")), writes=[dst])

    def fft_filter_stage2(self, Gp, Gm, Tf, rS, sk, KA_d, KB_d):
        k = self.k
        FBS = 4
        with ExitStack() as ls:
            gp = [self.sb(ls, f"f2_gp{i}", [128, FBS, 1024], BF16) for i in range(2)]
            gm = [self.sb(ls, f"f2_gm{i}", [128, FBS, 1024], BF16) for i in range(2)]
            tmp = [self.sb(ls, f"f2_t{i}", [128, 1024], F32) for i in range(2)]
            ko = [self.sb(ls, f"f2_k{i}", [128, 1024], BF16) for i in range(4)]
            for fb_ in range(64 // FBS):
                gp_, gm_ = gp[fb_ % 2], gm[fb_ % 2]
                self._load_gblock(gp_, Gp, fb_ * FBS, FBS)
                self._load_gblock(gm_, Gm, fb_ * FBS, FBS)
                for fi in range(FBS):
                    fa = fb_ * FBS + fi
                    pA, pB = self.psum(), self.psum()
                    for (p, m1, m2) in ((pA, 0, 1), (pB, 2, 3)):
                        for h in range(2):
                            hs = slice(h * 512, (h + 1) * 512)
                            k.op("pe", lambda e: e.matmul(p[:, hs], lhsT=Tf[:, m1, fa, :], rhs=gp_[:, fi, hs], start=True, stop=False),
                                 reads=[Tf, gp_], writes=[p])
                            k.op("pe", lambda e: e.matmul(p[:, hs], lhsT=Tf[:, m2, fa, :], rhs=gm_[:, fi, hs], start=False, stop=True),
                                 reads=[Tf, gm_], writes=[p])
                    t_ = tmp[fa % 2]
                    ka, kb = ko[(fa % 2) * 2], ko[(fa % 2) * 2 + 1]
                    k.op("dve", lambda e: e.tensor_tensor(out=t_[:], in0=pA[:], in1=rS[:], op=ALU.mult), reads=[pA, rS], writes=[t_])
                    k.op("pool", lambda e: e.tensor_tensor(out=ka[:], in0=t_[:], in1=sk[:], op=ALU.add), reads=[t_, sk], writes=[ka])
                    k.op("dve", lambda e: e.tensor_tensor(out=kb[:], in0=pB[:], in1=rS[:], op=ALU.mult), reads=[pB, rS], writes=[kb])
                    k.dma("pool", lambda e: e.dma_start(out=KA_d[fa], in_=ka[:]), reads=[ka])
                    k.dma("pool", lambda e: e.dma_start(out=KB_d[fa], in_=kb[:]), reads=[kb])
            self.k.barrier()

    def fft_conv(self, src_d, Gd, Zd, W1, W4, Td, KA_d, KB_d, gate_d, z_d):
        k = self.k
        self.fft_stage1(src_d, Gd, W1)
        FBS = 4
        with ExitStack() as ls:
            gb = [self.sb(ls, f"f3_g{i}", [128, FBS, 1024], BF16) for i in range(2)]
            kk = [self.sb(ls, f"f3_k{i}", [128, 1024], BF16) for i in range(4)]
            t1 = [self.sb(ls, f"f3_t1{i}", [128, 1024], F32) for i in range(2)]
            t2 = [self.sb(ls, f"f3_t2{i}", [128, 1024], F32) for i in range(2)]
            Y = [self.sb(ls, f"f3_Y{i}", [128, 1024], BF16) for i in range(2)]
            zt = [self.sb(ls, f"f3_z{i}", [128, 1024], BF16) for i in range(2)]
            pUs = {}

            def stX(fa):
                fb_, fi = fa // FBS, fa % FBS
                g_ = gb[fb_ % 2]
                if fi == 0:
                    self._load_gblock(g_, Gd, fb_ * FBS, FBS)
                ka, kb = kk[(fa % 2) * 2], kk[(fa % 2) * 2 + 1]
                k.dma("sp", lambda e: e.dma_start(out=ka[:], in_=KA_d[fa]), writes=[ka])
                k.dma("sp", lambda e: e.dma_start(out=kb[:], in_=KB_d[fa]), writes=[kb])
                pU, pS = self.psum(), self.psum()
                for (p, m) in ((pU, 0), (pS, 1)):
                    for h in range(2):
                        hs = slice(h * 512, (h + 1) * 512)
                        k.op("pe", lambda e: e.matmul(p[:, hs], lhsT=Td[:, m, fa, :], rhs=g_[:, fi, hs], start=True, stop=True),
                             reads=[Td, g_], writes=[p])
                a_, b_, y_ = t1[fa % 2], t2[fa % 2], Y[fa % 2]
                k.op("dve", lambda e: e.tensor_tensor(out=a_[:], in0=pU[:], in1=ka[:], op=ALU.mult), reads=[pU, ka], writes=[a_])
                k.op("dve", lambda e: e.tensor_tensor(out=b_[:], in0=pS[:], in1=kb[:], op=ALU.mult), reads=[pS, kb], writes=[b_])
                k.op("pool", lambda e: e.tensor_tensor(out=y_[:], in0=a_[:], in1=b_[:], op=ALU.add), reads=[a_, b_], writes=[y_])
                pUs[fa] = pU

            def stZ(fa):
                y_, z_ = Y[fa % 2], zt[fa % 2]
                pZ = pUs.pop(fa)
                for h in range(2):
                    hs = slice(h * 512, (h + 1) * 512)
                    k.op("pe", lambda e: e.matmul(pZ[:, hs], lhsT=Td[:, 2, fa, :], rhs=y_[:, hs], start=True, stop=True),
                         reads=[Td, y_], writes=[pZ])
                k.op("act", lambda e: e.copy(out=z_[:], in_=pZ[:]), reads=[pZ], writes=[z_])
                for r in range(2):
                    k.dma("pool", lambda e: e.dma_start(out=Zd[r, fa, :, :], in_=z_[r * 64:(r + 1) * 64, :]), reads=[z_])

            stX(0)
            for fa in range(64):
                if fa + 1 < 64:
                    stX(fa + 1)
                stZ(fa)
            self.k.barrier()
        BBS = 8
        with ExitStack() as ls:
            zz = [self.sb(ls, f"f4_z{i}", [128, BBS, 1024], BF16) for i in range(2)]
            gg = [self.sb(ls, f"f4_g{i}", [64, BBS, 1024], BF16) for i in range(2)]
            oo = [self.sb(ls, f"f4_o{i}", [64, BBS, 1024], BF16) for i in range(2)]
            zv = Zd.rearrange("r f b c -> (r f) b c")
            gv = gate_d.rearrange("(a b) c -> a b c", b=64)
            ov = z_d.rearrange("(a b) c -> a b c", b=64)
            for bb in range(64 // BBS):
                z_, g_, o_ = zz[bb % 2], gg[bb % 2], oo[bb % 2]
                bs = slice(bb * BBS, (bb + 1) * BBS)
                k.dma("sp", lambda e: e.dma_start(out=z_[:], in_=zv[:, bs, :]), writes=[z_])
                k.dma("sp", lambda e: e.dma_start(out=g_[:], in_=gv[:, bs, :]), writes=[g_])
                for bi in range(BBS):
                    p = self.psum()
                    for h in range(2):
                        hs = slice(h * 512, (h + 1) * 512)
                        k.op("pe", lambda e: e.matmul(p[0:64, hs], lhsT=W4[:], rhs=z_[:, bi, hs], start=True, stop=True),
                             reads=[W4, z_], writes=[p])
                    k.op("dve", lambda e: e.tensor_tensor(out=o_[:, bi, :], in0=p[0:64, :], in1=g_[:, bi, :], op=ALU.mult),
                         reads=[p, g_], writes=[o_])
                k.dma("pool", lambda e: e.dma_start(out=ov[:, bs, :], in_=o_[:]), reads=[o_])
            self.k.barrier()
```
